# Optimizing a Trainium2 kernel written in Bass

```python
import jax, jax.numpy as jnp
from jax import lax
import numpy as np

D_MODEL = 1024
BATCH = 4
SEQ = 8192
DEPTH = 4
DEC_BATCH = 8
DEC_SEQ = 64
PAST_LEN = 2048

CHUNK = 64
N_MIXERS = 2
N_ATTN = (DEPTH + 1) // 2
N_HGRN = DEPTH // 2
N_HEADS = 16
N_KV = 4
HEAD_DIM = 64
GROUP = N_HEADS // N_KV
ROT_DIM = HEAD_DIM // 4
ROPE_THETA = 500000.0
WINDOW = 128
WIN_CHUNKS = WINDOW // CHUNK
Q_DIM = N_HEADS * HEAD_DIM
KV_DIM = N_KV * HEAD_DIM
HG_EXPAND = 128
HG_HEADS = D_MODEL // HG_EXPAND
HG_DK = HG_EXPAND
HG_DV = D_MODEL // HG_HEADS
HG_F = HG_HEADS * HG_DK
D_FF = 4 * D_MODEL
EPS = 1e-5

kernel_name = "hybrid_swa_sink_hgrn2_stream_step"


def rmsnorm(x, g):
    xf = x.astype(jnp.float32)
    y = xf * lax.rsqrt(jnp.mean(xf * xf, axis=-1, keepdims=True) + EPS)
    return (y * g.astype(jnp.float32)).astype(x.dtype)


def rope_partial(x, pos):
    half = ROT_DIM // 2
    inv_freq = ROPE_THETA ** (-(jnp.arange(half, dtype=jnp.float32) * 2.0) / ROT_DIM)
    ang = pos[:, None] * inv_freq[None, :]
    cos = jnp.cos(ang)[None, :, None, :]
    sin = jnp.sin(ang)[None, :, None, :]
    xf = x.astype(jnp.float32)
    x1, x2, rest = xf[..., :half], xf[..., half:ROT_DIM], xf[..., ROT_DIM:]
    out = jnp.concatenate([x1 * cos - x2 * sin, x2 * cos + x1 * sin, rest], axis=-1)
    return out.astype(x.dtype)


def attn_project(h, w_qkv, pos):
    B, T, _ = h.shape
    qkv = h @ w_qkv
    q, k, v = jnp.split(qkv, [Q_DIM, Q_DIM + KV_DIM], axis=-1)
    q = rope_partial(q.reshape(B, T, N_HEADS, HEAD_DIM), pos)
    k = rope_partial(k.reshape(B, T, N_KV, HEAD_DIM), pos)
    v = v.reshape(B, T, N_KV, HEAD_DIM)
    return q, k, v


def banded_sink_attention(q, k, v, valid, sinks):
    B, N, Q = q.shape[:3]
    qg = q.reshape(B, N, Q, N_KV, GROUP, HEAD_DIM)
    s = jnp.einsum('bnqkgd,bnskd->bnkgqs', qg, k,
                   preferred_element_type=jnp.float32) * (HEAD_DIM ** -0.5)
    s = jnp.where(valid[None, :, None, None, None, :], s, -jnp.inf)
    sink = sinks.astype(jnp.float32).reshape(N_KV, GROUP)[None, None, :, :, None, None]
    m = jnp.maximum(jnp.max(s, axis=-1, keepdims=True), sink)
    e = jnp.exp(s - m)
    p = e / (jnp.sum(e, axis=-1, keepdims=True) + jnp.exp(sink - m))
    o = jnp.einsum('bnkgqs,bnskd->bnqkgd', p.astype(v.dtype), v)
    return o.reshape(B, N, Q, N_HEADS, HEAD_DIM)


def swa_prompt(h, w_qkv, w_o, sinks):
    B, T, _ = h.shape
    pos = jnp.arange(T, dtype=jnp.float32)
    q, k, v = attn_project(h, w_qkv, pos)
    nc = T // CHUNK
    qc = q.reshape(B, nc, CHUNK, N_HEADS, HEAD_DIM)
    pad = ((0, 0), (WIN_CHUNKS, 0), (0, 0), (0, 0), (0, 0))
    kp = jnp.pad(k.reshape(B, nc, CHUNK, N_KV, HEAD_DIM), pad)
    vp = jnp.pad(v.reshape(B, nc, CHUNK, N_KV, HEAD_DIM), pad)
    kb = jnp.concatenate([kp[:, w:w + nc] for w in range(WIN_CHUNKS + 1)], axis=2)
    vb = jnp.concatenate([vp[:, w:w + nc] for w in range(WIN_CHUNKS + 1)], axis=2)
    key_chunk = jnp.repeat(jnp.arange(WIN_CHUNKS + 1), CHUNK)
    valid = (jnp.arange(nc)[:, None] + key_chunk[None, :] - WIN_CHUNKS) >= 0
    o = banded_sink_attention(qc, kb, vb, valid, sinks)
    out = o.reshape(B, T, Q_DIM) @ w_o
    return out, k[:, T - WINDOW:], v[:, T - WINDOW:]


def swa_sample(h, cache_k, cache_v, w_qkv, w_o, sinks):
    B, T, _ = h.shape
    pos = PAST_LEN + jnp.arange(T, dtype=jnp.float32)
    q, k, v = attn_project(h, w_qkv, pos)
    keys = jnp.concatenate([cache_k.astype(k.dtype), k], axis=1)
    vals = jnp.concatenate([cache_v.astype(v.dtype), v], axis=1)
    valid = jnp.ones((1, keys.shape[1]), dtype=bool)
    o = banded_sink_attention(q[:, None], keys[:, None], vals[:, None], valid, sinks)
    out = o.reshape(B, T, Q_DIM) @ w_o
    W = cache_k.shape[1]
    return out, keys[:, T:].astype(cache_k.dtype), vals[:, T:].astype(cache_v.dtype)


def hgrn_scan(q, k, v, logf, s0):
    B, T, H, DK = q.shape
    DV = v.shape[-1]
    L = CHUNK if T % CHUNK == 0 else T
    n = T // L

    def to_blocks(a):
        return a.reshape(B, n, L, H, a.shape[-1]).transpose(1, 0, 3, 2, 4)

    causal = jnp.tril(jnp.ones((L, L), dtype=bool))[:, :, None]

    def step(S, blk):
        qb, kb, vb, gb = blk
        b = jnp.cumsum(gb, axis=2)
        diff = b[:, :, :, None, :] - b[:, :, None, :, :]
        decay = jnp.where(causal, jnp.exp(jnp.where(causal, diff, 0.0)), 0.0)
        scores = jnp.einsum('bhtk,bhsk,bhtsk->bhts', qb, kb, decay)
        o = (jnp.einsum('bhts,bhsv->bhtv', scores, vb)
             + jnp.einsum('bhtk,bhkv->bhtv', qb * jnp.exp(b), S))
        b_last = b[:, :, -1:, :]
        S_new = (jnp.exp(b_last[:, :, 0, :])[..., None] * S
                 + jnp.einsum('bhsk,bhsv->bhkv', kb * jnp.exp(b_last - b), vb))
        return S_new, o

    S_fin, o = lax.scan(step, s0, (to_blocks(q), to_blocks(k), to_blocks(v), to_blocks(logf)))
    o = o.transpose(1, 0, 3, 2, 4).reshape(B, T, H, DV)
    return o, S_fin


def hgrn2_mix(h, s0, w_in, lb, out_norm, w_o):
    B, T, _ = h.shape
    z = h @ w_in
    zq, zf, zi, zg = jnp.split(z, [HG_F, 2 * HG_F, 2 * HG_F + D_MODEL], axis=-1)
    q = jax.nn.silu(zq.astype(jnp.float32)).reshape(B, T, HG_HEADS, HG_DK)
    zf = zf.astype(jnp.float32).reshape(B, T, HG_HEADS, HG_DK)
    lbh = lb.reshape(HG_HEADS, HG_DK)
    logf = jnp.log(lbh + (1.0 - lbh) * jax.nn.sigmoid(zf))
    k = (1.0 - lbh) * jax.nn.sigmoid(-zf)
    v = zi.astype(jnp.float32).reshape(B, T, HG_HEADS, HG_DV)
    o, S = hgrn_scan(q, k, v, logf, s0)
    o = rmsnorm(o, out_norm) * jax.nn.silu(zg.astype(jnp.float32).reshape(B, T, HG_HEADS, HG_DV))
    out = o.reshape(B, T, D_MODEL).astype(h.dtype) @ w_o
    return out, S


def sqrelu_mlp(h, w_up, w_down):
    return jnp.square(jax.nn.relu(h @ w_up)) @ w_down


def setup_inputs(seed: int = 0) -> dict:
    key = jax.random.key(seed)
    ks = jax.random.split(key, 20)
    f32 = jnp.float32

    def nrm(k, shape, scale):
        return jax.random.normal(k, shape, f32) * scale

    cache_rows = min(WINDOW, PAST_LEN)
    return {
        "x_prompt": nrm(ks[0], (BATCH, SEQ, D_MODEL), 1.0),
        "x_sample": nrm(ks[1], (DEC_BATCH, DEC_SEQ, D_MODEL), 1.0),
        "cache_k": nrm(ks[2], (N_ATTN, DEC_BATCH, cache_rows, N_KV, HEAD_DIM), 1.0),
        "cache_v": nrm(ks[3], (N_ATTN, DEC_BATCH, cache_rows, N_KV, HEAD_DIM), 1.0),
        "state_s": nrm(ks[4], (N_HGRN, DEC_BATCH, HG_HEADS, HG_DK, HG_DV), 0.5),
        "mixer_norm": 1.0 + nrm(ks[5], (DEPTH, D_MODEL), 0.01),
        "mlp_norm": 1.0 + nrm(ks[6], (DEPTH, D_MODEL), 0.01),
        "attn_w_qkv": nrm(ks[7], (N_ATTN, D_MODEL, Q_DIM + 2 * KV_DIM), D_MODEL ** -0.5),
        "attn_w_o": nrm(ks[8], (N_ATTN, Q_DIM, D_MODEL), Q_DIM ** -0.5),
        "attn_sinks": nrm(ks[9], (N_ATTN, N_HEADS), 0.5),
        "hgrn_w_in": nrm(ks[10], (N_HGRN, D_MODEL, 2 * HG_F + 2 * D_MODEL), D_MODEL ** -0.5),
        "hgrn_lb": nrm(ks[11], (N_HGRN, HG_F), 1.0),
        "hgrn_out_norm": 1.0 + nrm(ks[12], (N_HGRN, HG_DV), 0.01),
        "hgrn_w_o": nrm(ks[13], (N_HGRN, D_MODEL, D_MODEL), D_MODEL ** -0.5),
        "mlp_w_up": nrm(ks[14], (DEPTH, D_MODEL, D_FF), D_MODEL ** -0.5),
        "mlp_w_down": nrm(ks[15], (DEPTH, D_FF, D_MODEL), D_FF ** -0.5),
        "final_norm": 1.0 + nrm(ks[16], (D_MODEL,), 0.01),
    }


def reference(x_prompt, x_sample, cache_k, cache_v, state_s, mixer_norm, mlp_norm,
              attn_w_qkv, attn_w_o, attn_sinks, hgrn_w_in, hgrn_lb, hgrn_out_norm,
              hgrn_w_o, mlp_w_up, mlp_w_down, final_norm):
    yp, ys = x_prompt, x_sample
    lb_all = jnp.cumsum(jax.nn.softmax(hgrn_lb.astype(jnp.float32), axis=0), axis=0)
    lb_all = lb_all - lb_all[0:1]
    kp_l, vp_l, sp_l, ks_l, vs_l, ss_l = [], [], [], [], [], []
    for i in range(DEPTH):
        j = i // N_MIXERS
        hp = rmsnorm(yp, mixer_norm[i])
        hs = rmsnorm(ys, mixer_norm[i])
        if i % N_MIXERS == 0:
            op, kp, vp = swa_prompt(hp, attn_w_qkv[j], attn_w_o[j], attn_sinks[j])
            os_, kn, vn = swa_sample(hs, cache_k[j], cache_v[j], attn_w_qkv[j], attn_w_o[j], attn_sinks[j])
            kp_l.append(kp.astype(cache_k.dtype))
            vp_l.append(vp.astype(cache_v.dtype))
            ks_l.append(kn)
            vs_l.append(vn)
        else:
            s0 = jnp.zeros((yp.shape[0], HG_HEADS, HG_DK, HG_DV), jnp.float32)
            op, sp = hgrn2_mix(hp, s0, hgrn_w_in[j], lb_all[j], hgrn_out_norm[j], hgrn_w_o[j])
            os_, sn = hgrn2_mix(hs, state_s[j].astype(jnp.float32), hgrn_w_in[j], lb_all[j],
                                hgrn_out_norm[j], hgrn_w_o[j])
            sp_l.append(sp.astype(state_s.dtype))
            ss_l.append(sn.astype(state_s.dtype))
        yp = yp + op
        ys = ys + os_
        yp = yp + sqrelu_mlp(rmsnorm(yp, mlp_norm[i]), mlp_w_up[i], mlp_w_down[i])
        ys = ys + sqrelu_mlp(rmsnorm(ys, mlp_norm[i]), mlp_w_up[i], mlp_w_down[i])
    yp = rmsnorm(yp, final_norm)
    ys = rmsnorm(ys, final_norm)
    return (yp, ys, jnp.stack(kp_l), jnp.stack(vp_l), jnp.stack(sp_l),
            jnp.stack(ks_l), jnp.stack(vs_l), jnp.stack(ss_l))
```

```python
import numpy as np
from contextlib import ExitStack
import concourse.bass as bass
import concourse.mybir as mybir
from concourse.bass_utils import run_bass_kernel_spmd

F32 = mybir.dt.float32
BF16 = mybir.dt.bfloat16
AF = mybir.ActivationFunctionType
ALU = mybir.AluOpType

D = 1024
TT = 512
EPS = 1e-5
ENG = ["tensor", "vector", "scalar", "gpsimd", "sync"]
SAME_ENG_SYNC = True
SEM_ROT = 20000


class Op:
    __slots__ = ("eng", "fn", "deps", "sig", "idx", "stream", "val", "cnt", "inc")


class Prog:
    def __init__(self):
        self.ops = {e: [] for e in ENG}
        self.lastw = {}
        self.readers = {}
        self.stream_cnt = {}

    def op(self, eng, fn, r=(), w=(), stream=None, inc=16):
        o = Op()
        o.eng, o.fn, o.deps, o.sig, o.stream, o.val, o.cnt = eng, fn, set(), False, stream, 0, 0
        for x in r:
            lw = self.lastw.get(x)
            if lw is not None:
                o.deps.add(lw)
        for x in w:
            lw = self.lastw.get(x)
            if lw is not None:
                o.deps.add(lw)
            for rd in self.readers.get(x, ()):
                o.deps.add(rd)
        for x in r:
            self.readers.setdefault(x, []).append(o)
        for x in w:
            self.lastw[x] = o
            self.readers[x] = []
        o.deps.discard(o)
        if stream is not None:
            self.stream_cnt[stream] = self.stream_cnt.get(stream, 0) + inc
            o.val = self.stream_cnt[stream]
            o.inc = inc
        o.idx = len(self.ops[eng])
        self.ops[eng].append(o)
        return o

    def emit(self, nc, es):
        for e in ENG:
            for o in self.ops[e]:
                best = {}
                keep = []
                for d in o.deps:
                    if d.stream is not None:
                        keep.append(d)
                        continue
                    if d.eng == e and (e == "tensor" or not SAME_ENG_SYNC):
                        continue
                    if d.eng not in best or best[d.eng].idx < d.idx:
                        best[d.eng] = d
                for d in best.values():
                    d.sig = True
                    keep.append(d)
                o.deps = keep
        for e in ENG:
            c = 0
            for o in self.ops[e]:
                if o.sig and o.stream is None:
                    c += 1
                    o.cnt = c
        sems = {}

        def getsem(key):
            if key not in sems:
                sems[key] = es.enter_context(nc.semaphore("s_" + str(key).replace(" ", "")))
            return sems[key]

        def semval(d):
            if d.stream is not None:
                return getsem(("st", d.stream)), d.val
            k = (d.cnt - 1) // SEM_ROT
            return getsem((d.eng, k)), d.cnt - k * SEM_ROT

        block = es.enter_context(nc.Block())

        def run(e):
            def body(engobj):
                known = {}
                for o in self.ops[e]:
                    need = {}
                    for d in o.deps:
                        s, v = semval(d)
                        if need.get(s, (None, 0))[1] < v:
                            need[s] = (s, v)
                    for s, v in need.values():
                        if known.get(s, 0) < v:
                            engobj.wait_ge(s, v)
                            known[s] = v
                    ins = o.fn(engobj)
                    if o.stream is not None:
                        ins.then_inc(getsem(("st", o.stream)), o.inc)
                    elif o.sig:
                        k = (o.cnt - 1) // SEM_ROT
                        ins.then_inc(getsem((e, k)), 1)
                for st, tot in self.stream_cnt.items():
                    if self.stream_eng.get(st) == e:
                        engobj.wait_ge(getsem(("st", st)), tot)
            getattr(block, e)(body)

        self.stream_eng = {}
        for e in ENG:
            for o in self.ops[e]:
                if o.stream is not None:
                    self.stream_eng[o.stream] = e
        for e in ENG:
            run(e)


def _fblock(cols):
    return np.ascontiguousarray(cols.reshape(8, 128, 512).transpose(1, 0, 2)).reshape(128, 4096)


def _dblock(w):
    return np.ascontiguousarray(w.reshape(32, 128, 128).transpose(1, 0, 2)).reshape(128, 4096)


def _swap_cols(w, nh):
    w4 = w.reshape(1024, nh, 64)
    s = np.zeros_like(w4)
    s[:, :, 0:8] = w4[:, :, 8:16]
    s[:, :, 8:16] = w4[:, :, 0:8]
    return s.reshape(1024, nh * 64)


def build_weight_blocks(attn_w_qkv, attn_w_o, hgrn_w_in, hgrn_w_o, mlp_w_up, mlp_w_down):
    blocks = []
    for i in range(4):
        j = i // 2
        if i % 2 == 0:
            w = attn_w_qkv[j]
            wq, wk, wv = w[:, :1024], w[:, 1024:1280], w[:, 1280:1536]
            wqs = _swap_cols(wq, 16)
            wks = _swap_cols(wk, 4)
            for qb in range(4):
                cols = np.concatenate([wq[:, (2 * qb) * 128:(2 * qb + 1) * 128], wqs[:, (2 * qb) * 128:(2 * qb + 1) * 128],
                                       wq[:, (2 * qb + 1) * 128:(2 * qb + 2) * 128], wqs[:, (2 * qb + 1) * 128:(2 * qb + 2) * 128]], axis=1)
                blocks.append(_fblock(cols))
            for kb in range(2):
                parts = []
                for g in (2 * kb, 2 * kb + 1):
                    kg = wk[:, g * 64:(g + 1) * 64]
                    ksg = wks[:, g * 64:(g + 1) * 64]
                    parts += [kg, kg, ksg, ksg]
                blocks.append(_fblock(np.concatenate(parts, axis=1)))
            parts = []
            for g in range(4):
                vg = wv[:, g * 64:(g + 1) * 64]
                parts += [vg, vg]
            blocks.append(_fblock(np.concatenate(parts, axis=1)))
            wo = attn_w_o[j]
        else:
            w = hgrn_w_in[j]
            zq, zf, zi, zg = w[:, :1024], w[:, 1024:2048], w[:, 2048:3072], w[:, 3072:4096]
            for b in range(2):
                blocks.append(_fblock(zi[:, b * 512:(b + 1) * 512]))
            for b in range(2):
                blocks.append(_fblock(zg[:, b * 512:(b + 1) * 512]))
            for b in range(4):
                h0, h1 = 2 * b, 2 * b + 1
                cols = np.concatenate([zf[:, h0 * 128:(h0 + 1) * 128], zq[:, h0 * 128:(h0 + 1) * 128],
                                       zf[:, h1 * 128:(h1 + 1) * 128], zq[:, h1 * 128:(h1 + 1) * 128]], axis=1)
                blocks.append(_fblock(cols))
            wo = hgrn_w_o[j]
        for b in range(2):
            blocks.append(_fblock(wo[:, b * 512:(b + 1) * 512]))
        for b in range(8):
            blocks.append(_fblock(mlp_w_up[i][:, b * 512:(b + 1) * 512]))
        for m in range(8):
            blocks.append(_dblock(mlp_w_down[i][:, m * 128:(m + 1) * 128]))
    return np.stack(blocks).astype(np.float32)


NB_ATT = 4 + 2 + 1 + 2 + 16
NB_HG = 2 + 2 + 4 + 2 + 16
NB_ALL = 2 * (NB_ATT + NB_HG)


def rope_tables(positions):
    half = 8
    inv = (np.float32(500000.0) ** (-(np.arange(half, dtype=np.float32) * np.float32(2.0)) / np.float32(16))).astype(np.float32)
    ang = (positions.astype(np.float32)[:, None] * inv[None, :]).astype(np.float32)
    cos = np.cos(ang).astype(np.float32).T
    sin = np.sin(ang).astype(np.float32).T
    n = positions.shape[0]
    C = np.ones((64, n), np.float32)
    S = np.zeros((64, n), np.float32)
    C[0:8] = cos
    C[8:16] = cos
    S[0:8] = -sin
    S[8:16] = sin
    return np.concatenate([C, C], 0), np.concatenate([S, S], 0)


def build_program(NT, NS=2, depth=4, pipe=False):
    nc = bass.Bass("TRN2", target_bir_lowering=False)
    P = Prog()
    es = ExitStack()
    NSTEP = NT + 1 if pipe else NT
    if pipe:
        NS, depth = 3, 2
    NL = 1 if pipe else 2
    SEQ = NSTEP * TT
    NBU = (NB_ATT + NB_HG) * (depth // 2) if depth % 2 == 0 else NB_ATT

    def din(name, shape, dt=F32):
        return nc.dram_tensor(name, list(shape), dt, kind="ExternalInput").ap()

    def dout(name, shape):
        return nc.dram_tensor(name, list(shape), F32, kind="ExternalOutput").ap()

    xp = din("xp", [SEQ, D])
    xs = din("xs", [NS, 64, D])
    ck = din("ck", [NL, NS, 128, 256])
    cv = din("cv", [NL, NS, 128, 256])
    st = din("st", [NL, NS, 8, 128, 128])
    wblk = din("wblk", [NBU, 128, 4096])
    small = din("small", [128, 128])
    sinkrow = din("sinkrow", [1, 2048])
    if pipe:
        ein = nc.dram_tensor("ein", [TT, D], F32)
        eout = nc.dram_tensor("eout", [2 * TT, D], F32)
        eins = nc.dram_tensor("eins", [64, D], F32)
        eouts = nc.dram_tensor("eouts", [128, D], F32)
    cident = din("cident", [128, 128])
    cmask = din("cmask", [128, 128])
    creset = din("creset", [128, TT])
    ropeC = din("ropeC", [128, SEQ + 64 * NS])
    ropeS = din("ropeS", [128, SEQ + 64 * NS])
    yp = dout("yp", [SEQ, D])
    ys = dout("ys", [NS, 64, D])
    kpo = dout("kpo", [2, 128, 256])
    vpo = dout("vpo", [2, 128, 256])
    spo = dout("spo", [2, 8, 128, 128])
    kso = dout("kso", [NL, NS, 128, 256])
    vso = dout("vso", [NL, NS, 128, 256])
    sso = dout("sso", [NL, NS, 8, 128, 128])
    wscr = nc.dram_tensor("wscr", [NBU, 128, 4096], BF16, kind="Internal").ap()

    def sb(name, shape, dt=F32):
        return es.enter_context(nc.sbuf_tensor(name, list(shape), dt))

    def ps(name, shape, dt=F32):
        return es.enter_context(nc.psum_tensor(name, list(shape), dt))

    x = sb("x", [128, 8, TT])
    hb = sb("hb", [128, 8, TT], BF16)
    NSLOT = 4
    wsl = [sb(f"ws{i}", [128, 4096], BF16) for i in range(NSLOT)]
    big = [sb(f"big{i}", [128, 8, TT], BF16) for i in range(4)]
    ktok = sb("ktok", [128, 8, 4, 128], BF16)
    KA = sb("KA", [128, 4, 128 + TT], BF16)
    KB = sb("KB", [128, 4, 128 + TT], BF16)
    Vst = sb("Vst", [64, 2 + TT // 64, 512], BF16)
    KAc = [sb(f"KAc{j}", [128, 4, 128], BF16) for j in range(NL)]
    KBc = [sb(f"KBc{j}", [128, 4, 128], BF16) for j in range(NL)]
    Vc = [sb(f"Vc{j}", [64, 2, 512], BF16) for j in range(NL)]
    kf = sb("kf", [128, 4, 128])
    vf = sb("vf", [64, 2, 256])
    kvo = sb("kvo", [128, 256])
    pT = [sb(f"pT{i}", [64, 384], BF16) for i in range(4)]
    rec = [sb(f"rec{i}", [128, 128]) for i in range(4)]
    rC = [sb(f"rC{i}", [128, TT]) for i in range(1)]
    rS = [sb(f"rS{i}", [128, TT]) for i in range(1)]
    NTMP = 12
    tmp = [sb(f"tmp{i}", [128, TT]) for i in range(NTMP)]
    oh = sb("oh", [128, TT])
    osq = sb("osq", [128, TT])
    oh2 = sb("oh2", [128, TT])
    osq2 = sb("osq2", [128, TT])
    S = [sb(f"S{j}", [128, 8, 128]) for j in range(NL)]
    S2 = [sb(f"S{j}b", [128, 8, 128]) for j in range(NL)]
    scur = [0 for j in range(NL)]
    vones = sb("vones", [64, 128], BF16)

    Sb = sb("Sb", [128, 8, 128], BF16)
    smask = [sb(f"smask{i}", [128, 128], BF16) for i in range(2)]
    Elast = sb("Elast", [128, 8, 8])
    Emid = sb("Emid", [128, 8, 8])
    Elm = sb("Elm", [128, 8, 8])
    dl = sb("dl", [128, 8, 8])
    ident_f = sb("ident_f", [128, 128])
    ident_b = sb("ident_b", [128, 128], BF16)
    ones_f = sb("ones_f", [128, 128])
    ones_b = sb("ones_b", [128, 128], BF16)
    mask_f = sb("mask_f", [128, 128])
    resetm = sb("resetm", [128, TT])
    sm = sb("sm", [128, 128])
    lb = sb("lb", [128, 2, 8])
    oml = sb("oml", [128, 2, 8])
    esink = sb("esink", [1, 2048])
    eshl = sb("eshl", [2, 2048], BF16)
    ckst = sb("ckst", [128, 4, 2, 64])
    ktz = [sb(f"ktz{i}", [128, 768], BF16) for i in range(2)]
    ktz2 = [sb(f"ktzb{i}", [128, 768], BF16) for i in range(2)]
    pm = [ps(f"pm{i}", [128, 512]) for i in range(4)]
    pS = [ps(f"pS{i}", [128, 512]) for i in range(2)]
    pO = [ps(f"pO{i}", [128, 512]) for i in range(2)]

    cnt = {"pm": 0, "tmp": 0, "w": 0, "wl": 0, "pm_n": 4}

    def next_pm():
        i = cnt["pm"] % cnt["pm_n"]
        cnt["pm"] += 1
        return pm[i], f"pm{i}"

    def next_tmp():
        i = cnt["tmp"] % NTMP
        cnt["tmp"] += 1
        return tmp[i], f"tmp{i}"

    SM_MIX, SM_MLP, SM_FIN, SM_GNO, SM_LB = 0, 32, 64, 72, 80
    SM_FLAG, SM_SELA, SM_SELB, SM_LBSEL = 96, 120, 121, 122

    def dma(eng, out, in_, r, w, stream):
        P.op(eng, lambda e, out=out, in_=in_: e.dma_start(out=out, in_=in_), r=r, w=w, stream=stream)

    dma("sync", ident_f[:], cident, [], ["ident_f"], "c0")
    dma("sync", mask_f[:], cmask, [], ["mask_f"], "c1")
    dma("sync", resetm[:], creset, [], ["resetm"], "c2")
    dma("sync", sm[:], small, [], ["sm"], "c3")
    dma("sync", esink[:], sinkrow, [], ["esink"], "c4")
    P.op("vector", lambda e: e.tensor_copy(out=ident_b[:], in_=ident_f[:]), r=["ident_f"], w=["ident_b"])
    P.op("vector", lambda e: e.memset(ones_f[:], 1.0), w=["ones_f"])
    P.op("vector", lambda e: e.memset(ones_b[:], 1.0), w=["ones_b"])
    P.op("scalar", lambda e: e.activation(out=esink[:], in_=esink[:], func=AF.Exp), r=["esink"], w=["esink"])
    P.op("vector", lambda e: e.tensor_copy(out=eshl[0:1, :], in_=esink[:]), r=["esink"], w=["eshl"])
    for q4 in range(4):
        ta, tan = tmp[2 * q4], f"tmp{2 * q4}"
        tb_, tbn = tmp[2 * q4 + 1], f"tmp{2 * q4 + 1}"
        P.op("vector", lambda e, ta=ta, q4=q4: e.tensor_copy(out=ta[0:1, :], in_=eshl[0:1, q4 * 512:(q4 + 1) * 512]), r=["eshl"], w=[tan])
        P.op("vector", lambda e, ta=ta, tb_=tb_, q4=q4: e.tensor_tensor(out=tb_[0:1, :].bitcast(BF16)[:, 0:512], in0=esink[0:1, q4 * 512:(q4 + 1) * 512], in1=ta[0:1, :], op=ALU.subtract),
             r=["esink", tan], w=[tbn])
        dma("sync", eshl[1:2, q4 * 512:(q4 + 1) * 512], tb_[0:1, :].bitcast(BF16)[:, 0:512], [tbn], ["eshl"], f"c5{q4}")
    lbr = sm[:, SM_LB:SM_LB + 16].rearrange("p (l h) -> p l h", l=2)
    P.op("vector", lambda e: e.memset(lb[:], 0.0), w=["lb"])
    P.op("vector", lambda e: e.tensor_tensor(out=dl[:, 0, :], in0=lbr[:, 1, :], in1=lbr[:, 0, :], op=ALU.subtract), r=["sm"], w=["dl"])
    P.op("scalar", lambda e: e.activation(out=lb[:, 1, :], in_=dl[:, 0, :], func=AF.Sigmoid), r=["dl", "lb"], w=["lb"])
    if pipe:
        P.op("vector", lambda e: e.tensor_scalar(out=lb[:, 0, :], in0=lb[:, 1, :], scalar1=sm[:, SM_LBSEL:SM_LBSEL + 1], scalar2=None, op0=ALU.mult), r=["lb", "sm"], w=["lb"])
    P.op("vector", lambda e: e.tensor_scalar(out=oml[:], in0=lb[:], scalar1=-1.0, scalar2=1.0, op0=ALU.mult, op1=ALU.add), r=["lb"], w=["oml"])
    for j in range(NL):
        P.op("gpsimd", lambda e, j=j: e.memset(S[j][:], 0.0), w=[f"S{j}h{h}" for h in range(8)])
        P.op("gpsimd", lambda e, j=j: e.memset(S2[j][:], 0.0), w=[f"S{j}bh{h}" for h in range(8)])
    if pipe:
        P.op("gpsimd", lambda e: e.memset(tmp[11][:], 0.0), w=["tmp11"])
        for q4 in range(4):
            for half in range(2):
                dma("sync", ein[q4 * 128:(q4 + 1) * 128, half * 512:(half + 1) * 512], tmp[11][:], ["tmp11"], ["ein"], f"ez{q4}{half}")
        for half in range(2):
            dma("sync", eins[:, half * 512:(half + 1) * 512], tmp[11][0:64, :], ["tmp11"], ["eins"], f"ez4{half}")
    P.op("gpsimd", lambda e: e.memset(KA[:], 0.0), w=["KA"])
    for i in range(2):
        P.op("gpsimd", lambda e, i=i: e.memset(ktz[i][:], 0.0), w=[f"ktz{i}"])
        P.op("gpsimd", lambda e, i=i: e.memset(ktz2[i][:], 0.0), w=[f"ktzb{i}"])
    P.op("gpsimd", lambda e: e.memset(KB[:], 0.0), w=["KB"])
    for j in range(NL):
        P.op("gpsimd", lambda e, j=j: e.memset(KAc[j][:], 0.0), w=[f"KAc{j}"])
        P.op("gpsimd", lambda e, j=j: e.memset(KBc[j][:], 0.0), w=[f"KBc{j}"])
        P.op("gpsimd", lambda e, j=j: e.memset(Vc[j][:], 0.0), w=[f"Vc{j}"])
    nb_used = NBU
    for i in range(nb_used):
        dma("gpsimd", wscr[i], wblk[i], [], [f"wscr{i}", f"wcs{i % 8}"], f"wc{i % 8}")

    wstate = {"next_load": 0, "seq": []}

    def w_issue():
        k = wstate["next_load"]
        if k >= len(wstate["seq"]):
            return
        blk = wstate["seq"][k]
        slot = k % NSLOT
        dma("sync", wsl[slot][:], wscr[blk], [f"wscr{blk}"], [f"ws{slot}"], f"wl{slot}")
        wstate["next_load"] += 1

    def wget():
        k = cnt["w"]
        cnt["w"] += 1
        while wstate["next_load"] < min(k + NSLOT, len(wstate["seq"])):
            w_issue()
        slot = k % NSLOT
        return wsl[slot], f"ws{slot}"

    def mm(out, lhsT, rhs, start, stop, r, w):
        P.op("tensor", lambda e: e.matmul(out, lhsT, rhs, start=start, stop=stop), r=r, w=w)

    def proj_f(W, wn, col0, src, srcn, T, kcs=8):
        pt, pn = next_pm()
        Wv = W[:].rearrange("p (k n) -> p k n", k=kcs)
        for kc in range(kcs):
            if callable(src):
                rhs, rn = src(kc)
            else:
                rhs, rn = src[:, kc, :T], srcn
            mm(pt[:, :T], Wv[:, kc, col0:col0 + 128], rhs, kc == 0, kc == kcs - 1, [wn, rn], [pn])
        return pt, pn

    def rmsnorm(T, gcol, final=False):
        pt, pn = next_pm()
        for c in range(8):
            t, tn = next_tmp()
            P.op("scalar", lambda e, t=t, c=c: e.activation(out=t[:, :T], in_=x[:, c, :T], func=AF.Square), r=["x"], w=[tn])
            mm(pt[:, :T], ones_f[:], t[:, :T], c == 0, c == 7, ["ones_f", tn], [pn])
        sd, sdn = next_tmp()
        P.op("scalar", lambda e: e.activation(out=sd[:, :T], in_=pt[:, :T], func=AF.Ln, scale=1.0 / D, bias=epsb[:, 0:1]), r=[pn, "epsb"], w=[sdn])
        P.op("scalar", lambda e: e.activation(out=sd[:, :T], in_=sd[:, :T], func=AF.Exp, scale=-0.5), r=[sdn], w=[sdn])
        for c in range(8):
            if final:
                P.op("vector", lambda e, c=c: e.scalar_tensor_tensor(out=x[:, c, :T], in0=x[:, c, :T], scalar=sm[:, gcol + c:gcol + c + 1],
                                                                      in1=sd[:, :T], op0=ALU.mult, op1=ALU.mult), r=["x", "sm", sdn], w=["x"])
            else:
                P.op("vector", lambda e, c=c: e.scalar_tensor_tensor(out=hb[:, c, :T], in0=x[:, c, :T], scalar=sm[:, gcol + c:gcol + c + 1],
                                                                      in1=sd[:, :T], op0=ALU.mult, op1=ALU.mult), r=["x", "sm", sdn], w=["hb"])

    epsb = sb("epsb", [128, 1])
    P.op("vector", lambda e: e.memset(epsb[:], EPS), w=["epsb"])

    def out_proj(T, src, srcn):
        for b in range(2):
            W, wn = wget()
            for m in range(4):
                cm = 4 * b + m
                pt, pn = proj_f(W, wn, m * 128, src, srcn, T)
                P.op("vector", lambda e, pt=pt, cm=cm: e.tensor_tensor(out=x[:, cm, :T], in0=pt[:, :T], in1=x[:, cm, :T], op=ALU.add), r=[pn, "x"], w=["x"])

    def mlp(T, i):
        import os
        if os.environ.get("KSTOP", "") in ("a", "b", "b1", "b2", "b3", "c", "d", "e"):
            return
        rmsnorm(T, SM_MLP + 8 * i)
        for b in range(8):
            W, wn = wget()
            for m in range(4):
                kc = 4 * b + m
                pt, pn = proj_f(W, wn, m * 128, hb, "hb", T)
                t, tn = next_tmp()
                P.op("scalar", lambda e, pt=pt, t=t: e.activation(out=t[:, :T], in_=pt[:, :T], func=AF.Relu), r=[pn], w=[tn])
                P.op("gpsimd", lambda e, t=t, kc=kc: e.tensor_tensor(out=big[kc // 8][:, kc % 8, :T], in0=t[:, :T], in1=t[:, :T], op=ALU.mult),
                     r=[tn], w=[f"big{kc // 8}"])
        for m in range(8):
            W, wn = wget()
            pt, pn = proj_f(W, wn, 0, lambda kc: (big[kc // 8][:, kc % 8, :T], f"big{kc // 8}"), None, T, kcs=32)
            P.op("vector", lambda e, pt=pt, m=m: e.tensor_tensor(out=x[:, m, :T], in0=pt[:, :T], in1=x[:, m, :T], op=ALU.add), r=[pn, "x"], w=["x"])

    def attn_layer(T, j, tile):
        i = 2 * j
        qT, oT = big[0], big[1]
        nch = T // 64
        rslot = tile["rslot"]
        rmsnorm(T, SM_MIX + 8 * i)
        if tile["kind"] == "prompt":
            P.op("gpsimd", lambda e: e.tensor_copy(out=KA[:, :, 0:128], in_=KAc[j][:]), r=[f"KAc{j}"], w=["KA"])
            P.op("gpsimd", lambda e: e.tensor_copy(out=KB[:, :, 0:128], in_=KBc[j][:]), r=[f"KBc{j}"], w=["KB"])
            P.op("gpsimd", lambda e: e.tensor_copy(out=Vst[:, 0:2, :], in_=Vc[j][:]), r=[f"Vc{j}"], w=["Vst"])
        else:
            s = tile["s"]
            for u in range(2):
                dma("sync", ckst[:, :, u, :], ck[j, s].rearrange("t (g d) -> t g d", g=4), [], ["ckst"], f"ck{u}")
            for g in range(4):
                pt, pn = next_pm()
                P.op("tensor", lambda e, pt=pt, g=g: e.transpose(out=pt[:, 0:128], in_=ckst[:, g, :, :].rearrange("t u d -> t (u d)"), identity=ident_f[:]),
                     r=["ckst", "ident_f"], w=[pn])
                P.op("scalar", lambda e, pt=pt, g=g: e.copy(out=KA[0:64, g, 0:128], in_=pt[0:64, 0:128]), r=[pn], w=["KA"])
                P.op("scalar", lambda e, pt=pt, g=g: e.copy(out=KB[64:128, g, 0:128], in_=pt[64:128, 0:128]), r=[pn], w=["KB"])
            for blk in range(2):
                for u in range(2):
                    dma("gpsimd", Vst[0:64, blk, :].rearrange("p (g u d) -> p g u d", g=4, u=2)[:, :, u, :],
                        cv[j, s, blk * 64:(blk + 1) * 64, :].rearrange("t (g d) -> t g d", g=4), [], ["Vst"], f"cv{blk}{u}")
            dma("sync", kso[j, s, 0:64, :], ck[j, s, 64:128, :], [], [], "kso_c")
            dma("sync", vso[j, s, 0:64, :], cv[j, s, 64:128, :], [], [], "vso_c")
        need_out = tile["last"]
        uni = tile.get("uniform", False)
        osel = tile.get("osel", j)
        if uni and tile["kind"] == "prompt":
            fc = SM_FLAG + tile["idx"]
            P.op("vector", lambda e: e.tensor_scalar(out=vones[:], in0=ones_b[0:64, :], scalar1=sm[0:64, fc:fc + 1], scalar2=None, op0=ALU.mult), r=["ones_b", "sm"], w=["vones"])
        import os
        STOP = os.environ.get("KSTOP", "")
        if STOP == "a":
            return
        for qb in range(4):
            W, wn = wget()
            for jj in range(2):
                cq = 2 * qb + jj
                pa, pan = proj_f(W, wn, (2 * jj) * 128, hb, "hb", T)
                pb, pbn = proj_f(W, wn, (2 * jj + 1) * 128, hb, "hb", T)
                t1, t1n = next_tmp()
                t2, t2n = next_tmp()
                P.op("vector", lambda e, pa=pa, t1=t1: e.tensor_tensor(out=t1[:, :T], in0=pa[:, :T], in1=rC[rslot][:, :T], op=ALU.mult), r=[pan, f"rC{rslot}"], w=[t1n])
                P.op("vector", lambda e, pb=pb, t2=t2: e.tensor_tensor(out=t2[:, :T], in0=pb[:, :T], in1=rS[rslot][:, :T], op=ALU.mult), r=[pbn, f"rS{rslot}"], w=[t2n])
                P.op("gpsimd", lambda e, t1=t1, t2=t2, cq=cq: e.tensor_tensor(out=qT[:, cq, :T], in0=t1[:, :T], in1=t2[:, :T], op=ALU.add), r=[t1n, t2n], w=["big0"])
        if STOP == "b1":
            return
        for kb in range(2):
            W, wn = wget()
            for jj in range(2):
                g = 2 * kb + jj
                pa, pan = proj_f(W, wn, (2 * jj) * 128, hb, "hb", T)
                pb, pbn = proj_f(W, wn, (2 * jj + 1) * 128, hb, "hb", T)
                t1, t1n = next_tmp()
                t2, t2n = next_tmp()
                P.op("vector", lambda e, pa=pa, t1=t1: e.tensor_tensor(out=t1[:, :T], in0=pa[:, :T], in1=rC[rslot][:, :T], op=ALU.mult), r=[pan, f"rC{rslot}"], w=[t1n])
                P.op("vector", lambda e, pb=pb, t2=t2: e.tensor_tensor(out=t2[:, :T], in0=pb[:, :T], in1=rS[rslot][:, :T], op=ALU.mult), r=[pbn, f"rS{rslot}"], w=[t2n])
                P.op("gpsimd", lambda e, t1=t1, t2=t2, g=g: e.tensor_tensor(out=KA[0:64, g, 128:128 + T], in0=t1[0:64, :T], in1=t2[0:64, :T], op=ALU.add), r=[t1n, t2n], w=["KA"])
                P.op("gpsimd", lambda e, t1=t1, t2=t2, g=g: e.tensor_tensor(out=KB[64:128, g, 128:128 + T], in0=t1[64:128, :T], in1=t2[64:128, :T], op=ALU.add), r=[t1n, t2n], w=["KB"])
                if need_out:
                    n0 = T - 128 if T >= 128 else 0
                    nn = T - n0
                    P.op("vector", lambda e, t1=t1, t2=t2, g=g, n0=n0, nn=nn: e.tensor_tensor(out=kf[0:64, g, 0:nn], in0=t1[0:64, n0:T], in1=t2[0:64, n0:T], op=ALU.add),
                         r=[t1n, t2n], w=["kf"])
        if STOP == "b2":
            return
        W, wn = wget()
        Wv = W[:].rearrange("p (k n) -> p k n", k=8)
        for cb in range(nch):
            pt, pn = next_pm()
            for kc in range(8):
                mm(pt[0:64, :], hb[:, kc, cb * 64:(cb + 1) * 64], Wv[:, kc, :], kc == 0, kc == 7, ["hb", wn], [pn])
            P.op("scalar", lambda e, pt=pt, cb=cb: e.copy(out=Vst[0:64, 2 + cb, :], in_=pt[0:64, :]), r=[pn], w=["Vst"])
            if need_out and cb >= nch - 2 and STOP != "b3":
                oi = cb - (nch - 2) if nch >= 2 else 0
                for g in range(4):
                    P.op("scalar", lambda e, pt=pt, oi=oi, g=g: e.copy(out=vf[0:64, oi, g * 64:(g + 1) * 64], in_=pt[0:64, g * 128:g * 128 + 64]), r=[pn], w=["vf"])
        if STOP in ("b", "b3"):
            return
        units = []
        for c in range(nch):
            gc = tile["chunk0"] + c
            if tile["kind"] == "prompt" and uni:
                slots = [c, c + 1, c + 2]
            elif tile["kind"] == "prompt":
                blocks = [b for b in (gc - 2, gc - 1, gc) if b >= 0]
                slots = [b - tile["chunk0"] + 2 for b in blocks]
            else:
                slots = [0, 1, 2]
            for cq in range(8):
                units.append((c, cq, slots))

        def bufs(k):
            ki = k % 4
            return ([(pS[0], "pS0"), (pS[1], "pS1"), (pm[0], "pm0"), (pm[1], "pm1")][ki], [(pO[0], "pO0"), (pO[1], "pO1"), (pm[2], "pm2"), (pm[3], "pm3")][ki],
                    (pT[ki], f"pT{ki}"), (rec[ki], f"rec{ki}"))

        def stage_a(k):
            c, cq, slots = units[k]
            g = cq // 2
            (pst, psn), _, (ptt, ptn), _ = bufs(k)
            nbk = len(slots)
            for bi, sl in enumerate(slots):
                for hh in range(2):
                    Kt, Kn = (KA, "KA") if hh == 0 else (KB, "KB")
                    mm(pst[0:64, bi * 128 + hh * 64: bi * 128 + hh * 64 + 64], Kt[:, g, sl * 64:(sl + 1) * 64], qT[:, cq, c * 64:(c + 1) * 64],
                       True, True, [Kn, "big0"], [psn])
            P.op("scalar", lambda e, pst=pst, ptt=ptt, nbk=nbk: e.activation(out=ptt[:, 0:nbk * 128], in_=pst[0:64, 0:nbk * 128], func=AF.Exp, scale=0.125),
                 r=[psn], w=[ptn])

        def stage_b(k):
            c, cq, slots = units[k]
            g = cq // 2
            _, (pot, pon), (ptt, ptn), (rct, rcn) = bufs(k)
            nbk = len(slots)
            for bi, sl in enumerate(slots):
                mm(pot[:, 0:128], Vst[0:64, sl, g * 128:(g + 1) * 128], ptt[:, bi * 128:(bi + 1) * 128], bi == 0, bi == nbk - 1, ["Vst", ptn], [pon])
            for bi, sl in enumerate(slots):
                if uni and tile["kind"] == "prompt" and sl < 2:
                    mm(pot[:, 128:256], vones[:], ptt[:, bi * 128:(bi + 1) * 128], bi == 0, False, ["vones", ptn], [pon])
                else:
                    mm(pot[:, 128:256], ones_b[0:64, :], ptt[:, bi * 128:(bi + 1) * 128], bi == 0, False, ["ones_b", ptn], [pon])
            mm(pot[:, 128:256], ones_b[0:2, :], eshl[0:2, j * 1024 + cq * 128: j * 1024 + (cq + 1) * 128], False, True, ["ones_b", "eshl"], [pon])
            P.op("scalar", lambda e, pot=pot, rct=rct: e.activation(out=rct[:], in_=pot[:, 128:256], func=AF.Ln), r=[pon], w=[rcn])
            P.op("scalar", lambda e, rct=rct: e.activation(out=rct[:], in_=rct[:], func=AF.Exp, scale=-1.0), r=[rcn], w=[rcn])
            for hh in range(2):
                lo = hh * 64
                P.op("vector", lambda e, pot=pot, rct=rct, lo=lo, cq=cq, c=c: e.tensor_tensor(out=oT[lo:lo + 64, cq, c * 64:(c + 1) * 64], in0=pot[lo:lo + 64, lo:lo + 64],
                                                                                         in1=rct[lo:lo + 64, lo:lo + 64], op=ALU.mult), r=[pon, rcn], w=["big1"])

        LOOK = 2
        for k in range(min(LOOK, len(units))):
            stage_a(k)
        for k in range(len(units)):
            if k + LOOK < len(units):
                stage_a(k + LOOK)
            stage_b(k)
        if STOP == "c":
            return
        if tile["kind"] == "prompt":
            P.op("gpsimd", lambda e: e.tensor_copy(out=KAc[j][:], in_=KA[:, :, T:T + 128]), r=["KA"], w=[f"KAc{j}"])
            P.op("gpsimd", lambda e: e.tensor_copy(out=KBc[j][:], in_=KB[:, :, T:T + 128]), r=["KB"], w=[f"KBc{j}"])
            P.op("gpsimd", lambda e: e.tensor_copy(out=Vc[j][:], in_=Vst[:, nch:nch + 2, :]), r=["Vst"], w=[f"Vc{j}"])
        if need_out:
            nn = min(T, 128)
            for g in range(4):
                pt, pn = next_pm()
                P.op("tensor", lambda e, pt=pt, g=g: e.transpose(out=pt[0:nn, 0:64], in_=kf[0:64, g, 0:nn], identity=ident_f[0:64, 0:64]), r=["kf", "ident_f"], w=[pn])
                P.op("vector", lambda e, pt=pt, g=g: e.tensor_copy(out=kvo[0:nn, g * 64:(g + 1) * 64], in_=pt[0:nn, 0:64]), r=[pn], w=["kvo"])
            if tile["kind"] == "prompt":
                dma("sync", kpo[osel], kvo[:], ["kvo"], [], "kvo")
                for oi in range(2):
                    dma("sync", vpo[osel, oi * 64:(oi + 1) * 64, :], vf[0:64, oi, :], ["vf"], [], f"vfo{oi}")
            else:
                s = tile["s"]
                dma("sync", kso[j, s, 64:128, :], kvo[0:64, :], ["kvo"], [], "kvo")
                dma("sync", vso[j, s, 64:128, :], vf[0:64, 0, :], ["vf"], [], "vfo0")
        if STOP == "d":
            return
        out_proj(T, oT, "big1")

    def hgrn_layer(T, j, tile):
        i = 2 * j + 1
        qt, kt, Vh, gate = big[0], big[1], big[2], big[3]
        ob = big[3]
        nch = T // 64
        npair = (T + 127) // 128
        Sbufs = [(S[j], f"S{j}"), (S2[j], f"S{j}b")]
        Sj, Sn = Sbufs[scur[j]]
        rmsnorm(T, SM_MIX + 8 * i)
        first = tile["first"]
        if tile["kind"] == "sample":
            dma("sync", Sj[:], st[j, tile["s"]].rearrange("h k v -> k h v"), [], [f"{Sn}h{h}" for h in range(8)], f"stl{j}")
        Vh2 = Vh[:].rearrange("p a b -> p (a b)").rearrange("p (t n) -> p t n", n=1024)
        for b in range(2):
            W, wn = wget()
            Wv = W[:].rearrange("p (k n) -> p k n", k=8)
            for tb in range(npair):
                np_ = min(128, T - tb * 128)
                pt, pn = next_pm()
                for kc in range(8):
                    mm(pt[0:np_, :], hb[:, kc, tb * 128: tb * 128 + np_], Wv[:, kc, :], kc == 0, kc == 7, ["hb", wn], [pn])
                P.op("scalar", lambda e, pt=pt, tb=tb, b=b, np_=np_: e.copy(out=Vh2[0:np_, tb, b * 512:(b + 1) * 512], in_=pt[0:np_, :]), r=[pn], w=["big2"])
        for b in range(2):
            W, wn = wget()
            for m in range(4):
                h = 4 * b + m
                pt, pn = proj_f(W, wn, m * 128, hb, "hb", T)
                P.op("scalar", lambda e, pt=pt, h=h: e.activation(out=gate[:, h, :T], in_=pt[:, :T], func=AF.Silu), r=[pn], w=["big3"])
        pbfs = [pm[2][:].bitcast(BF16), pm[3][:].bitcast(BF16)]
        cnt["pm_n"] = 2
        for b in range(4):
            W, wn = wget()
            hs = (2 * b, 2 * b + 1)
            ctx = {}
            for jj, h in enumerate(hs):
                pf, pfn = proj_f(W, wn, (2 * jj) * 128, hb, "hb", T)
                pq, pqn = proj_f(W, wn, (2 * jj + 1) * 128, hb, "hb", T)
                c = dict(h=h, pf=pf, pfn=pfn)
                c["zq"], c["zqn"] = next_tmp()
                P.op("scalar", lambda e, pq=pq, c=c: e.copy(out=c["zq"][:, :T], in_=pq[:, :T]), r=[pqn], w=[c["zqn"]])
                c["sig"], c["sign"] = next_tmp()
                c["omu"], c["omun"] = next_tmp()
                c["bt"], c["btn"] = next_tmp()
                c["br"], c["brn"] = next_tmp()
                c["e1"], c["e1n"] = next_tmp()
                c["omlh"] = oml[:, j, h:h + 1]
                c["lbh"] = lb[:, j, h:h + 1]
                c["bt3"] = c["bt"][:, :T].rearrange("p (c s) -> p c s", s=64)
                c["br3"] = c["br"][:, :T].rearrange("p (c s) -> p c s", s=64)
                ctx[h] = c
                P.op("scalar", lambda e, c=c: e.activation(out=c["sig"][:, :T], in_=c["pf"][:, :T], func=AF.Sigmoid), r=[c["pfn"]], w=[c["sign"]])
            for h in hs:
                c = ctx[h]
                P.op("vector", lambda e, c=c: e.tensor_scalar(out=c["sig"][:, :T], in0=c["sig"][:, :T], scalar1=c["omlh"], scalar2=c["lbh"], op0=ALU.mult, op1=ALU.add),
                     r=[c["sign"], "oml", "lb"], w=[c["sign"]])
                P.op("gpsimd", lambda e, c=c: e.tensor_scalar(out=c["omu"][:, :T], in0=c["sig"][:, :T], scalar1=-1.0, scalar2=1.0, op0=ALU.mult, op1=ALU.add),
                     r=[c["sign"]], w=[c["omun"]])
            for h in hs:
                c = ctx[h]
                P.op("scalar", lambda e, c=c: e.activation(out=c["e1"][:, :T], in_=c["sig"][:, :T], func=AF.Ln), r=[c["sign"]], w=[c["e1n"]])
            for h in hs:
                c = ctx[h]
                P.op("vector", lambda e, c=c: e.tensor_tensor_scan(out=c["bt"][:, :T], data0=resetm[:, :T], data1=c["e1"][:, :T], initial=0.0, op0=ALU.mult, op1=ALU.add),
                     r=[c["e1n"], "resetm"], w=[c["btn"]])
                P.op("vector", lambda e, c=c: e.tensor_tensor(out=c["br3"], in0=c["bt3"], in1=c["bt3"][:, :, 31:32].to_broadcast([128, nch, 64]), op=ALU.subtract),
                     r=[c["btn"]], w=[c["brn"]])
                P.op("vector", lambda e, c=c, h=h: e.tensor_tensor(out=dl[:, h, 0:nch], in0=c["bt3"][:, :, 63], in1=c["bt3"][:, :, 31], op=ALU.subtract), r=[c["btn"]], w=["dl"])
            for h in hs:
                c = ctx[h]
                P.op("scalar", lambda e, c=c: e.activation(out=c["e1"][:, :T], in_=c["br"][:, :T], func=AF.Exp), r=[c["brn"]], w=[c["e1n"]])
                P.op("scalar", lambda e, c=c: e.activation(out=c["br"][:, :T], in_=c["br"][:, :T], func=AF.Exp, scale=-1.0), r=[c["brn"]], w=[c["brn"]])
                P.op("scalar", lambda e, c=c, h=h: e.activation(out=Elast[:, h, 0:nch], in_=c["bt3"][:, :, 63], func=AF.Exp), r=[c["btn"]], w=["Elast"])
                P.op("scalar", lambda e, c=c, h=h: e.activation(out=Emid[:, h, 0:nch], in_=c["bt3"][:, :, 31], func=AF.Exp), r=[c["btn"]], w=["Emid"])
                P.op("scalar", lambda e, h=h: e.activation(out=Elm[:, h, 0:nch], in_=dl[:, h, 0:nch], func=AF.Exp), r=["dl"], w=["Elm"])
            for h in hs:
                c = ctx[h]
                P.op("vector", lambda e, c=c, h=h: e.tensor_tensor(out=kt[:, h, :T], in0=c["omu"][:, :T], in1=c["br"][:, :T], op=ALU.mult),
                     r=[c["omun"], c["brn"]], w=["big1"])
                kz, kzn = ktz[h % 2], f"ktz{h % 2}"
                kz2, kz2n = ktz2[h % 2], f"ktzb{h % 2}"
                c.update(kz=kz, kzn=kzn, kz2=kz2, kz2n=kz2n)
                if T >= 128:
                    om4 = c["omu"][:, :T].rearrange("p (a b c) -> p a b c", b=2, c=64)
                    br4 = c["br"][:, :T].rearrange("p (a b c) -> p a b c", b=2, c=64)
                    P.op("vector", lambda e, kz=kz, om4=om4, br4=br4: e.tensor_tensor(out=kz[:, 0:npair * 192].rearrange("p (a b c) -> p a b c", b=3, c=64)[:, :, 0::2, :],
                                                                                   in0=om4, in1=br4, op=ALU.mult), r=[c["omun"], c["brn"]], w=[kzn])
                    P.op("gpsimd", lambda e, kz2=kz2, om4=om4, br4=br4: e.tensor_tensor(out=kz2[:, 0:npair * 192].rearrange("p (a b c) -> p a b c", b=3, c=64)[:, :, 0::2, 0:32],
                                                                                     in0=om4[:, :, :, 0:32], in1=br4[:, :, :, 0:32], op=ALU.mult), r=[c["omun"], c["brn"]], w=[kz2n])
                else:
                    P.op("gpsimd", lambda e, kz=kz, h=h: e.tensor_copy(out=kz[:, 0:64], in_=kt[:, h, 0:64]), r=["big1"], w=[kzn])
                    P.op("gpsimd", lambda e, kz2=kz2, h=h: e.tensor_copy(out=kz2[:, 0:32], in_=kt[:, h, 0:32]), r=["big1"], w=[kz2n])
            for h in hs:
                c = ctx[h]
                P.op("scalar", lambda e, c=c: e.activation(out=c["sig"][:, :T], in_=c["zq"][:, :T], func=AF.Silu), r=[c["zqn"], c["sign"]], w=[c["sign"]])
                P.op("vector", lambda e, c=c, h=h: e.tensor_tensor(out=qt[:, h, :T], in0=c["sig"][:, :T], in1=c["e1"][:, :T], op=ALU.mult), r=[c["sign"], c["e1n"]], w=["big0"])
            ohs = {hs[0]: (oh, "oh", osq, "osq"), hs[1]: (oh2, "oh2", osq2, "osq2")}
            import os
            for hs_run in ([hs] if not os.environ.get('KSEQ') else [(hs[0],), (hs[1],)]):
                par = scur[j]
                for p in range(npair):
                    np_ = min(128, T - p * 128)
                    t0 = p * 128
                    ncc = np_ // 64
                    stb = [next_pm() for _ in range(ncc)]
                    for hi, h in [(hh % 2, hh) for hh in hs_run]:
                        c = ctx[h]
                        kz, kzn, kz2, kz2n = c["kz"], c["kzn"], c["kz2"], c["kz2n"]
                        pbv, pbn = pbfs[hi][:, 0:128], f"pm{2 + hi}"
                        psv, psn_ = pS[hi][:, 0:128], f"pS{hi}"
                        P.op("tensor", lambda e, h=h, t0=t0, np_=np_, pbv=pbv: e.transpose(out=pbv[0:np_, :], in_=kt[:, h, t0:t0 + np_], identity=ident_b[:]),
                             r=["big1", "ident_b"], w=[pbn])
                        P.op("scalar", lambda e, h=h, p=p, np_=np_, pbv=pbv: e.copy(out=ktok[0:np_, h, p, :], in_=pbv[0:np_, :]), r=[pbn], w=[f"ktok{hi}"])
                        if np_ == 128:
                            mm(psv[:, 0:32], kz2[:, p * 192: p * 192 + 128], qt[:, h, t0:t0 + 32], True, True, [kz2n, "big0"], [psn_])
                            mm(psv[:, 32:64], kz[:, p * 192: p * 192 + 128], qt[:, h, t0 + 32:t0 + 64], True, True, [kzn, "big0"], [psn_])
                            mm(psv[:, 64:96], kz2[:, p * 192 + 64: p * 192 + 192], qt[:, h, t0 + 64:t0 + 96], True, True, [kz2n, "big0"], [psn_])
                            mm(psv[:, 96:128], kz[:, p * 192 + 64: p * 192 + 192], qt[:, h, t0 + 96:t0 + 128], True, True, [kzn, "big0"], [psn_])
                        else:
                            mm(psv[0:64, 0:32], kz2[:, 0:64], qt[:, h, 0:32], True, True, [kz2n, "big0"], [psn_])
                            mm(psv[0:64, 32:64], kz[:, 0:64], qt[:, h, 32:64], True, True, [kzn, "big0"], [psn_])
                        smk, smn = smask[hi], f"smask{hi}"
                        P.op("vector", lambda e, smk=smk, np_=np_, psv=psv: e.tensor_tensor(out=smk[0:np_, 0:np_], in0=psv[0:np_, 0:np_], in1=mask_f[0:np_, 0:np_], op=ALU.mult),
                             r=[psn_, "mask_f"], w=[smn])
                        pot, pon = pO[hi], f"pO{hi}"
                        mm(pot[:, 0:np_], Vh2[0:np_, p, h * 128:(h + 1) * 128], smk[0:np_, 0:np_], True, False, ["big2", smn], [pon])
                        for cc in range(ncc):
                            lo = cc * 64
                            pst, psn = stb[cc]
                            mm(pst[:, hi * 128:(hi + 1) * 128], ktok[lo:lo + 64, h, p, :], Vh2[lo:lo + 64, p, h * 128:(h + 1) * 128], True, True, [f"ktok{hi}", "big2"], [psn])
                    for cc in range(ncc):
                        ci = p * 2 + cc
                        lastmm = cc == ncc - 1
                        import os
                        PP = not os.environ.get("KNOPP")
                        (Sc, Scn), (Sx, Sxn) = Sbufs[par], Sbufs[(1 - par) if PP else par]
                        for hi, h in [(hh % 2, hh) for hh in hs_run]:
                            c = ctx[h]
                            pot, pon = pO[hi], f"pO{hi}"
                            pst, psn = stb[cc]
                            P.op("scalar", lambda e, h=h, ci=ci, Sc=Sc: e.activation(out=Sb[:, h, :], in_=Sc[:, h, :], func=AF.Copy, scale=Emid[:, h, ci:ci + 1]),
                                 r=[f"{Scn}h{h}", "Emid"], w=[f"Sb{h}"])
                            mm(pot[:, cc * 64:(cc + 1) * 64], Sb[:, h, :], qt[:, h, t0 + cc * 64: t0 + (cc + 1) * 64], False, lastmm, [f"Sb{h}", "big0"], [pon])
                            P.op("vector", lambda e, h=h, ci=ci, Sc=Sc, Sx=Sx: e.tensor_scalar(out=Sx[:, h, :], in0=Sc[:, h, :], scalar1=Elast[:, h, ci:ci + 1], scalar2=None, op0=ALU.mult),
                                 r=[f"{Scn}h{h}", "Elast"], w=[f"{Sxn}h{h}"])
                            P.op("vector", lambda e, pst=pst, h=h, ci=ci, hi=hi, Sx=Sx: e.scalar_tensor_tensor(out=Sx[:, h, :], in0=pst[:, hi * 128:(hi + 1) * 128], scalar=Elm[:, h, ci:ci + 1], in1=Sx[:, h, :],
                                                                                               op0=ALU.mult, op1=ALU.add), r=[psn, "Elm", f"{Sxn}h{h}"], w=[f"{Sxn}h{h}"])
                        par = (1 - par) if PP else par
                    for hi, h in [(hh % 2, hh) for hh in hs_run]:
                        pot, pon = pO[hi], f"pO{hi}"
                        oht, ohn, oqt, oqn = ohs[h]
                        P.op("scalar", lambda e, pot=pot, t0=t0, np_=np_, oht=oht: e.copy(out=oht[:, t0:t0 + np_], in_=pot[:, 0:np_]), r=[pon], w=[ohn])
                        P.op("scalar", lambda e, pot=pot, t0=t0, np_=np_, oqt=oqt: e.activation(out=oqt[:, t0:t0 + np_], in_=pot[:, 0:np_], func=AF.Square), r=[pon], w=[oqn])
            sds = {}
            for h in hs:
                oht, ohn, oqt, oqn = ohs[h]
                pt, pn = next_pm()
                mm(pt[:, :T], ones_f[:], oqt[:, :T], True, True, ["ones_f", oqn], [pn])
                sds[h] = (pt, pn) + next_tmp()
            for h in hs:
                pt, pn, sd, sdn = sds[h]
                P.op("scalar", lambda e, pt=pt, sd=sd: e.activation(out=sd[:, :T], in_=pt[:, :T], func=AF.Ln, scale=1.0 / 128, bias=epsb[:, 0:1]), r=[pn, "epsb"], w=[sdn])
                P.op("scalar", lambda e, sd=sd: e.activation(out=sd[:, :T], in_=sd[:, :T], func=AF.Exp, scale=-0.5), r=[sdn], w=[sdn])
            for h in hs:
                pt, pn, sd, sdn = sds[h]
                oht, ohn, oqt, oqn = ohs[h]
                P.op("vector", lambda e, sd=sd, oht=oht: e.scalar_tensor_tensor(out=sd[:, :T], in0=oht[:, :T], scalar=sm[:, SM_GNO + j:SM_GNO + j + 1], in1=sd[:, :T], op0=ALU.mult, op1=ALU.mult),
                     r=[ohn, "sm", sdn], w=[sdn])
                P.op("gpsimd", lambda e, sd=sd, h=h: e.tensor_tensor(out=ob[:, h, :T], in0=sd[:, :T], in1=gate[:, h, :T], op=ALU.mult), r=[sdn, "big3"], w=["big3"])
        cnt["pm_n"] = 4
        import os
        if not os.environ.get("KNOPP"):
            scur[j] ^= (nch % 2)
        Sj, Sn = Sbufs[scur[j]]
        Sres = [f"{Sn}h{h}" for h in range(8)]
        if tile["last"]:
            if tile["kind"] == "prompt":
                dma("sync", spo[tile.get("osel", j)].rearrange("h k v -> k h v"), Sj[:], Sres, [], f"so{j}")
            else:
                dma("sync", sso[j, tile["s"]].rearrange("h k v -> k h v"), Sj[:], Sres, [], f"so{j}")
        out_proj(T, ob, "big3")

    tiles = []
    if pipe:
        for t in range(NSTEP):
            tiles.append(dict(kind="prompt", T=TT, chunk0=t * 8, first=False, last=(t >= NT - 1), osel=(0 if t == NT - 1 else 1), pos0=t * TT, idx=t, uniform=True))
        for s in range(NS):
            tiles.append(dict(kind="sample", T=64, chunk0=0, first=False, last=True, pos0=SEQ + 64 * s, s=s, idx=NSTEP + s, uniform=True))
    else:
        for t in range(NT):
            tiles.append(dict(kind="prompt", T=TT, chunk0=t * 8, first=(t == 0), last=(t == NT - 1), pos0=t * TT, idx=t))
        for s in range(NS):
            tiles.append(dict(kind="sample", T=64, chunk0=0, first=False, last=True, pos0=SEQ, s=s, idx=NT + s))
    RG = [[0, 1], [2, 3], [4, 5], [6, 7]]
    per_pass = []
    off = 0
    for i in range(depth):
        n = NB_ATT if i % 2 == 0 else NB_HG
        per_pass += list(range(off, off + n))
        off += n
    for _ in tiles:
        wstate["seq"] += per_pass

    for ti, tile in enumerate(tiles):
        T = tile["T"]
        rslot = 0
        tile["rslot"] = rslot
        dma("sync", rC[rslot][:, :T], ropeC[:, tile["pos0"]:tile["pos0"] + T], [], [f"rC{rslot}"], f"rC{rslot}")
        dma("sync", rS[rslot][:, :T], ropeS[:, tile["pos0"]:tile["pos0"] + T], [], [f"rS{rslot}"], f"rS{rslot}")
        if pipe:
            if tile["kind"] == "prompt":
                P.op("gpsimd", lambda e: e.collective_compute("AllGather", ALU.bypass, replica_groups=RG, ins=[ein[:]], outs=[eout[:]]), r=["ein"], w=["eout"], stream="cc", inc=1)
            else:
                P.op("gpsimd", lambda e: e.collective_compute("AllGather", ALU.bypass, replica_groups=RG, ins=[eins[:]], outs=[eouts[:]]), r=["eins"], w=["eouts"], stream="cc", inc=1)
        ntb = (T + 127) // 128
        for tb in range(ntb):
            np_ = min(128, T - tb * 128)
            for half in range(2):
                cs = slice(half * 512, (half + 1) * 512)
                xi, xn = next_tmp()
                if tile["kind"] == "prompt":
                    src = xp[tile["pos0"] + tb * 128: tile["pos0"] + tb * 128 + np_, cs]
                else:
                    src = xs[tile["s"], :, cs]
                dma("sync", xi[0:np_, :], src, [], [xn], xn + "i")
                if pipe:
                    x2, x2n = next_tmp()
                    if tile["kind"] == "prompt":
                        src2, s2n = eout[tb * 128: tb * 128 + np_, cs], "eout"
                    else:
                        src2, s2n = eouts[0:64, cs], "eouts"
                    dma("sync", x2[0:np_, :], src2, [s2n], [x2n], x2n + "i")
                    P.op("vector", lambda e, xi=xi, np_=np_: e.tensor_scalar(out=xi[0:np_, :], in0=xi[0:np_, :], scalar1=sm[0:np_, SM_SELA:SM_SELA + 1], scalar2=None, op0=ALU.mult),
                         r=[xn, "sm"], w=[xn])
                    P.op("vector", lambda e, xi=xi, x2=x2, np_=np_: e.scalar_tensor_tensor(out=xi[0:np_, :], in0=x2[0:np_, :], scalar=sm[0:np_, SM_SELB:SM_SELB + 1], in1=xi[0:np_, :],
                                                                                       op0=ALU.mult, op1=ALU.add), r=[x2n, xn, "sm"], w=[xn])
                pt, pn = next_pm()
                for cc in range(4):
                    P.op("tensor", lambda e, pt=pt, xi=xi, cc=cc, np_=np_: e.transpose(out=pt[:, cc * 128: cc * 128 + np_], in_=xi[0:np_, cc * 128:(cc + 1) * 128], identity=ident_f[0:np_, 0:np_]),
                         r=[xn, "ident_f"], w=[pn])
                P.op("scalar", lambda e, pt=pt, half=half, tb=tb, np_=np_: e.copy(out=x[:, half * 4:half * 4 + 4, tb * 128: tb * 128 + np_],
                                                                                 in_=pt[:].rearrange("p (c t) -> p c t", c=4)[:, :, 0:np_]), r=[pn], w=["x"])
        for i in range(depth):
            if i % 2 == 0:
                attn_layer(T, i // 2, tile)
            else:
                hgrn_layer(T, i // 2, tile)
            mlp(T, i)
        for phase in (("ex", "y") if pipe else ("y",)):
            if phase == "y":
                rmsnorm(T, SM_FIN, final=True)
            for tb in range(ntb):
                np_ = min(128, T - tb * 128)
                for half in range(2):
                    cs = slice(half * 512, (half + 1) * 512)
                    pt, pn = next_pm()
                    for cc in range(4):
                        c = half * 4 + cc
                        P.op("tensor", lambda e, pt=pt, c=c, cc=cc, tb=tb, np_=np_: e.transpose(out=pt[0:np_, cc * 128:(cc + 1) * 128], in_=x[:, c, tb * 128: tb * 128 + np_], identity=ident_f[:]),
                             r=["x", "ident_f"], w=[pn])
                    yo, yn = next_tmp()
                    P.op("vector", lambda e, pt=pt, yo=yo, np_=np_: e.tensor_copy(out=yo[0:np_, :], in_=pt[0:np_, :]), r=[pn], w=[yn])
                    if phase == "ex":
                        if tile["kind"] == "prompt":
                            dma("sync", ein[tb * 128: tb * 128 + np_, cs], yo[0:np_, :], [yn], ["ein"], yn + "o")
                        else:
                            dma("sync", eins[0:64, cs], yo[0:np_, :], [yn], ["eins"], yn + "o")
                    else:
                        if tile["kind"] == "prompt":
                            dst = yp[tile["pos0"] + tb * 128: tile["pos0"] + tb * 128 + np_, cs]
                        else:
                            dst = ys[tile["s"], :, cs]
                        dma("sync", dst, yo[0:np_, :], [yn], [], yn + "o")

    P.emit(nc, es)
    es.close()
    return nc


def host_consts(SEQ):
    ident = np.eye(128, dtype=np.float32)
    m = np.zeros((128, 128), np.float32)
    for s in range(128):
        for t in range(128):
            if s // 64 == t // 64 and s <= t:
                m[s, t] = 1.0
    reset = np.ones((128, TT), np.float32)
    reset[:, 0::64] = 0.0
    pos = np.concatenate([np.arange(SEQ, dtype=np.float32), 2048.0 + np.arange(64, dtype=np.float32)])
    C, S = rope_tables(pos)
    return dict(cident=ident, cmask=m, creset=reset, ropeC=np.ascontiguousarray(C), ropeS=np.ascontiguousarray(S))


def host_small(mixer_norm, mlp_norm, final_norm, hgrn_out_norm, hgrn_lb):
    sm = np.zeros((128, 128), np.float32)
    for i in range(4):
        sm[:, 0 + 8 * i: 8 + 8 * i] = mixer_norm[i].reshape(8, 128).T
        sm[:, 32 + 8 * i: 40 + 8 * i] = mlp_norm[i].reshape(8, 128).T
    sm[:, 64:72] = final_norm.reshape(8, 128).T
    for j in range(2):
        sm[:, 72 + j] = hgrn_out_norm[j]
        sm[:, 80 + 8 * j: 88 + 8 * j] = hgrn_lb[j].reshape(8, 128).T
    return sm


_CACHE = {}


def run_pipe(inputs, NT=16):
    f = lambda k: np.asarray(inputs[k], dtype=np.float32)
    NSTEP = NT + 1
    SEQ = NSTEP * TT
    key = ("pipe", NT)
    if key not in _CACHE:
        _CACHE[key] = build_program(NT, 3, 2, pipe=True)
    nc = _CACHE[key]
    wb = build_weight_blocks(f("attn_w_qkv"), f("attn_w_o"), f("hgrn_w_in"), f("hgrn_w_o"), f("mlp_w_up"), f("mlp_w_down"))
    nbh = NB_ATT + NB_HG
    ident = np.eye(128, dtype=np.float32)
    base = host_consts(TT)
    mixer_norm, mlp_norm, final_norm = f("mixer_norm"), f("mlp_norm"), f("final_norm")
    gno, lbr, sinks = f("hgrn_out_norm"), f("hgrn_lb"), f("attn_sinks")
    xp_, xs_, ck_, cv_, st_ = f("x_prompt"), f("x_sample"), f("cache_k"), f("cache_v"), f("state_s")
    ck_ = ck_.reshape(2, 8, 128, 256)
    cv_ = cv_.reshape(2, 8, 128, 256)
    spos = 2048.0 + np.arange(64, dtype=np.float32)
    in_maps = []
    for c in range(8):
        b, role = c // 2, c % 2
        sm = np.zeros((128, 128), np.float32)
        for li in range(2):
            L = 2 * role + li
            sm[:, 8 * li: 8 * li + 8] = mixer_norm[L].reshape(8, 128).T
            sm[:, 32 + 8 * li: 40 + 8 * li] = mlp_norm[L].reshape(8, 128).T
        sm[:, 64:72] = final_norm.reshape(8, 128).T
        sm[:, 72] = gno[role]
        for j in range(2):
            sm[:, 80 + 8 * j: 88 + 8 * j] = lbr[j].reshape(8, 128).T
        for t in range(NSTEP):
            sm[:, 96 + t] = 1.0 if (t - role) >= 1 else 0.0
        sm[:, 120] = 1.0 if role == 0 else 0.0
        sm[:, 121] = 1.0 if role == 1 else 0.0
        sm[:, 122] = 1.0 if role == 1 else 0.0
        sinkrow = np.zeros((1, 2048), np.float32)
        sinkrow[0, :1024] = np.repeat(sinks[role], 64)
        xp = np.zeros((SEQ, D), np.float32)
        xs = np.zeros((3, 64, D), np.float32)
        ck = np.zeros((1, 3, 128, 256), np.float32)
        cv = np.zeros((1, 3, 128, 256), np.float32)
        st = np.zeros((1, 3, 8, 128, 128), np.float32)
        pos = np.zeros((SEQ + 192,), np.float32)
        if role == 0:
            xp[:NT * TT] = xp_[b, :NT * TT]
            xp[NT * TT:] = xp_[b, (NT - 1) * TT:NT * TT]
            xs[0:2] = xs_[2 * b:2 * b + 2]
            pos[:NT * TT] = np.arange(NT * TT, dtype=np.float32)
            so = 0
        else:
            pos[TT:TT + NT * TT] = np.arange(NT * TT, dtype=np.float32)
            so = 1
        ck[0, so:so + 2] = ck_[role, 2 * b:2 * b + 2]
        cv[0, so:so + 2] = cv_[role, 2 * b:2 * b + 2]
        st[0, so:so + 2] = st_[role, 2 * b:2 * b + 2]
        for k in range(3):
            pos[SEQ + 64 * k: SEQ + 64 * (k + 1)] = spos
        C, S_ = rope_tables(pos)
        m = dict(xp=xp, xs=xs, ck=ck, cv=cv, st=st, wblk=np.ascontiguousarray(wb[role * nbh:(role + 1) * nbh]), small=sm, sinkrow=sinkrow,
                 cident=ident, cmask=base["cmask"], creset=base["creset"], ropeC=np.ascontiguousarray(C), ropeS=np.ascontiguousarray(S_))
        in_maps.append(m)
    res = run_bass_kernel_spmd(nc, in_maps, core_ids=list(range(8)))
    R = list(res.results)
    B = 4
    yp = np.stack([R[2 * b + 1]["yp"][TT:TT + NT * TT] for b in range(B)])
    ys = np.concatenate([R[2 * b + 1]["ys"][1:3] for b in range(B)], 0)
    kp = np.stack([np.stack([R[2 * b + r]["kpo"][r] for b in range(B)]) for r in range(2)]).reshape(2, B, 128, 4, 64)
    vp = np.stack([np.stack([R[2 * b + r]["vpo"][r] for b in range(B)]) for r in range(2)]).reshape(2, B, 128, 4, 64)
    sp = np.stack([np.stack([R[2 * b + r]["spo"][r] for b in range(B)]) for r in range(2)])
    ks = np.stack([np.concatenate([R[2 * b + r]["kso"][0, r:r + 2] for b in range(B)], 0) for r in range(2)]).reshape(2, 8, 128, 4, 64)
    vs = np.stack([np.concatenate([R[2 * b + r]["vso"][0, r:r + 2] for b in range(B)], 0) for r in range(2)]).reshape(2, 8, 128, 4, 64)
    ss = np.stack([np.concatenate([R[2 * b + r]["sso"][0, r:r + 2] for b in range(B)], 0) for r in range(2)])
    return tuple(np.ascontiguousarray(a, dtype=np.float32) for a in (yp, ys, kp, vp, sp, ks, vs, ss))


def run(inputs, NT=16, n_cores=4, depth=4):
    f = lambda k: np.asarray(inputs[k], dtype=np.float32)
    SEQ = NT * TT
    key = (NT, depth)
    if key not in _CACHE:
        _CACHE[key] = build_program(NT, 2, depth)
    nc = _CACHE[key]
    wb = build_weight_blocks(f("attn_w_qkv"), f("attn_w_o"), f("hgrn_w_in"), f("hgrn_w_o"), f("mlp_w_up"), f("mlp_w_down"))
    consts = host_consts(SEQ)
    sm = host_small(f("mixer_norm"), f("mlp_norm"), f("final_norm"), f("hgrn_out_norm"), f("hgrn_lb"))
    sinkrow = np.repeat(f("attn_sinks").reshape(-1), 64).reshape(1, 2048).astype(np.float32)
    xp_, xs_, ck_, cv_, st_ = f("x_prompt"), f("x_sample"), f("cache_k"), f("cache_v"), f("state_s")
    in_maps = []
    for c in range(n_cores):
        b = c % 4
        m = dict(xp=np.ascontiguousarray(xp_[b, :SEQ]), xs=np.ascontiguousarray(xs_[2 * b:2 * b + 2]),
                 ck=np.ascontiguousarray(ck_[:, 2 * b:2 * b + 2].reshape(2, 2, 128, 256)),
                 cv=np.ascontiguousarray(cv_[:, 2 * b:2 * b + 2].reshape(2, 2, 128, 256)),
                 st=np.ascontiguousarray(st_[:, 2 * b:2 * b + 2]), wblk=wb, small=sm, sinkrow=sinkrow)
        m.update(consts)
        in_maps.append(m)
    res = run_bass_kernel_spmd(nc, in_maps, core_ids=list(range(n_cores)))
    R = list(res.results)
    R = [R[b % len(R)] for b in range(4)]
    B = 4
    yp = np.stack([R[b]["yp"] for b in range(B)]).reshape(B, SEQ, D)
    ys = np.concatenate([R[b]["ys"] for b in range(B)], 0).reshape(8, 64, D)
    kp = np.stack([R[b]["kpo"] for b in range(B)], 1).reshape(2, B, 128, 4, 64)
    vp = np.stack([R[b]["vpo"] for b in range(B)], 1).reshape(2, B, 128, 4, 64)
    sp = np.stack([R[b]["spo"] for b in range(B)], 1).reshape(2, B, 8, 128, 128)
    ks = np.concatenate([R[b]["kso"] for b in range(B)], 1).reshape(2, 8, 128, 4, 64)
    vs = np.concatenate([R[b]["vso"] for b in range(B)], 1).reshape(2, 8, 128, 4, 64)
    ss = np.concatenate([R[b]["sso"] for b in range(B)], 1).reshape(2, 8, 8, 128, 128)
    return tuple(np.ascontiguousarray(a, dtype=np.float32) for a in (yp, ys, kp, vp, sp, ks, vs, ss))


def kernel(**inputs):
    return run_pipe(inputs, NT=16)
```

```python
import numpy as np
from contextlib import ExitStack
import concourse.bass as bass
import concourse.mybir as mybir
from concourse.bass_utils import run_bass_kernel_spmd

F32 = mybir.dt.float32
BF16 = mybir.dt.bfloat16
AF = mybir.ActivationFunctionType
ALU = mybir.AluOpType

D = 1024
TT = 512
EPS = 1e-5
ENG = ["tensor", "vector", "scalar", "gpsimd", "sync"]
SAME_ENG_SYNC = True
SEM_ROT = 20000


class Op:
    __slots__ = ("eng", "fn", "deps", "sig", "idx", "stream", "val", "cnt", "inc")


class Prog:
    def __init__(self):
        self.ops = {e: [] for e in ENG}
        self.lastw = {}
        self.readers = {}
        self.stream_cnt = {}

    def op(self, eng, fn, r=(), w=(), stream=None, inc=16):
        o = Op()
        o.eng, o.fn, o.deps, o.sig, o.stream, o.val, o.cnt = eng, fn, set(), False, stream, 0, 0
        for x in r:
            lw = self.lastw.get(x)
            if lw is not None:
                o.deps.add(lw)
        for x in w:
            lw = self.lastw.get(x)
            if lw is not None:
                o.deps.add(lw)
            for rd in self.readers.get(x, ()):
                o.deps.add(rd)
        for x in r:
            self.readers.setdefault(x, []).append(o)
        for x in w:
            self.lastw[x] = o
            self.readers[x] = []
        o.deps.discard(o)
        if stream is not None:
            self.stream_cnt[stream] = self.stream_cnt.get(stream, 0) + inc
            o.val = self.stream_cnt[stream]
            o.inc = inc
        o.idx = len(self.ops[eng])
        self.ops[eng].append(o)
        return o

    def emit(self, nc, es):
        for e in ENG:
            for o in self.ops[e]:
                best = {}
                keep = []
                for d in o.deps:
                    if d.stream is not None:
                        keep.append(d)
                        continue
                    if d.eng == e and (e == "tensor" or not SAME_ENG_SYNC):
                        continue
                    if d.eng not in best or best[d.eng].idx < d.idx:
                        best[d.eng] = d
                for d in best.values():
                    d.sig = True
                    keep.append(d)
                o.deps = keep
        for e in ENG:
            c = 0
            for o in self.ops[e]:
                if o.sig and o.stream is None:
                    c += 1
                    o.cnt = c
        sems = {}

        def getsem(key):
            if key not in sems:
                sems[key] = es.enter_context(nc.semaphore("s_" + str(key).replace(" ", "")))
            return sems[key]

        def semval(d):
            if d.stream is not None:
                return getsem(("st", d.stream)), d.val
            k = (d.cnt - 1) // SEM_ROT
            return getsem((d.eng, k)), d.cnt - k * SEM_ROT

        block = es.enter_context(nc.Block())

        def run(e):
            def body(engobj):
                known = {}
                for o in self.ops[e]:
                    need = {}
                    for d in o.deps:
                        s, v = semval(d)
                        if need.get(s, (None, 0))[1] < v:
                            need[s] = (s, v)
                    for s, v in need.values():
                        if known.get(s, 0) < v:
                            engobj.wait_ge(s, v)
                            known[s] = v
                    ins = o.fn(engobj)
                    if o.stream is not None:
                        ins.then_inc(getsem(("st", o.stream)), o.inc)
                    elif o.sig:
                        k = (o.cnt - 1) // SEM_ROT
                        ins.then_inc(getsem((e, k)), 1)
                for st, tot in self.stream_cnt.items():
                    if self.stream_eng.get(st) == e:
                        engobj.wait_ge(getsem(("st", st)), tot)
            getattr(block, e)(body)

        self.stream_eng = {}
        for e in ENG:
            for o in self.ops[e]:
                if o.stream is not None:
                    self.stream_eng[o.stream] = e
        for e in ENG:
            run(e)


def _fblock(cols):
    return np.ascontiguousarray(cols.reshape(8, 128, 512).transpose(1, 0, 2)).reshape(128, 4096)


def _dblock(w):
    return np.ascontiguousarray(w.reshape(32, 128, 128).transpose(1, 0, 2)).reshape(128, 4096)


def _swap_cols(w, nh):
    w4 = w.reshape(1024, nh, 64)
    s = np.zeros_like(w4)
    s[:, :, 0:8] = w4[:, :, 8:16]
    s[:, :, 8:16] = w4[:, :, 0:8]
    return s.reshape(1024, nh * 64)


def build_weight_blocks(attn_w_qkv, attn_w_o, hgrn_w_in, hgrn_w_o, mlp_w_up, mlp_w_down):
    blocks = []
    for i in range(4):
        j = i // 2
        if i % 2 == 0:
            w = attn_w_qkv[j]
            wq, wk, wv = w[:, :1024], w[:, 1024:1280], w[:, 1280:1536]
            wqs = _swap_cols(wq, 16)
            wks = _swap_cols(wk, 4)
            for qb in range(4):
                cols = np.concatenate([wq[:, (2 * qb) * 128:(2 * qb + 1) * 128], wqs[:, (2 * qb) * 128:(2 * qb + 1) * 128],
                                       wq[:, (2 * qb + 1) * 128:(2 * qb + 2) * 128], wqs[:, (2 * qb + 1) * 128:(2 * qb + 2) * 128]], axis=1)
                blocks.append(_fblock(cols))
            for kb in range(2):
                parts = []
                for g in (2 * kb, 2 * kb + 1):
                    kg = wk[:, g * 64:(g + 1) * 64]
                    ksg = wks[:, g * 64:(g + 1) * 64]
                    parts += [kg, kg, ksg, ksg]
                blocks.append(_fblock(np.concatenate(parts, axis=1)))
            parts = []
            for g in range(4):
                vg = wv[:, g * 64:(g + 1) * 64]
                parts += [vg, vg]
            blocks.append(_fblock(np.concatenate(parts, axis=1)))
            wo = attn_w_o[j]
        else:
            w = hgrn_w_in[j]
            zq, zf, zi, zg = w[:, :1024], w[:, 1024:2048], w[:, 2048:3072], w[:, 3072:4096]
            for b in range(2):
                blocks.append(_fblock(zi[:, b * 512:(b + 1) * 512]))
            for b in range(2):
                blocks.append(_fblock(zg[:, b * 512:(b + 1) * 512]))
            for b in range(4):
                h0, h1 = 2 * b, 2 * b + 1
                cols = np.concatenate([zf[:, h0 * 128:(h0 + 1) * 128], zq[:, h0 * 128:(h0 + 1) * 128],
                                       zf[:, h1 * 128:(h1 + 1) * 128], zq[:, h1 * 128:(h1 + 1) * 128]], axis=1)
                blocks.append(_fblock(cols))
            wo = hgrn_w_o[j]
        for b in range(2):
            blocks.append(_fblock(wo[:, b * 512:(b + 1) * 512]))
        for b in range(8):
            blocks.append(_fblock(mlp_w_up[i][:, b * 512:(b + 1) * 512]))
        for m in range(8):
            blocks.append(_dblock(mlp_w_down[i][:, m * 128:(m + 1) * 128]))
    return np.stack(blocks).astype(np.float32)


NB_ATT = 4 + 2 + 1 + 2 + 16
NB_HG = 2 + 2 + 4 + 2 + 16
NB_ALL = 2 * (NB_ATT + NB_HG)


def rope_tables(positions):
    half = 8
    inv = (np.float32(500000.0) ** (-(np.arange(half, dtype=np.float32) * np.float32(2.0)) / np.float32(16))).astype(np.float32)
    ang = (positions.astype(np.float32)[:, None] * inv[None, :]).astype(np.float32)
    cos = np.cos(ang).astype(np.float32).T
    sin = np.sin(ang).astype(np.float32).T
    n = positions.shape[0]
    C = np.ones((64, n), np.float32)
    S = np.zeros((64, n), np.float32)
    C[0:8] = cos
    C[8:16] = cos
    S[0:8] = -sin
    S[8:16] = sin
    return np.concatenate([C, C], 0), np.concatenate([S, S], 0)


def build_program(NT, NS=2, depth=4, pipe=False):
    nc = bass.Bass("TRN2", target_bir_lowering=False)
    P = Prog()
    es = ExitStack()
    NSTEP = NT + 1 if pipe else NT
    if pipe:
        NS, depth = 3, 2
    NL = 1 if pipe else 2
    SEQ = NSTEP * TT
    NBU = (NB_ATT + NB_HG) * (depth // 2) if depth % 2 == 0 else NB_ATT

    def din(name, shape, dt=F32):
        return nc.dram_tensor(name, list(shape), dt, kind="ExternalInput").ap()

    def dout(name, shape):
        return nc.dram_tensor(name, list(shape), F32, kind="ExternalOutput").ap()

    xp = din("xp", [SEQ, D])
    xs = din("xs", [NS, 64, D])
    ck = din("ck", [NL, NS, 128, 256])
    cv = din("cv", [NL, NS, 128, 256])
    st = din("st", [NL, NS, 8, 128, 128])
    wblk = din("wblk", [NBU, 128, 4096])
    small = din("small", [128, 128])
    sinkrow = din("sinkrow", [1, 2048])
    if pipe:
        ein = nc.dram_tensor("ein", [TT, D], F32)
        eout = nc.dram_tensor("eout", [2 * TT, D], F32)
        eins = nc.dram_tensor("eins", [64, D], F32)
        eouts = nc.dram_tensor("eouts", [128, D], F32)
    cident = din("cident", [128, 128])
    cmask = din("cmask", [128, 128])
    creset = din("creset", [128, TT])
    ropeC = din("ropeC", [128, SEQ + 64 * NS])
    ropeS = din("ropeS", [128, SEQ + 64 * NS])
    yp = dout("yp", [SEQ, D])
    ys = dout("ys", [NS, 64, D])
    kpo = dout("kpo", [2, 128, 256])
    vpo = dout("vpo", [2, 128, 256])
    spo = dout("spo", [2, 8, 128, 128])
    kso = dout("kso", [NL, NS, 128, 256])
    vso = dout("vso", [NL, NS, 128, 256])
    sso = dout("sso", [NL, NS, 8, 128, 128])
    wscr = nc.dram_tensor("wscr", [NBU, 128, 4096], BF16, kind="Internal").ap()

    def sb(name, shape, dt=F32):
        return es.enter_context(nc.sbuf_tensor(name, list(shape), dt))

    def ps(name, shape, dt=F32):
        return es.enter_context(nc.psum_tensor(name, list(shape), dt))

    x = sb("x", [128, 8, TT])
    hb = sb("hb", [128, 8, TT], BF16)
    NSLOT = 4
    wsl = [sb(f"ws{i}", [128, 4096], BF16) for i in range(NSLOT)]
    big = [sb(f"big{i}", [128, 8, TT], BF16) for i in range(4)]
    ktok = sb("ktok", [128, 8, 4, 128], BF16)
    KA = sb("KA", [128, 4, 128 + TT], BF16)
    KB = sb("KB", [128, 4, 128 + TT], BF16)
    Vst = sb("Vst", [64, 2 + TT // 64, 512], BF16)
    KAc = [sb(f"KAc{j}", [128, 4, 128], BF16) for j in range(NL)]
    KBc = [sb(f"KBc{j}", [128, 4, 128], BF16) for j in range(NL)]
    Vc = [sb(f"Vc{j}", [64, 2, 512], BF16) for j in range(NL)]
    kf = sb("kf", [128, 4, 128])
    vf = sb("vf", [64, 2, 256])
    kvo = sb("kvo", [128, 256])
    pT = [sb(f"pT{i}", [64, 384], BF16) for i in range(4)]
    rec = [sb(f"rec{i}", [128, 128]) for i in range(4)]
    rC = [sb(f"rC{i}", [128, TT]) for i in range(1)]
    rS = [sb(f"rS{i}", [128, TT]) for i in range(1)]
    NTMP = 12
    tmp = [sb(f"tmp{i}", [128, TT]) for i in range(NTMP)]
    oh = sb("oh", [128, TT])
    osq = sb("osq", [128, TT])
    oh2 = sb("oh2", [128, TT])
    osq2 = sb("osq2", [128, TT])
    S = [sb(f"S{j}", [128, 8, 128]) for j in range(NL)]
    S2 = [sb(f"S{j}b", [128, 8, 128]) for j in range(NL)]
    scur = [0 for j in range(NL)]
    vones = sb("vones", [64, 128], BF16)

    Sb = sb("Sb", [128, 8, 128], BF16)
    smask = [sb(f"smask{i}", [128, 128], BF16) for i in range(2)]
    Elast = sb("Elast", [128, 8, 8])
    Emid = sb("Emid", [128, 8, 8])
    Elm = sb("Elm", [128, 8, 8])
    dl = sb("dl", [128, 8, 8])
    ident_f = sb("ident_f", [128, 128])
    ident_b = sb("ident_b", [128, 128], BF16)
    ones_f = sb("ones_f", [128, 128])
    ones_b = sb("ones_b", [128, 128], BF16)
    mask_f = sb("mask_f", [128, 128])
    resetm = sb("resetm", [128, TT])
    sm = sb("sm", [128, 128])
    lb = sb("lb", [128, 2, 8])
    oml = sb("oml", [128, 2, 8])
    esink = sb("esink", [1, 2048])
    eshl = sb("eshl", [2, 2048], BF16)
    ckst = sb("ckst", [128, 4, 2, 64])
    ktz = [sb(f"ktz{i}", [128, 768], BF16) for i in range(2)]
    ktz2 = [sb(f"ktzb{i}", [128, 768], BF16) for i in range(2)]
    pm = [ps(f"pm{i}", [128, 512]) for i in range(4)]
    pS = [ps(f"pS{i}", [128, 512]) for i in range(2)]
    pO = [ps(f"pO{i}", [128, 512]) for i in range(2)]

    cnt = {"pm": 0, "tmp": 0, "w": 0, "wl": 0, "pm_n": 4}

    def next_pm():
        i = cnt["pm"] % cnt["pm_n"]
        cnt["pm"] += 1
        return pm[i], f"pm{i}"

    def next_tmp():
        i = cnt["tmp"] % NTMP
        cnt["tmp"] += 1
        return tmp[i], f"tmp{i}"

    SM_MIX, SM_MLP, SM_FIN, SM_GNO, SM_LB = 0, 32, 64, 72, 80
    SM_FLAG, SM_SELA, SM_SELB, SM_LBSEL = 96, 120, 121, 122

    def dma(eng, out, in_, r, w, stream):
        P.op(eng, lambda e, out=out, in_=in_: e.dma_start(out=out, in_=in_), r=r, w=w, stream=stream)

    dma("sync", ident_f[:], cident, [], ["ident_f"], "c0")
    dma("sync", mask_f[:], cmask, [], ["mask_f"], "c1")
    dma("sync", resetm[:], creset, [], ["resetm"], "c2")
    dma("sync", sm[:], small, [], ["sm"], "c3")
    dma("sync", esink[:], sinkrow, [], ["esink"], "c4")
    P.op("vector", lambda e: e.tensor_copy(out=ident_b[:], in_=ident_f[:]), r=["ident_f"], w=["ident_b"])
    P.op("vector", lambda e: e.memset(ones_f[:], 1.0), w=["ones_f"])
    P.op("vector", lambda e: e.memset(ones_b[:], 1.0), w=["ones_b"])
    P.op("scalar", lambda e: e.activation(out=esink[:], in_=esink[:], func=AF.Exp), r=["esink"], w=["esink"])
    P.op("vector", lambda e: e.tensor_copy(out=eshl[0:1, :], in_=esink[:]), r=["esink"], w=["eshl"])
    for q4 in range(4):
        ta, tan = tmp[2 * q4], f"tmp{2 * q4}"
        tb_, tbn = tmp[2 * q4 + 1], f"tmp{2 * q4 + 1}"
        P.op("vector", lambda e, ta=ta, q4=q4: e.tensor_copy(out=ta[0:1, :], in_=eshl[0:1, q4 * 512:(q4 + 1) * 512]), r=["eshl"], w=[tan])
        P.op("vector", lambda e, ta=ta, tb_=tb_, q4=q4: e.tensor_tensor(out=tb_[0:1, :].bitcast(BF16)[:, 0:512], in0=esink[0:1, q4 * 512:(q4 + 1) * 512], in1=ta[0:1, :], op=ALU.subtract),
             r=["esink", tan], w=[tbn])
        dma("sync", eshl[1:2, q4 * 512:(q4 + 1) * 512], tb_[0:1, :].bitcast(BF16)[:, 0:512], [tbn], ["eshl"], f"c5{q4}")
    lbr = sm[:, SM_LB:SM_LB + 16].rearrange("p (l h) -> p l h", l=2)
    P.op("vector", lambda e: e.memset(lb[:], 0.0), w=["lb"])
    P.op("vector", lambda e: e.tensor_tensor(out=dl[:, 0, :], in0=lbr[:, 1, :], in1=lbr[:, 0, :], op=ALU.subtract), r=["sm"], w=["dl"])
    P.op("scalar", lambda e: e.activation(out=lb[:, 1, :], in_=dl[:, 0, :], func=AF.Sigmoid), r=["dl", "lb"], w=["lb"])
    if pipe:
        P.op("vector", lambda e: e.tensor_scalar(out=lb[:, 0, :], in0=lb[:, 1, :], scalar1=sm[:, SM_LBSEL:SM_LBSEL + 1], scalar2=None, op0=ALU.mult), r=["lb", "sm"], w=["lb"])
    P.op("vector", lambda e: e.tensor_scalar(out=oml[:], in0=lb[:], scalar1=-1.0, scalar2=1.0, op0=ALU.mult, op1=ALU.add), r=["lb"], w=["oml"])
    for j in range(NL):
        P.op("gpsimd", lambda e, j=j: e.memset(S[j][:], 0.0), w=[f"S{j}h{h}" for h in range(8)])
        P.op("gpsimd", lambda e, j=j: e.memset(S2[j][:], 0.0), w=[f"S{j}bh{h}" for h in range(8)])
    if pipe:
        P.op("gpsimd", lambda e: e.memset(tmp[11][:], 0.0), w=["tmp11"])
        for q4 in range(4):
            for half in range(2):
                dma("sync", ein[q4 * 128:(q4 + 1) * 128, half * 512:(half + 1) * 512], tmp[11][:], ["tmp11"], ["ein"], f"ez{q4}{half}")
        for half in range(2):
            dma("sync", eins[:, half * 512:(half + 1) * 512], tmp[11][0:64, :], ["tmp11"], ["eins"], f"ez4{half}")
    P.op("gpsimd", lambda e: e.memset(KA[:], 0.0), w=["KA"])
    for i in range(2):
        P.op("gpsimd", lambda e, i=i: e.memset(ktz[i][:], 0.0), w=[f"ktz{i}"])
        P.op("gpsimd", lambda e, i=i: e.memset(ktz2[i][:], 0.0), w=[f"ktzb{i}"])
    P.op("gpsimd", lambda e: e.memset(KB[:], 0.0), w=["KB"])
    for j in range(NL):
        P.op("gpsimd", lambda e, j=j: e.memset(KAc[j][:], 0.0), w=[f"KAc{j}"])
        P.op("gpsimd", lambda e, j=j: e.memset(KBc[j][:], 0.0), w=[f"KBc{j}"])
        P.op("gpsimd", lambda e, j=j: e.memset(Vc[j][:], 0.0), w=[f"Vc{j}"])
    nb_used = NBU
    for i in range(nb_used):
        dma("gpsimd", wscr[i], wblk[i], [], [f"wscr{i}", f"wcs{i % 8}"], f"wc{i % 8}")

    wstate = {"next_load": 0, "seq": []}

    def w_issue():
        k = wstate["next_load"]
        if k >= len(wstate["seq"]):
            return
        blk = wstate["seq"][k]
        slot = k % NSLOT
        dma("sync", wsl[slot][:], wscr[blk], [f"wscr{blk}"], [f"ws{slot}"], f"wl{slot}")
        wstate["next_load"] += 1

    def wget():
        k = cnt["w"]
        cnt["w"] += 1
        while wstate["next_load"] < min(k + NSLOT, len(wstate["seq"])):
            w_issue()
        slot = k % NSLOT
        return wsl[slot], f"ws{slot}"

    def mm(out, lhsT, rhs, start, stop, r, w):
        P.op("tensor", lambda e: e.matmul(out, lhsT, rhs, start=start, stop=stop), r=r, w=w)

    def proj_f(W, wn, col0, src, srcn, T, kcs=8):
        pt, pn = next_pm()
        Wv = W[:].rearrange("p (k n) -> p k n", k=kcs)
        for kc in range(kcs):
            if callable(src):
                rhs, rn = src(kc)
            else:
                rhs, rn = src[:, kc, :T], srcn
            mm(pt[:, :T], Wv[:, kc, col0:col0 + 128], rhs, kc == 0, kc == kcs - 1, [wn, rn], [pn])
        return pt, pn

    def rmsnorm(T, gcol, final=False):
        pt, pn = next_pm()
        for c in range(8):
            t, tn = next_tmp()
            tb16 = t[:].bitcast(BF16)
            P.op("scalar", lambda e, tb16=tb16, c=c: e.activation(out=tb16[:, :T], in_=x[:, c, :T], func=AF.Square), r=["x"], w=[tn])
            mm(pt[:, :T], ones_b[:], tb16[:, :T], c == 0, c == 7, ["ones_b", tn], [pn])
        sd, sdn = next_tmp()
        P.op("scalar", lambda e: e.activation(out=sd[:, :T], in_=pt[:, :T], func=AF.Ln, scale=1.0 / D, bias=epsb[:, 0:1]), r=[pn, "epsb"], w=[sdn])
        P.op("scalar", lambda e: e.activation(out=sd[:, :T], in_=sd[:, :T], func=AF.Exp, scale=-0.5), r=[sdn], w=[sdn])
        for c in range(8):
            if final:
                P.op("vector", lambda e, c=c: e.scalar_tensor_tensor(out=x[:, c, :T], in0=x[:, c, :T], scalar=sm[:, gcol + c:gcol + c + 1],
                                                                      in1=sd[:, :T], op0=ALU.mult, op1=ALU.mult), r=["x", "sm", sdn], w=["x"])
            else:
                P.op("vector", lambda e, c=c: e.scalar_tensor_tensor(out=hb[:, c, :T], in0=x[:, c, :T], scalar=sm[:, gcol + c:gcol + c + 1],
                                                                      in1=sd[:, :T], op0=ALU.mult, op1=ALU.mult), r=["x", "sm", sdn], w=["hb"])

    epsb = sb("epsb", [128, 1])
    P.op("vector", lambda e: e.memset(epsb[:], EPS), w=["epsb"])

    def out_proj(T, src, srcn):
        for b in range(2):
            W, wn = wget()
            for m in range(4):
                cm = 4 * b + m
                pt, pn = proj_f(W, wn, m * 128, src, srcn, T)
                P.op("vector", lambda e, pt=pt, cm=cm: e.tensor_tensor(out=x[:, cm, :T], in0=pt[:, :T], in1=x[:, cm, :T], op=ALU.add), r=[pn, "x"], w=["x"])

    def mlp(T, i):
        import os
        if os.environ.get("KSTOP", "") in ("a", "b", "b1", "b2", "b3", "c", "d", "e"):
            return
        rmsnorm(T, SM_MLP + 8 * i)
        for b in range(8):
            W, wn = wget()
            for m in range(4):
                kc = 4 * b + m
                pt, pn = proj_f(W, wn, m * 128, hb, "hb", T)
                t, tn = next_tmp()
                P.op("scalar", lambda e, pt=pt, t=t: e.activation(out=t[:, :T], in_=pt[:, :T], func=AF.Relu), r=[pn], w=[tn])
                P.op("gpsimd", lambda e, t=t, kc=kc: e.tensor_tensor(out=big[kc // 8][:, kc % 8, :T], in0=t[:, :T], in1=t[:, :T], op=ALU.mult),
                     r=[tn], w=[f"big{kc // 8}"])
        for m in range(8):
            W, wn = wget()
            pt, pn = proj_f(W, wn, 0, lambda kc: (big[kc // 8][:, kc % 8, :T], f"big{kc // 8}"), None, T, kcs=32)
            P.op("vector", lambda e, pt=pt, m=m: e.tensor_tensor(out=x[:, m, :T], in0=pt[:, :T], in1=x[:, m, :T], op=ALU.add), r=[pn, "x"], w=["x"])

    def attn_layer(T, j, tile):
        i = 2 * j
        qT, oT = big[0], big[1]
        nch = T // 64
        rslot = tile["rslot"]
        rmsnorm(T, SM_MIX + 8 * i)
        if tile["kind"] == "prompt":
            P.op("gpsimd", lambda e: e.tensor_copy(out=KA[:, :, 0:128], in_=KAc[j][:]), r=[f"KAc{j}"], w=["KA"])
            P.op("gpsimd", lambda e: e.tensor_copy(out=KB[:, :, 0:128], in_=KBc[j][:]), r=[f"KBc{j}"], w=["KB"])
            P.op("gpsimd", lambda e: e.tensor_copy(out=Vst[:, 0:2, :], in_=Vc[j][:]), r=[f"Vc{j}"], w=["Vst"])
        else:
            s = tile["s"]
            for u in range(2):
                dma("sync", ckst[:, :, u, :], ck[j, s].rearrange("t (g d) -> t g d", g=4), [], ["ckst"], f"ck{u}")
            for g in range(4):
                pt, pn = next_pm()
                P.op("tensor", lambda e, pt=pt, g=g: e.transpose(out=pt[:, 0:128], in_=ckst[:, g, :, :].rearrange("t u d -> t (u d)"), identity=ident_f[:]),
                     r=["ckst", "ident_f"], w=[pn])
                P.op("scalar", lambda e, pt=pt, g=g: e.copy(out=KA[0:64, g, 0:128], in_=pt[0:64, 0:128]), r=[pn], w=["KA"])
                P.op("scalar", lambda e, pt=pt, g=g: e.copy(out=KB[64:128, g, 0:128], in_=pt[64:128, 0:128]), r=[pn], w=["KB"])
            for blk in range(2):
                for u in range(2):
                    dma("gpsimd", Vst[0:64, blk, :].rearrange("p (g u d) -> p g u d", g=4, u=2)[:, :, u, :],
                        cv[j, s, blk * 64:(blk + 1) * 64, :].rearrange("t (g d) -> t g d", g=4), [], ["Vst"], f"cv{blk}{u}")
            dma("sync", kso[j, s, 0:64, :], ck[j, s, 64:128, :], [], [], "kso_c")
            dma("sync", vso[j, s, 0:64, :], cv[j, s, 64:128, :], [], [], "vso_c")
        need_out = tile["last"]
        uni = tile.get("uniform", False)
        osel = tile.get("osel", j)
        if uni and tile["kind"] == "prompt":
            fc = SM_FLAG + tile["idx"]
            P.op("vector", lambda e: e.tensor_scalar(out=vones[:], in0=ones_b[0:64, :], scalar1=sm[0:64, fc:fc + 1], scalar2=None, op0=ALU.mult), r=["ones_b", "sm"], w=["vones"])
        import os
        STOP = os.environ.get("KSTOP", "")
        if STOP == "a":
            return
        for qb in range(4):
            W, wn = wget()
            for jj in range(2):
                cq = 2 * qb + jj
                pa, pan = proj_f(W, wn, (2 * jj) * 128, hb, "hb", T)
                pb, pbn = proj_f(W, wn, (2 * jj + 1) * 128, hb, "hb", T)
                t1, t1n = next_tmp()
                t2, t2n = next_tmp()
                P.op("vector", lambda e, pa=pa, t1=t1: e.tensor_tensor(out=t1[:, :T], in0=pa[:, :T], in1=rC[rslot][:, :T], op=ALU.mult), r=[pan, f"rC{rslot}"], w=[t1n])
                P.op("vector", lambda e, pb=pb, t2=t2: e.tensor_tensor(out=t2[:, :T], in0=pb[:, :T], in1=rS[rslot][:, :T], op=ALU.mult), r=[pbn, f"rS{rslot}"], w=[t2n])
                P.op("gpsimd", lambda e, t1=t1, t2=t2, cq=cq: e.tensor_tensor(out=qT[:, cq, :T], in0=t1[:, :T], in1=t2[:, :T], op=ALU.add), r=[t1n, t2n], w=["big0"])
        if STOP == "b1":
            return
        for kb in range(2):
            W, wn = wget()
            for jj in range(2):
                g = 2 * kb + jj
                pa, pan = proj_f(W, wn, (2 * jj) * 128, hb, "hb", T)
                pb, pbn = proj_f(W, wn, (2 * jj + 1) * 128, hb, "hb", T)
                t1, t1n = next_tmp()
                t2, t2n = next_tmp()
                P.op("vector", lambda e, pa=pa, t1=t1: e.tensor_tensor(out=t1[:, :T], in0=pa[:, :T], in1=rC[rslot][:, :T], op=ALU.mult), r=[pan, f"rC{rslot}"], w=[t1n])
                P.op("vector", lambda e, pb=pb, t2=t2: e.tensor_tensor(out=t2[:, :T], in0=pb[:, :T], in1=rS[rslot][:, :T], op=ALU.mult), r=[pbn, f"rS{rslot}"], w=[t2n])
                P.op("gpsimd", lambda e, t1=t1, t2=t2, g=g: e.tensor_tensor(out=KA[0:64, g, 128:128 + T], in0=t1[0:64, :T], in1=t2[0:64, :T], op=ALU.add), r=[t1n, t2n], w=["KA"])
                P.op("gpsimd", lambda e, t1=t1, t2=t2, g=g: e.tensor_tensor(out=KB[64:128, g, 128:128 + T], in0=t1[64:128, :T], in1=t2[64:128, :T], op=ALU.add), r=[t1n, t2n], w=["KB"])
                if need_out:
                    n0 = T - 128 if T >= 128 else 0
                    nn = T - n0
                    P.op("vector", lambda e, t1=t1, t2=t2, g=g, n0=n0, nn=nn: e.tensor_tensor(out=kf[0:64, g, 0:nn], in0=t1[0:64, n0:T], in1=t2[0:64, n0:T], op=ALU.add),
                         r=[t1n, t2n], w=["kf"])
        if STOP == "b2":
            return
        W, wn = wget()
        Wv = W[:].rearrange("p (k n) -> p k n", k=8)
        for cb in range(nch):
            pt, pn = next_pm()
            for kc in range(8):
                mm(pt[0:64, :], hb[:, kc, cb * 64:(cb + 1) * 64], Wv[:, kc, :], kc == 0, kc == 7, ["hb", wn], [pn])
            P.op("scalar", lambda e, pt=pt, cb=cb: e.copy(out=Vst[0:64, 2 + cb, :], in_=pt[0:64, :]), r=[pn], w=["Vst"])
            if need_out and cb >= nch - 2 and STOP != "b3":
                oi = cb - (nch - 2) if nch >= 2 else 0
                for g in range(4):
                    P.op("scalar", lambda e, pt=pt, oi=oi, g=g: e.copy(out=vf[0:64, oi, g * 64:(g + 1) * 64], in_=pt[0:64, g * 128:g * 128 + 64]), r=[pn], w=["vf"])
        if STOP in ("b", "b3"):
            return
        units = []
        for c in range(nch):
            gc = tile["chunk0"] + c
            if tile["kind"] == "prompt" and uni:
                slots = [c, c + 1, c + 2]
            elif tile["kind"] == "prompt":
                blocks = [b for b in (gc - 2, gc - 1, gc) if b >= 0]
                slots = [b - tile["chunk0"] + 2 for b in blocks]
            else:
                slots = [0, 1, 2]
            for cq in range(8):
                units.append((c, cq, slots))

        def bufs(k):
            ki = k % 4
            return ([(pS[0], "pS0"), (pS[1], "pS1"), (pm[0], "pm0"), (pm[1], "pm1")][ki], [(pO[0], "pO0"), (pO[1], "pO1"), (pm[2], "pm2"), (pm[3], "pm3")][ki],
                    (pT[ki], f"pT{ki}"), (rec[ki], f"rec{ki}"))

        def stage_a(k):
            c, cq, slots = units[k]
            g = cq // 2
            (pst, psn), _, (ptt, ptn), _ = bufs(k)
            nbk = len(slots)
            for bi, sl in enumerate(slots):
                for hh in range(2):
                    Kt, Kn = (KA, "KA") if hh == 0 else (KB, "KB")
                    mm(pst[0:64, bi * 128 + hh * 64: bi * 128 + hh * 64 + 64], Kt[:, g, sl * 64:(sl + 1) * 64], qT[:, cq, c * 64:(c + 1) * 64],
                       True, True, [Kn, "big0"], [psn])
            P.op("scalar", lambda e, pst=pst, ptt=ptt, nbk=nbk: e.activation(out=ptt[:, 0:nbk * 128], in_=pst[0:64, 0:nbk * 128], func=AF.Exp, scale=0.125),
                 r=[psn], w=[ptn])

        def stage_b(k):
            c, cq, slots = units[k]
            g = cq // 2
            _, (pot, pon), (ptt, ptn), (rct, rcn) = bufs(k)
            nbk = len(slots)
            for bi, sl in enumerate(slots):
                mm(pot[:, 0:128], Vst[0:64, sl, g * 128:(g + 1) * 128], ptt[:, bi * 128:(bi + 1) * 128], bi == 0, bi == nbk - 1, ["Vst", ptn], [pon])
            for bi, sl in enumerate(slots):
                if uni and tile["kind"] == "prompt" and sl < 2:
                    mm(pot[:, 128:256], vones[:], ptt[:, bi * 128:(bi + 1) * 128], bi == 0, False, ["vones", ptn], [pon])
                else:
                    mm(pot[:, 128:256], ones_b[0:64, :], ptt[:, bi * 128:(bi + 1) * 128], bi == 0, False, ["ones_b", ptn], [pon])
            mm(pot[:, 128:256], ones_b[0:2, :], eshl[0:2, j * 1024 + cq * 128: j * 1024 + (cq + 1) * 128], False, True, ["ones_b", "eshl"], [pon])
            P.op("scalar", lambda e, pot=pot, rct=rct: e.activation(out=rct[:], in_=pot[:, 128:256], func=AF.Ln), r=[pon], w=[rcn])
            P.op("scalar", lambda e, rct=rct: e.activation(out=rct[:], in_=rct[:], func=AF.Exp, scale=-1.0), r=[rcn], w=[rcn])
            for hh in range(2):
                lo = hh * 64
                P.op("vector", lambda e, pot=pot, rct=rct, lo=lo, cq=cq, c=c: e.tensor_tensor(out=oT[lo:lo + 64, cq, c * 64:(c + 1) * 64], in0=pot[lo:lo + 64, lo:lo + 64],
                                                                                         in1=rct[lo:lo + 64, lo:lo + 64], op=ALU.mult), r=[pon, rcn], w=["big1"])

        LOOK = 2
        for k in range(min(LOOK, len(units))):
            stage_a(k)
        for k in range(len(units)):
            if k + LOOK < len(units):
                stage_a(k + LOOK)
            stage_b(k)
        if STOP == "c":
            return
        if tile["kind"] == "prompt":
            P.op("gpsimd", lambda e: e.tensor_copy(out=KAc[j][:], in_=KA[:, :, T:T + 128]), r=["KA"], w=[f"KAc{j}"])
            P.op("gpsimd", lambda e: e.tensor_copy(out=KBc[j][:], in_=KB[:, :, T:T + 128]), r=["KB"], w=[f"KBc{j}"])
            P.op("gpsimd", lambda e: e.tensor_copy(out=Vc[j][:], in_=Vst[:, nch:nch + 2, :]), r=["Vst"], w=[f"Vc{j}"])
        if need_out:
            nn = min(T, 128)
            for g in range(4):
                pt, pn = next_pm()
                P.op("tensor", lambda e, pt=pt, g=g: e.transpose(out=pt[0:nn, 0:64], in_=kf[0:64, g, 0:nn], identity=ident_f[0:64, 0:64]), r=["kf", "ident_f"], w=[pn])
                P.op("vector", lambda e, pt=pt, g=g: e.tensor_copy(out=kvo[0:nn, g * 64:(g + 1) * 64], in_=pt[0:nn, 0:64]), r=[pn], w=["kvo"])
            if tile["kind"] == "prompt":
                dma("sync", kpo[osel], kvo[:], ["kvo"], [], "kvo")
                for oi in range(2):
                    dma("sync", vpo[osel, oi * 64:(oi + 1) * 64, :], vf[0:64, oi, :], ["vf"], [], f"vfo{oi}")
            else:
                s = tile["s"]
                dma("sync", kso[j, s, 64:128, :], kvo[0:64, :], ["kvo"], [], "kvo")
                dma("sync", vso[j, s, 64:128, :], vf[0:64, 0, :], ["vf"], [], "vfo0")
        if STOP == "d":
            return
        out_proj(T, oT, "big1")

    def hgrn_layer(T, j, tile):
        i = 2 * j + 1
        qt, kt, Vh, gate = big[0], big[1], big[2], big[3]
        ob = big[3]
        nch = T // 64
        npair = (T + 127) // 128
        Sbufs = [(S[j], f"S{j}"), (S2[j], f"S{j}b")]
        Sj, Sn = Sbufs[scur[j]]
        rmsnorm(T, SM_MIX + 8 * i)
        first = tile["first"]
        if tile["kind"] == "sample":
            dma("sync", Sj[:], st[j, tile["s"]].rearrange("h k v -> k h v"), [], [f"{Sn}h{h}" for h in range(8)], f"stl{j}")
        Vh2 = Vh[:].rearrange("p a b -> p (a b)").rearrange("p (t n) -> p t n", n=1024)
        for b in range(2):
            W, wn = wget()
            Wv = W[:].rearrange("p (k n) -> p k n", k=8)
            for tb in range(npair):
                np_ = min(128, T - tb * 128)
                pt, pn = next_pm()
                for kc in range(8):
                    mm(pt[0:np_, :], hb[:, kc, tb * 128: tb * 128 + np_], Wv[:, kc, :], kc == 0, kc == 7, ["hb", wn], [pn])
                P.op("scalar", lambda e, pt=pt, tb=tb, b=b, np_=np_: e.copy(out=Vh2[0:np_, tb, b * 512:(b + 1) * 512], in_=pt[0:np_, :]), r=[pn], w=["big2"])
        for b in range(2):
            W, wn = wget()
            for m in range(4):
                h = 4 * b + m
                pt, pn = proj_f(W, wn, m * 128, hb, "hb", T)
                P.op("scalar", lambda e, pt=pt, h=h: e.activation(out=gate[:, h, :T], in_=pt[:, :T], func=AF.Silu), r=[pn], w=["big3"])
        pbfs = [pm[2][:].bitcast(BF16), pm[3][:].bitcast(BF16)]
        cnt["pm_n"] = 2
        for b in range(4):
            W, wn = wget()
            hs = (2 * b, 2 * b + 1)
            ctx = {}
            for jj, h in enumerate(hs):
                pf, pfn = proj_f(W, wn, (2 * jj) * 128, hb, "hb", T)
                pq, pqn = proj_f(W, wn, (2 * jj + 1) * 128, hb, "hb", T)
                c = dict(h=h, pf=pf, pfn=pfn)
                c["zq"], c["zqn"] = next_tmp()
                P.op("scalar", lambda e, pq=pq, c=c: e.copy(out=c["zq"][:, :T], in_=pq[:, :T]), r=[pqn], w=[c["zqn"]])
                c["sig"], c["sign"] = next_tmp()
                c["omu"], c["omun"] = next_tmp()
                c["bt"], c["btn"] = next_tmp()
                c["br"], c["brn"] = next_tmp()
                c["e1"], c["e1n"] = next_tmp()
                c["omlh"] = oml[:, j, h:h + 1]
                c["lbh"] = lb[:, j, h:h + 1]
                c["bt3"] = c["bt"][:, :T].rearrange("p (c s) -> p c s", s=64)
                c["br3"] = c["br"][:, :T].rearrange("p (c s) -> p c s", s=64)
                ctx[h] = c
                P.op("scalar", lambda e, c=c: e.activation(out=c["sig"][:, :T], in_=c["pf"][:, :T], func=AF.Sigmoid), r=[c["pfn"]], w=[c["sign"]])
            for h in hs:
                c = ctx[h]
                P.op("vector", lambda e, c=c: e.tensor_scalar(out=c["sig"][:, :T], in0=c["sig"][:, :T], scalar1=c["omlh"], scalar2=c["lbh"], op0=ALU.mult, op1=ALU.add),
                     r=[c["sign"], "oml", "lb"], w=[c["sign"]])
                P.op("gpsimd", lambda e, c=c: e.tensor_scalar(out=c["omu"][:, :T], in0=c["sig"][:, :T], scalar1=-1.0, scalar2=1.0, op0=ALU.mult, op1=ALU.add),
                     r=[c["sign"]], w=[c["omun"]])
            for h in hs:
                c = ctx[h]
                P.op("scalar", lambda e, c=c: e.activation(out=c["e1"][:, :T], in_=c["sig"][:, :T], func=AF.Ln), r=[c["sign"]], w=[c["e1n"]])
            for h in hs:
                c = ctx[h]
                P.op("vector", lambda e, c=c: e.tensor_tensor_scan(out=c["bt"][:, :T], data0=resetm[:, :T], data1=c["e1"][:, :T], initial=0.0, op0=ALU.mult, op1=ALU.add),
                     r=[c["e1n"], "resetm"], w=[c["btn"]])
                P.op("vector", lambda e, c=c: e.tensor_tensor(out=c["br3"], in0=c["bt3"], in1=c["bt3"][:, :, 31:32].to_broadcast([128, nch, 64]), op=ALU.subtract),
                     r=[c["btn"]], w=[c["brn"]])
                P.op("vector", lambda e, c=c, h=h: e.tensor_tensor(out=dl[:, h, 0:nch], in0=c["bt3"][:, :, 63], in1=c["bt3"][:, :, 31], op=ALU.subtract), r=[c["btn"]], w=["dl"])
            for h in hs:
                c = ctx[h]
                P.op("scalar", lambda e, c=c: e.activation(out=c["e1"][:, :T], in_=c["br"][:, :T], func=AF.Exp), r=[c["brn"]], w=[c["e1n"]])
                P.op("scalar", lambda e, c=c: e.activation(out=c["br"][:, :T], in_=c["br"][:, :T], func=AF.Exp, scale=-1.0), r=[c["brn"]], w=[c["brn"]])
                P.op("scalar", lambda e, c=c, h=h: e.activation(out=Elast[:, h, 0:nch], in_=c["bt3"][:, :, 63], func=AF.Exp), r=[c["btn"]], w=["Elast"])
                P.op("scalar", lambda e, c=c, h=h: e.activation(out=Emid[:, h, 0:nch], in_=c["bt3"][:, :, 31], func=AF.Exp), r=[c["btn"]], w=["Emid"])
                P.op("scalar", lambda e, h=h: e.activation(out=Elm[:, h, 0:nch], in_=dl[:, h, 0:nch], func=AF.Exp), r=["dl"], w=["Elm"])
            for h in hs:
                c = ctx[h]
                P.op("vector", lambda e, c=c, h=h: e.tensor_tensor(out=kt[:, h, :T], in0=c["omu"][:, :T], in1=c["br"][:, :T], op=ALU.mult),
                     r=[c["omun"], c["brn"]], w=["big1"])
                kz, kzn = ktz[h % 2], f"ktz{h % 2}"
                kz2, kz2n = ktz2[h % 2], f"ktzb{h % 2}"
                c.update(kz=kz, kzn=kzn, kz2=kz2, kz2n=kz2n)
                if T >= 128:
                    om4 = c["omu"][:, :T].rearrange("p (a b c) -> p a b c", b=2, c=64)
                    br4 = c["br"][:, :T].rearrange("p (a b c) -> p a b c", b=2, c=64)
                    P.op("vector", lambda e, kz=kz, om4=om4, br4=br4: e.tensor_tensor(out=kz[:, 0:npair * 192].rearrange("p (a b c) -> p a b c", b=3, c=64)[:, :, 0::2, :],
                                                                                   in0=om4, in1=br4, op=ALU.mult), r=[c["omun"], c["brn"]], w=[kzn])
                    P.op("gpsimd", lambda e, kz2=kz2, om4=om4, br4=br4: e.tensor_tensor(out=kz2[:, 0:npair * 192].rearrange("p (a b c) -> p a b c", b=3, c=64)[:, :, 0::2, 0:32],
                                                                                     in0=om4[:, :, :, 0:32], in1=br4[:, :, :, 0:32], op=ALU.mult), r=[c["omun"], c["brn"]], w=[kz2n])
                else:
                    P.op("gpsimd", lambda e, kz=kz, h=h: e.tensor_copy(out=kz[:, 0:64], in_=kt[:, h, 0:64]), r=["big1"], w=[kzn])
                    P.op("gpsimd", lambda e, kz2=kz2, h=h: e.tensor_copy(out=kz2[:, 0:32], in_=kt[:, h, 0:32]), r=["big1"], w=[kz2n])
            for h in hs:
                c = ctx[h]
                P.op("scalar", lambda e, c=c: e.activation(out=c["sig"][:, :T], in_=c["zq"][:, :T], func=AF.Silu), r=[c["zqn"], c["sign"]], w=[c["sign"]])
                P.op("vector", lambda e, c=c, h=h: e.tensor_tensor(out=qt[:, h, :T], in0=c["sig"][:, :T], in1=c["e1"][:, :T], op=ALU.mult), r=[c["sign"], c["e1n"]], w=["big0"])
            ohs = {hs[0]: (oh, "oh", osq, "osq"), hs[1]: (oh2, "oh2", osq2, "osq2")}
            import os
            for hs_run in ([hs] if not os.environ.get('KSEQ') else [(hs[0],), (hs[1],)]):
                par = scur[j]
                for p in range(npair):
                    np_ = min(128, T - p * 128)
                    t0 = p * 128
                    ncc = np_ // 64
                    stb = [next_pm() for _ in range(ncc)]
                    for hi, h in [(hh % 2, hh) for hh in hs_run]:
                        c = ctx[h]
                        kz, kzn, kz2, kz2n = c["kz"], c["kzn"], c["kz2"], c["kz2n"]
                        pbv, pbn = pbfs[hi][:, 0:128], f"pm{2 + hi}"
                        psv, psn_ = pS[hi][:, 0:128], f"pS{hi}"
                        P.op("tensor", lambda e, h=h, t0=t0, np_=np_, pbv=pbv: e.transpose(out=pbv[0:np_, :], in_=kt[:, h, t0:t0 + np_], identity=ident_b[:]),
                             r=["big1", "ident_b"], w=[pbn])
                        P.op("scalar", lambda e, h=h, p=p, np_=np_, pbv=pbv: e.copy(out=ktok[0:np_, h, p, :], in_=pbv[0:np_, :]), r=[pbn], w=[f"ktok{hi}"])
                        if np_ == 128:
                            mm(psv[:, 0:32], kz2[:, p * 192: p * 192 + 128], qt[:, h, t0:t0 + 32], True, True, [kz2n, "big0"], [psn_])
                            mm(psv[:, 32:64], kz[:, p * 192: p * 192 + 128], qt[:, h, t0 + 32:t0 + 64], True, True, [kzn, "big0"], [psn_])
                            mm(psv[:, 64:96], kz2[:, p * 192 + 64: p * 192 + 192], qt[:, h, t0 + 64:t0 + 96], True, True, [kz2n, "big0"], [psn_])
                            mm(psv[:, 96:128], kz[:, p * 192 + 64: p * 192 + 192], qt[:, h, t0 + 96:t0 + 128], True, True, [kzn, "big0"], [psn_])
                        else:
                            mm(psv[0:64, 0:32], kz2[:, 0:64], qt[:, h, 0:32], True, True, [kz2n, "big0"], [psn_])
                            mm(psv[0:64, 32:64], kz[:, 0:64], qt[:, h, 32:64], True, True, [kzn, "big0"], [psn_])
                        smk, smn = smask[hi], f"smask{hi}"
                        P.op("vector", lambda e, smk=smk, np_=np_, psv=psv: e.tensor_tensor(out=smk[0:np_, 0:np_], in0=psv[0:np_, 0:np_], in1=mask_f[0:np_, 0:np_], op=ALU.mult),
                             r=[psn_, "mask_f"], w=[smn])
                        pot, pon = pO[hi], f"pO{hi}"
                        mm(pot[:, 0:np_], Vh2[0:np_, p, h * 128:(h + 1) * 128], smk[0:np_, 0:np_], True, False, ["big2", smn], [pon])
                        for cc in range(ncc):
                            lo = cc * 64
                            pst, psn = stb[cc]
                            mm(pst[:, hi * 128:(hi + 1) * 128], ktok[lo:lo + 64, h, p, :], Vh2[lo:lo + 64, p, h * 128:(h + 1) * 128], True, True, [f"ktok{hi}", "big2"], [psn])
                    for cc in range(ncc):
                        ci = p * 2 + cc
                        lastmm = cc == ncc - 1
                        import os
                        PP = not os.environ.get("KNOPP")
                        (Sc, Scn), (Sx, Sxn) = Sbufs[par], Sbufs[(1 - par) if PP else par]
                        for hi, h in [(hh % 2, hh) for hh in hs_run]:
                            c = ctx[h]
                            pot, pon = pO[hi], f"pO{hi}"
                            pst, psn = stb[cc]
                            P.op("scalar", lambda e, h=h, ci=ci, Sc=Sc: e.activation(out=Sb[:, h, :], in_=Sc[:, h, :], func=AF.Copy, scale=Emid[:, h, ci:ci + 1]),
                                 r=[f"{Scn}h{h}", "Emid"], w=[f"Sb{h}"])
                            mm(pot[:, cc * 64:(cc + 1) * 64], Sb[:, h, :], qt[:, h, t0 + cc * 64: t0 + (cc + 1) * 64], False, lastmm, [f"Sb{h}", "big0"], [pon])
                            P.op("vector", lambda e, h=h, ci=ci, Sc=Sc, Sx=Sx: e.tensor_scalar(out=Sx[:, h, :], in0=Sc[:, h, :], scalar1=Elast[:, h, ci:ci + 1], scalar2=None, op0=ALU.mult),
                                 r=[f"{Scn}h{h}", "Elast"], w=[f"{Sxn}h{h}"])
                            P.op("vector", lambda e, pst=pst, h=h, ci=ci, hi=hi, Sx=Sx: e.scalar_tensor_tensor(out=Sx[:, h, :], in0=pst[:, hi * 128:(hi + 1) * 128], scalar=Elm[:, h, ci:ci + 1], in1=Sx[:, h, :],
                                                                                               op0=ALU.mult, op1=ALU.add), r=[psn, "Elm", f"{Sxn}h{h}"], w=[f"{Sxn}h{h}"])
                        par = (1 - par) if PP else par
                    for hi, h in [(hh % 2, hh) for hh in hs_run]:
                        pot, pon = pO[hi], f"pO{hi}"
                        oht, ohn, oqt, oqn = ohs[h]
                        P.op("scalar", lambda e, pot=pot, t0=t0, np_=np_, oht=oht: e.copy(out=oht[:, t0:t0 + np_], in_=pot[:, 0:np_]), r=[pon], w=[ohn])
                        P.op("scalar", lambda e, pot=pot, t0=t0, np_=np_, oqt=oqt: e.activation(out=oqt[:].bitcast(BF16)[:, t0:t0 + np_], in_=pot[:, 0:np_], func=AF.Square), r=[pon], w=[oqn])
            sds = {}
            for h in hs:
                oht, ohn, oqt, oqn = ohs[h]
                pt, pn = next_pm()
                mm(pt[:, :T], ones_b[:], oqt[:].bitcast(BF16)[:, :T], True, True, ["ones_b", oqn], [pn])
                sds[h] = (pt, pn) + next_tmp()
            for h in hs:
                pt, pn, sd, sdn = sds[h]
                P.op("scalar", lambda e, pt=pt, sd=sd: e.activation(out=sd[:, :T], in_=pt[:, :T], func=AF.Ln, scale=1.0 / 128, bias=epsb[:, 0:1]), r=[pn, "epsb"], w=[sdn])
                P.op("scalar", lambda e, sd=sd: e.activation(out=sd[:, :T], in_=sd[:, :T], func=AF.Exp, scale=-0.5), r=[sdn], w=[sdn])
            for h in hs:
                pt, pn, sd, sdn = sds[h]
                oht, ohn, oqt, oqn = ohs[h]
                P.op("vector", lambda e, sd=sd, oht=oht: e.scalar_tensor_tensor(out=sd[:, :T], in0=oht[:, :T], scalar=sm[:, SM_GNO + j:SM_GNO + j + 1], in1=sd[:, :T], op0=ALU.mult, op1=ALU.mult),
                     r=[ohn, "sm", sdn], w=[sdn])
                P.op("gpsimd", lambda e, sd=sd, h=h: e.tensor_tensor(out=ob[:, h, :T], in0=sd[:, :T], in1=gate[:, h, :T], op=ALU.mult), r=[sdn, "big3"], w=["big3"])
        cnt["pm_n"] = 4
        import os
        if not os.environ.get("KNOPP"):
            scur[j] ^= (nch % 2)
        Sj, Sn = Sbufs[scur[j]]
        Sres = [f"{Sn}h{h}" for h in range(8)]
        if tile["last"]:
            if tile["kind"] == "prompt":
                dma("sync", spo[tile.get("osel", j)].rearrange("h k v -> k h v"), Sj[:], Sres, [], f"so{j}")
            else:
                dma("sync", sso[j, tile["s"]].rearrange("h k v -> k h v"), Sj[:], Sres, [], f"so{j}")
        out_proj(T, ob, "big3")

    tiles = []
    if pipe:
        for t in range(NSTEP):
            tiles.append(dict(kind="prompt", T=TT, chunk0=t * 8, first=False, last=(t >= NT - 1), osel=(0 if t == NT - 1 else 1), pos0=t * TT, idx=t, uniform=True))
        for s in range(NS):
            tiles.append(dict(kind="sample", T=64, chunk0=0, first=False, last=True, pos0=SEQ + 64 * s, s=s, idx=NSTEP + s, uniform=True))
    else:
        for t in range(NT):
            tiles.append(dict(kind="prompt", T=TT, chunk0=t * 8, first=(t == 0), last=(t == NT - 1), pos0=t * TT, idx=t))
        for s in range(NS):
            tiles.append(dict(kind="sample", T=64, chunk0=0, first=False, last=True, pos0=SEQ, s=s, idx=NT + s))
    RG = [[0, 1], [2, 3], [4, 5], [6, 7]]
    per_pass = []
    off = 0
    for i in range(depth):
        n = NB_ATT if i % 2 == 0 else NB_HG
        per_pass += list(range(off, off + n))
        off += n
    for _ in tiles:
        wstate["seq"] += per_pass

    for ti, tile in enumerate(tiles):
        T = tile["T"]
        rslot = 0
        tile["rslot"] = rslot
        dma("sync", rC[rslot][:, :T], ropeC[:, tile["pos0"]:tile["pos0"] + T], [], [f"rC{rslot}"], f"rC{rslot}")
        dma("sync", rS[rslot][:, :T], ropeS[:, tile["pos0"]:tile["pos0"] + T], [], [f"rS{rslot}"], f"rS{rslot}")
        if pipe:
            if tile["kind"] == "prompt":
                P.op("gpsimd", lambda e: e.collective_compute("AllGather", ALU.bypass, replica_groups=RG, ins=[ein[:]], outs=[eout[:]]), r=["ein"], w=["eout"], stream="cc", inc=1)
            else:
                P.op("gpsimd", lambda e: e.collective_compute("AllGather", ALU.bypass, replica_groups=RG, ins=[eins[:]], outs=[eouts[:]]), r=["eins"], w=["eouts"], stream="cc", inc=1)
        ntb = (T + 127) // 128
        for tb in range(ntb):
            np_ = min(128, T - tb * 128)
            for half in range(2):
                cs = slice(half * 512, (half + 1) * 512)
                xi, xn = next_tmp()
                if tile["kind"] == "prompt":
                    src = xp[tile["pos0"] + tb * 128: tile["pos0"] + tb * 128 + np_, cs]
                else:
                    src = xs[tile["s"], :, cs]
                dma("sync", xi[0:np_, :], src, [], [xn], xn + "i")
                if pipe:
                    x2, x2n = next_tmp()
                    if tile["kind"] == "prompt":
                        src2, s2n = eout[tb * 128: tb * 128 + np_, cs], "eout"
                    else:
                        src2, s2n = eouts[0:64, cs], "eouts"
                    dma("sync", x2[0:np_, :], src2, [s2n], [x2n], x2n + "i")
                    P.op("vector", lambda e, xi=xi, np_=np_: e.tensor_scalar(out=xi[0:np_, :], in0=xi[0:np_, :], scalar1=sm[0:np_, SM_SELA:SM_SELA + 1], scalar2=None, op0=ALU.mult),
                         r=[xn, "sm"], w=[xn])
                    P.op("vector", lambda e, xi=xi, x2=x2, np_=np_: e.scalar_tensor_tensor(out=xi[0:np_, :], in0=x2[0:np_, :], scalar=sm[0:np_, SM_SELB:SM_SELB + 1], in1=xi[0:np_, :],
                                                                                       op0=ALU.mult, op1=ALU.add), r=[x2n, xn, "sm"], w=[xn])
                pt, pn = next_pm()
                for cc in range(4):
                    P.op("tensor", lambda e, pt=pt, xi=xi, cc=cc, np_=np_: e.transpose(out=pt[:, cc * 128: cc * 128 + np_], in_=xi[0:np_, cc * 128:(cc + 1) * 128], identity=ident_f[0:np_, 0:np_]),
                         r=[xn, "ident_f"], w=[pn])
                P.op("scalar", lambda e, pt=pt, half=half, tb=tb, np_=np_: e.copy(out=x[:, half * 4:half * 4 + 4, tb * 128: tb * 128 + np_],
                                                                                 in_=pt[:].rearrange("p (c t) -> p c t", c=4)[:, :, 0:np_]), r=[pn], w=["x"])
        for i in range(depth):
            if i % 2 == 0:
                attn_layer(T, i // 2, tile)
            else:
                hgrn_layer(T, i // 2, tile)
            mlp(T, i)
        for phase in (("ex", "y") if pipe else ("y",)):
            if phase == "y":
                rmsnorm(T, SM_FIN, final=True)
            for tb in range(ntb):
                np_ = min(128, T - tb * 128)
                for half in range(2):
                    cs = slice(half * 512, (half + 1) * 512)
                    pt, pn = next_pm()
                    for cc in range(4):
                        c = half * 4 + cc
                        P.op("tensor", lambda e, pt=pt, c=c, cc=cc, tb=tb, np_=np_: e.transpose(out=pt[0:np_, cc * 128:(cc + 1) * 128], in_=x[:, c, tb * 128: tb * 128 + np_], identity=ident_f[:]),
                             r=["x", "ident_f"], w=[pn])
                    yo, yn = next_tmp()
                    P.op("vector", lambda e, pt=pt, yo=yo, np_=np_: e.tensor_copy(out=yo[0:np_, :], in_=pt[0:np_, :]), r=[pn], w=[yn])
                    if phase == "ex":
                        if tile["kind"] == "prompt":
                            dma("sync", ein[tb * 128: tb * 128 + np_, cs], yo[0:np_, :], [yn], ["ein"], yn + "o")
                        else:
                            dma("sync", eins[0:64, cs], yo[0:np_, :], [yn], ["eins"], yn + "o")
                    else:
                        if tile["kind"] == "prompt":
                            dst = yp[tile["pos0"] + tb * 128: tile["pos0"] + tb * 128 + np_, cs]
                        else:
                            dst = ys[tile["s"], :, cs]
                        dma("sync", dst, yo[0:np_, :], [yn], [], yn + "o")

    P.emit(nc, es)
    es.close()
    return nc


def host_consts(SEQ):
    ident = np.eye(128, dtype=np.float32)
    m = np.zeros((128, 128), np.float32)
    for s in range(128):
        for t in range(128):
            if s // 64 == t // 64 and s <= t:
                m[s, t] = 1.0
    reset = np.ones((128, TT), np.float32)
    reset[:, 0::64] = 0.0
    pos = np.concatenate([np.arange(SEQ, dtype=np.float32), 2048.0 + np.arange(64, dtype=np.float32)])
    C, S = rope_tables(pos)
    return dict(cident=ident, cmask=m, creset=reset, ropeC=np.ascontiguousarray(C), ropeS=np.ascontiguousarray(S))


def host_small(mixer_norm, mlp_norm, final_norm, hgrn_out_norm, hgrn_lb):
    sm = np.zeros((128, 128), np.float32)
    for i in range(4):
        sm[:, 0 + 8 * i: 8 + 8 * i] = mixer_norm[i].reshape(8, 128).T
        sm[:, 32 + 8 * i: 40 + 8 * i] = mlp_norm[i].reshape(8, 128).T
    sm[:, 64:72] = final_norm.reshape(8, 128).T
    for j in range(2):
        sm[:, 72 + j] = hgrn_out_norm[j]
        sm[:, 80 + 8 * j: 88 + 8 * j] = hgrn_lb[j].reshape(8, 128).T
    return sm


_CACHE = {}


def run_pipe(inputs, NT=16):
    f = lambda k: np.asarray(inputs[k], dtype=np.float32)
    NSTEP = NT + 1
    SEQ = NSTEP * TT
    key = ("pipe", NT)
    if key not in _CACHE:
        _CACHE[key] = build_program(NT, 3, 2, pipe=True)
    nc = _CACHE[key]
    wb = build_weight_blocks(f("attn_w_qkv"), f("attn_w_o"), f("hgrn_w_in"), f("hgrn_w_o"), f("mlp_w_up"), f("mlp_w_down"))
    nbh = NB_ATT + NB_HG
    ident = np.eye(128, dtype=np.float32)
    base = host_consts(TT)
    mixer_norm, mlp_norm, final_norm = f("mixer_norm"), f("mlp_norm"), f("final_norm")
    gno, lbr, sinks = f("hgrn_out_norm"), f("hgrn_lb"), f("attn_sinks")
    xp_, xs_, ck_, cv_, st_ = f("x_prompt"), f("x_sample"), f("cache_k"), f("cache_v"), f("state_s")
    ck_ = ck_.reshape(2, 8, 128, 256)
    cv_ = cv_.reshape(2, 8, 128, 256)
    spos = 2048.0 + np.arange(64, dtype=np.float32)
    in_maps = []
    for c in range(8):
        b, role = c // 2, c % 2
        sm = np.zeros((128, 128), np.float32)
        for li in range(2):
            L = 2 * role + li
            sm[:, 8 * li: 8 * li + 8] = mixer_norm[L].reshape(8, 128).T
            sm[:, 32 + 8 * li: 40 + 8 * li] = mlp_norm[L].reshape(8, 128).T
        sm[:, 64:72] = final_norm.reshape(8, 128).T
        sm[:, 72] = gno[role]
        for j in range(2):
            sm[:, 80 + 8 * j: 88 + 8 * j] = lbr[j].reshape(8, 128).T
        for t in range(NSTEP):
            sm[:, 96 + t] = 1.0 if (t - role) >= 1 else 0.0
        sm[:, 120] = 1.0 if role == 0 else 0.0
        sm[:, 121] = 1.0 if role == 1 else 0.0
        sm[:, 122] = 1.0 if role == 1 else 0.0
        sinkrow = np.zeros((1, 2048), np.float32)
        sinkrow[0, :1024] = np.repeat(sinks[role], 64)
        xp = np.zeros((SEQ, D), np.float32)
        xs = np.zeros((3, 64, D), np.float32)
        ck = np.zeros((1, 3, 128, 256), np.float32)
        cv = np.zeros((1, 3, 128, 256), np.float32)
        st = np.zeros((1, 3, 8, 128, 128), np.float32)
        pos = np.zeros((SEQ + 192,), np.float32)
        if role == 0:
            xp[:NT * TT] = xp_[b, :NT * TT]
            xp[NT * TT:] = xp_[b, (NT - 1) * TT:NT * TT]
            xs[0:2] = xs_[2 * b:2 * b + 2]
            pos[:NT * TT] = np.arange(NT * TT, dtype=np.float32)
            so = 0
        else:
            pos[TT:TT + NT * TT] = np.arange(NT * TT, dtype=np.float32)
            so = 1
        ck[0, so:so + 2] = ck_[role, 2 * b:2 * b + 2]
        cv[0, so:so + 2] = cv_[role, 2 * b:2 * b + 2]
        st[0, so:so + 2] = st_[role, 2 * b:2 * b + 2]
        for k in range(3):
            pos[SEQ + 64 * k: SEQ + 64 * (k + 1)] = spos
        C, S_ = rope_tables(pos)
        m = dict(xp=xp, xs=xs, ck=ck, cv=cv, st=st, wblk=np.ascontiguousarray(wb[role * nbh:(role + 1) * nbh]), small=sm, sinkrow=sinkrow,
                 cident=ident, cmask=base["cmask"], creset=base["creset"], ropeC=np.ascontiguousarray(C), ropeS=np.ascontiguousarray(S_))
        in_maps.append(m)
    res = run_bass_kernel_spmd(nc, in_maps, core_ids=list(range(8)))
    R = list(res.results)
    B = 4
    yp = np.stack([R[2 * b + 1]["yp"][TT:TT + NT * TT] for b in range(B)])
    ys = np.concatenate([R[2 * b + 1]["ys"][1:3] for b in range(B)], 0)
    kp = np.stack([np.stack([R[2 * b + r]["kpo"][r] for b in range(B)]) for r in range(2)]).reshape(2, B, 128, 4, 64)
    vp = np.stack([np.stack([R[2 * b + r]["vpo"][r] for b in range(B)]) for r in range(2)]).reshape(2, B, 128, 4, 64)
    sp = np.stack([np.stack([R[2 * b + r]["spo"][r] for b in range(B)]) for r in range(2)])
    ks = np.stack([np.concatenate([R[2 * b + r]["kso"][0, r:r + 2] for b in range(B)], 0) for r in range(2)]).reshape(2, 8, 128, 4, 64)
    vs = np.stack([np.concatenate([R[2 * b + r]["vso"][0, r:r + 2] for b in range(B)], 0) for r in range(2)]).reshape(2, 8, 128, 4, 64)
    ss = np.stack([np.concatenate([R[2 * b + r]["sso"][0, r:r + 2] for b in range(B)], 0) for r in range(2)])
    return tuple(np.ascontiguousarray(a, dtype=np.float32) for a in (yp, ys, kp, vp, sp, ks, vs, ss))


def run(inputs, NT=16, n_cores=4, depth=4):
    f = lambda k: np.asarray(inputs[k], dtype=np.float32)
    SEQ = NT * TT
    key = (NT, depth)
    if key not in _CACHE:
        _CACHE[key] = build_program(NT, 2, depth)
    nc = _CACHE[key]
    wb = build_weight_blocks(f("attn_w_qkv"), f("attn_w_o"), f("hgrn_w_in"), f("hgrn_w_o"), f("mlp_w_up"), f("mlp_w_down"))
    consts = host_consts(SEQ)
    sm = host_small(f("mixer_norm"), f("mlp_norm"), f("final_norm"), f("hgrn_out_norm"), f("hgrn_lb"))
    sinkrow = np.repeat(f("attn_sinks").reshape(-1), 64).reshape(1, 2048).astype(np.float32)
    xp_, xs_, ck_, cv_, st_ = f("x_prompt"), f("x_sample"), f("cache_k"), f("cache_v"), f("state_s")
    in_maps = []
    for c in range(n_cores):
        b = c % 4
        m = dict(xp=np.ascontiguousarray(xp_[b, :SEQ]), xs=np.ascontiguousarray(xs_[2 * b:2 * b + 2]),
                 ck=np.ascontiguousarray(ck_[:, 2 * b:2 * b + 2].reshape(2, 2, 128, 256)),
                 cv=np.ascontiguousarray(cv_[:, 2 * b:2 * b + 2].reshape(2, 2, 128, 256)),
                 st=np.ascontiguousarray(st_[:, 2 * b:2 * b + 2]), wblk=wb, small=sm, sinkrow=sinkrow)
        m.update(consts)
        in_maps.append(m)
    res = run_bass_kernel_spmd(nc, in_maps, core_ids=list(range(n_cores)))
    R = list(res.results)
    R = [R[b % len(R)] for b in range(4)]
    B = 4
    yp = np.stack([R[b]["yp"] for b in range(B)]).reshape(B, SEQ, D)
    ys = np.concatenate([R[b]["ys"] for b in range(B)], 0).reshape(8, 64, D)
    kp = np.stack([R[b]["kpo"] for b in range(B)], 1).reshape(2, B, 128, 4, 64)
    vp = np.stack([R[b]["vpo"] for b in range(B)], 1).reshape(2, B, 128, 4, 64)
    sp = np.stack([R[b]["spo"] for b in range(B)], 1).reshape(2, B, 8, 128, 128)
    ks = np.concatenate([R[b]["kso"] for b in range(B)], 1).reshape(2, 8, 128, 4, 64)
    vs = np.concatenate([R[b]["vso"] for b in range(B)], 1).reshape(2, 8, 128, 4, 64)
    ss = np.concatenate([R[b]["sso"] for b in range(B)], 1).reshape(2, 8, 8, 128, 128)
    return tuple(np.ascontiguousarray(a, dtype=np.float32) for a in (yp, ys, kp, vp, sp, ks, vs, ss))


def kernel(**inputs):
    return run_pipe(inputs, NT=16)
```

```python
import numpy as np
from contextlib import ExitStack
import concourse.bass as bass
import concourse.mybir as mybir
from concourse.bass_utils import run_bass_kernel_spmd

F32 = mybir.dt.float32
BF16 = mybir.dt.bfloat16
AF = mybir.ActivationFunctionType
ALU = mybir.AluOpType

D = 1024
TT = 512
EPS = 1e-5
ENG = ["tensor", "vector", "scalar", "gpsimd", "sync"]
SAME_ENG_SYNC = True
SEM_ROT = 20000


class Op:
    __slots__ = ("eng", "fn", "deps", "sig", "idx", "stream", "val", "cnt", "inc")


class Prog:
    def __init__(self):
        self.ops = {e: [] for e in ENG}
        self.lastw = {}
        self.readers = {}
        self.stream_cnt = {}

    def op(self, eng, fn, r=(), w=(), stream=None, inc=16):
        o = Op()
        o.eng, o.fn, o.deps, o.sig, o.stream, o.val, o.cnt = eng, fn, set(), False, stream, 0, 0
        for x in r:
            lw = self.lastw.get(x)
            if lw is not None:
                o.deps.add(lw)
        for x in w:
            lw = self.lastw.get(x)
            if lw is not None:
                o.deps.add(lw)
            for rd in self.readers.get(x, ()):
                o.deps.add(rd)
        for x in r:
            self.readers.setdefault(x, []).append(o)
        for x in w:
            self.lastw[x] = o
            self.readers[x] = []
        o.deps.discard(o)
        if stream is not None:
            self.stream_cnt[stream] = self.stream_cnt.get(stream, 0) + inc
            o.val = self.stream_cnt[stream]
            o.inc = inc
        o.idx = len(self.ops[eng])
        self.ops[eng].append(o)
        return o

    def emit(self, nc, es):
        for e in ENG:
            for o in self.ops[e]:
                best = {}
                keep = []
                for d in o.deps:
                    if d.stream is not None:
                        keep.append(d)
                        continue
                    if d.eng == e and (e == "tensor" or not SAME_ENG_SYNC):
                        continue
                    if d.eng not in best or best[d.eng].idx < d.idx:
                        best[d.eng] = d
                for d in best.values():
                    d.sig = True
                    keep.append(d)
                o.deps = keep
        for e in ENG:
            c = 0
            for o in self.ops[e]:
                if o.sig and o.stream is None:
                    c += 1
                    o.cnt = c
        sems = {}

        def getsem(key):
            if key not in sems:
                sems[key] = es.enter_context(nc.semaphore("s_" + str(key).replace(" ", "")))
            return sems[key]

        def semval(d):
            if d.stream is not None:
                return getsem(("st", d.stream)), d.val
            k = (d.cnt - 1) // SEM_ROT
            return getsem((d.eng, k)), d.cnt - k * SEM_ROT

        block = es.enter_context(nc.Block())

        def run(e):
            def body(engobj):
                known = {}
                for o in self.ops[e]:
                    need = {}
                    for d in o.deps:
                        s, v = semval(d)
                        if need.get(s, (None, 0))[1] < v:
                            need[s] = (s, v)
                    for s, v in need.values():
                        if known.get(s, 0) < v:
                            engobj.wait_ge(s, v)
                            known[s] = v
                    ins = o.fn(engobj)
                    if o.stream is not None:
                        ins.then_inc(getsem(("st", o.stream)), o.inc)
                    elif o.sig:
                        k = (o.cnt - 1) // SEM_ROT
                        ins.then_inc(getsem((e, k)), 1)
                for st, tot in self.stream_cnt.items():
                    if self.stream_eng.get(st) == e:
                        engobj.wait_ge(getsem(("st", st)), tot)
            getattr(block, e)(body)

        self.stream_eng = {}
        for e in ENG:
            for o in self.ops[e]:
                if o.stream is not None:
                    self.stream_eng[o.stream] = e
        for e in ENG:
            run(e)


def _fblock(cols):
    return np.ascontiguousarray(cols.reshape(8, 128, 512).transpose(1, 0, 2)).reshape(128, 4096)


def _dblock(w):
    return np.ascontiguousarray(w.reshape(32, 128, 128).transpose(1, 0, 2)).reshape(128, 4096)


def _swap_cols(w, nh):
    w4 = w.reshape(1024, nh, 64)
    s = np.zeros_like(w4)
    s[:, :, 0:8] = w4[:, :, 8:16]
    s[:, :, 8:16] = w4[:, :, 0:8]
    return s.reshape(1024, nh * 64)


def build_weight_blocks(attn_w_qkv, attn_w_o, hgrn_w_in, hgrn_w_o, mlp_w_up, mlp_w_down):
    blocks = []
    for i in range(4):
        j = i // 2
        if i % 2 == 0:
            w = attn_w_qkv[j]
            wq, wk, wv = w[:, :1024], w[:, 1024:1280], w[:, 1280:1536]
            wqs = _swap_cols(wq, 16)
            wks = _swap_cols(wk, 4)
            for qb in range(4):
                cols = np.concatenate([wq[:, (2 * qb) * 128:(2 * qb + 1) * 128], wqs[:, (2 * qb) * 128:(2 * qb + 1) * 128],
                                       wq[:, (2 * qb + 1) * 128:(2 * qb + 2) * 128], wqs[:, (2 * qb + 1) * 128:(2 * qb + 2) * 128]], axis=1)
                blocks.append(_fblock(cols))
            for kb in range(2):
                parts = []
                for g in (2 * kb, 2 * kb + 1):
                    kg = wk[:, g * 64:(g + 1) * 64]
                    ksg = wks[:, g * 64:(g + 1) * 64]
                    parts += [kg, kg, ksg, ksg]
                blocks.append(_fblock(np.concatenate(parts, axis=1)))
            parts = []
            for g in range(4):
                vg = wv[:, g * 64:(g + 1) * 64]
                parts += [vg, vg]
            blocks.append(_fblock(np.concatenate(parts, axis=1)))
            wo = attn_w_o[j]
        else:
            w = hgrn_w_in[j]
            zq, zf, zi, zg = w[:, :1024], w[:, 1024:2048], w[:, 2048:3072], w[:, 3072:4096]
            for b in range(2):
                blocks.append(_fblock(zi[:, b * 512:(b + 1) * 512]))
            for b in range(2):
                blocks.append(_fblock(zg[:, b * 512:(b + 1) * 512]))
            for b in range(4):
                h0, h1 = 2 * b, 2 * b + 1
                cols = np.concatenate([zf[:, h0 * 128:(h0 + 1) * 128], zq[:, h0 * 128:(h0 + 1) * 128],
                                       zf[:, h1 * 128:(h1 + 1) * 128], zq[:, h1 * 128:(h1 + 1) * 128]], axis=1)
                blocks.append(_fblock(cols))
            wo = hgrn_w_o[j]
        for b in range(2):
            blocks.append(_fblock(wo[:, b * 512:(b + 1) * 512]))
        for b in range(8):
            blocks.append(_fblock(mlp_w_up[i][:, b * 512:(b + 1) * 512]))
        for m in range(8):
            blocks.append(_dblock(mlp_w_down[i][:, m * 128:(m + 1) * 128]))
    return np.stack(blocks).astype(np.float32)


NB_ATT = 4 + 2 + 1 + 2 + 16
NB_HG = 2 + 2 + 4 + 2 + 16
NB_ALL = 2 * (NB_ATT + NB_HG)


def rope_tables(positions):
    half = 8
    inv = (np.float32(500000.0) ** (-(np.arange(half, dtype=np.float32) * np.float32(2.0)) / np.float32(16))).astype(np.float32)
    ang = (positions.astype(np.float32)[:, None] * inv[None, :]).astype(np.float32)
    cos = np.cos(ang).astype(np.float32).T
    sin = np.sin(ang).astype(np.float32).T
    n = positions.shape[0]
    C = np.ones((64, n), np.float32)
    S = np.zeros((64, n), np.float32)
    C[0:8] = cos
    C[8:16] = cos
    S[0:8] = -sin
    S[8:16] = sin
    return np.concatenate([C, C], 0), np.concatenate([S, S], 0)


def build_program(NT, NS=2, depth=4, pipe=False):
    nc = bass.Bass("TRN2", target_bir_lowering=False)
    P = Prog()
    es = ExitStack()
    NSTEP = NT + 1 if pipe else NT
    if pipe:
        NS, depth = 3, 2
    NL = 1 if pipe else 2
    SEQ = NSTEP * TT
    NBU = (NB_ATT + NB_HG) * (depth // 2) if depth % 2 == 0 else NB_ATT

    def din(name, shape, dt=F32):
        return nc.dram_tensor(name, list(shape), dt, kind="ExternalInput").ap()

    def dout(name, shape):
        return nc.dram_tensor(name, list(shape), F32, kind="ExternalOutput").ap()

    xp = din("xp", [SEQ, D])
    xs = din("xs", [NS, 64, D])
    ck = din("ck", [NL, NS, 128, 256])
    cv = din("cv", [NL, NS, 128, 256])
    st = din("st", [NL, NS, 8, 128, 128])
    wblk = din("wblk", [NBU, 128, 4096])
    small = din("small", [128, 128])
    sinkrow = din("sinkrow", [1, 2048])
    if pipe:
        ein = nc.dram_tensor("ein", [TT, D], F32)
        eout = nc.dram_tensor("eout", [2 * TT, D], F32)
        eins = nc.dram_tensor("eins", [64, D], F32)
        eouts = nc.dram_tensor("eouts", [128, D], F32)
    cident = din("cident", [128, 128])
    cmask = din("cmask", [128, 128])
    creset = din("creset", [128, TT])
    ropeC = din("ropeC", [128, SEQ + 64 * NS])
    ropeS = din("ropeS", [128, SEQ + 64 * NS])
    yp = dout("yp", [SEQ, D])
    ys = dout("ys", [NS, 64, D])
    kpo = dout("kpo", [2, 128, 256])
    vpo = dout("vpo", [2, 128, 256])
    spo = dout("spo", [2, 8, 128, 128])
    kso = dout("kso", [NL, NS, 128, 256])
    vso = dout("vso", [NL, NS, 128, 256])
    sso = dout("sso", [NL, NS, 8, 128, 128])
    wscr = nc.dram_tensor("wscr", [NBU, 128, 4096], BF16, kind="Internal").ap()

    def sb(name, shape, dt=F32):
        return es.enter_context(nc.sbuf_tensor(name, list(shape), dt))

    def ps(name, shape, dt=F32):
        return es.enter_context(nc.psum_tensor(name, list(shape), dt))

    x = sb("x", [128, 8, TT])
    hb = sb("hb", [128, 8, TT], BF16)
    NSLOT = 4
    wsl = [sb(f"ws{i}", [128, 4096], BF16) for i in range(NSLOT)]
    big = [sb(f"big{i}", [128, 8, TT], BF16) for i in range(4)]
    ktok = sb("ktok", [128, 8, 4, 128], BF16)
    KA = sb("KA", [128, 4, 128 + TT], BF16)
    KB = sb("KB", [128, 4, 128 + TT], BF16)
    Vst = sb("Vst", [64, 2 + TT // 64, 512], BF16)
    KAc = [sb(f"KAc{j}", [128, 4, 128], BF16) for j in range(NL)]
    KBc = [sb(f"KBc{j}", [128, 4, 128], BF16) for j in range(NL)]
    Vc = [sb(f"Vc{j}", [64, 2, 512], BF16) for j in range(NL)]
    kf = sb("kf", [128, 4, 128])
    vf = sb("vf", [64, 2, 256])
    kvo = sb("kvo", [128, 256])
    pT = [sb(f"pT{i}", [64, 384], BF16) for i in range(4)]
    rec = [sb(f"rec{i}", [128, 128]) for i in range(4)]
    rC = [sb(f"rC{i}", [128, TT]) for i in range(1)]
    rS = [sb(f"rS{i}", [128, TT]) for i in range(1)]
    NTMP = 12
    tmp = [sb(f"tmp{i}", [128, TT]) for i in range(NTMP)]
    oh = sb("oh", [128, TT])
    osq = sb("osq", [128, TT])
    oh2 = sb("oh2", [128, TT])
    osq2 = sb("osq2", [128, TT])
    S = [sb(f"S{j}", [128, 8, 128]) for j in range(NL)]
    S2 = [sb(f"S{j}b", [128, 8, 128]) for j in range(NL)]
    scur = [0 for j in range(NL)]
    vones = sb("vones", [64, 128], BF16)

    Sb = sb("Sb", [128, 8, 128], BF16)
    smask = [sb(f"smask{i}", [128, 128], BF16) for i in range(2)]
    Elast = sb("Elast", [128, 8, 8])
    Emid = sb("Emid", [128, 8, 8])
    Elm = sb("Elm", [128, 8, 8])
    dl = sb("dl", [128, 8, 8])
    ident_f = sb("ident_f", [128, 128])
    ident_b = sb("ident_b", [128, 128], BF16)
    ones_f = sb("ones_f", [128, 128])
    ones_b = sb("ones_b", [128, 128], BF16)
    mask_f = sb("mask_f", [128, 128])
    resetm = sb("resetm", [128, TT])
    sm = sb("sm", [128, 128])
    lb = sb("lb", [128, 2, 8])
    oml = sb("oml", [128, 2, 8])
    esink = sb("esink", [1, 2048])
    eshl = sb("eshl", [2, 2048], BF16)
    ckst = sb("ckst", [128, 4, 2, 64])
    ktz = [sb(f"ktz{i}", [128, 768], BF16) for i in range(2)]
    ktz2 = [sb(f"ktzb{i}", [128, 768], BF16) for i in range(2)]
    pm = [ps(f"pm{i}", [128, 512]) for i in range(4)]
    pS = [ps(f"pS{i}", [128, 512]) for i in range(2)]
    pO = [ps(f"pO{i}", [128, 512]) for i in range(2)]

    cnt = {"pm": 0, "tmp": 0, "w": 0, "wl": 0, "pm_n": 4}

    def next_pm():
        i = cnt["pm"] % cnt["pm_n"]
        cnt["pm"] += 1
        return pm[i], f"pm{i}"

    def next_tmp():
        i = cnt["tmp"] % NTMP
        cnt["tmp"] += 1
        return tmp[i], f"tmp{i}"

    SM_MIX, SM_MLP, SM_FIN, SM_GNO, SM_LB = 0, 32, 64, 72, 80
    SM_FLAG, SM_SELA, SM_SELB, SM_LBSEL = 96, 120, 121, 122

    def dma(eng, out, in_, r, w, stream):
        P.op(eng, lambda e, out=out, in_=in_: e.dma_start(out=out, in_=in_), r=r, w=w, stream=stream)

    dma("sync", ident_f[:], cident, [], ["ident_f"], "c0")
    dma("sync", mask_f[:], cmask, [], ["mask_f"], "c1")
    dma("sync", resetm[:], creset, [], ["resetm"], "c2")
    dma("sync", sm[:], small, [], ["sm"], "c3")
    dma("sync", esink[:], sinkrow, [], ["esink"], "c4")
    P.op("vector", lambda e: e.tensor_copy(out=ident_b[:], in_=ident_f[:]), r=["ident_f"], w=["ident_b"])
    P.op("vector", lambda e: e.memset(ones_f[:], 1.0), w=["ones_f"])
    P.op("vector", lambda e: e.memset(ones_b[:], 1.0), w=["ones_b"])
    P.op("scalar", lambda e: e.activation(out=esink[:], in_=esink[:], func=AF.Exp), r=["esink"], w=["esink"])
    P.op("vector", lambda e: e.tensor_copy(out=eshl[0:1, :], in_=esink[:]), r=["esink"], w=["eshl"])
    for q4 in range(4):
        ta, tan = tmp[2 * q4], f"tmp{2 * q4}"
        tb_, tbn = tmp[2 * q4 + 1], f"tmp{2 * q4 + 1}"
        P.op("vector", lambda e, ta=ta, q4=q4: e.tensor_copy(out=ta[0:1, :], in_=eshl[0:1, q4 * 512:(q4 + 1) * 512]), r=["eshl"], w=[tan])
        P.op("vector", lambda e, ta=ta, tb_=tb_, q4=q4: e.tensor_tensor(out=tb_[0:1, :].bitcast(BF16)[:, 0:512], in0=esink[0:1, q4 * 512:(q4 + 1) * 512], in1=ta[0:1, :], op=ALU.subtract),
             r=["esink", tan], w=[tbn])
        dma("sync", eshl[1:2, q4 * 512:(q4 + 1) * 512], tb_[0:1, :].bitcast(BF16)[:, 0:512], [tbn], ["eshl"], f"c5{q4}")
    lbr = sm[:, SM_LB:SM_LB + 16].rearrange("p (l h) -> p l h", l=2)
    P.op("vector", lambda e: e.memset(lb[:], 0.0), w=["lb"])
    P.op("vector", lambda e: e.tensor_tensor(out=dl[:, 0, :], in0=lbr[:, 1, :], in1=lbr[:, 0, :], op=ALU.subtract), r=["sm"], w=["dl"])
    P.op("scalar", lambda e: e.activation(out=lb[:, 1, :], in_=dl[:, 0, :], func=AF.Sigmoid), r=["dl", "lb"], w=["lb"])
    if pipe:
        P.op("vector", lambda e: e.tensor_scalar(out=lb[:, 0, :], in0=lb[:, 1, :], scalar1=sm[:, SM_LBSEL:SM_LBSEL + 1], scalar2=None, op0=ALU.mult), r=["lb", "sm"], w=["lb"])
    P.op("vector", lambda e: e.tensor_scalar(out=oml[:], in0=lb[:], scalar1=-1.0, scalar2=1.0, op0=ALU.mult, op1=ALU.add), r=["lb"], w=["oml"])
    for j in range(NL):
        P.op("gpsimd", lambda e, j=j: e.memset(S[j][:], 0.0), w=[f"S{j}h{h}" for h in range(8)])
        P.op("gpsimd", lambda e, j=j: e.memset(S2[j][:], 0.0), w=[f"S{j}bh{h}" for h in range(8)])
    if pipe:
        P.op("gpsimd", lambda e: e.memset(tmp[11][:], 0.0), w=["tmp11"])
        for q4 in range(4):
            for half in range(2):
                dma("sync", ein[q4 * 128:(q4 + 1) * 128, half * 512:(half + 1) * 512], tmp[11][:], ["tmp11"], ["ein"], f"ez{q4}{half}")
        for half in range(2):
            dma("sync", eins[:, half * 512:(half + 1) * 512], tmp[11][0:64, :], ["tmp11"], ["eins"], f"ez4{half}")
    P.op("gpsimd", lambda e: e.memset(KA[:], 0.0), w=["KA"])
    for i in range(2):
        P.op("gpsimd", lambda e, i=i: e.memset(ktz[i][:], 0.0), w=[f"ktz{i}"])
        P.op("gpsimd", lambda e, i=i: e.memset(ktz2[i][:], 0.0), w=[f"ktzb{i}"])
    P.op("gpsimd", lambda e: e.memset(KB[:], 0.0), w=["KB"])
    for j in range(NL):
        P.op("gpsimd", lambda e, j=j: e.memset(KAc[j][:], 0.0), w=[f"KAc{j}"])
        P.op("gpsimd", lambda e, j=j: e.memset(KBc[j][:], 0.0), w=[f"KBc{j}"])
        P.op("gpsimd", lambda e, j=j: e.memset(Vc[j][:], 0.0), w=[f"Vc{j}"])
    nb_used = NBU
    cast_state = {"next": 0}

    def cast_upto(n):
        while cast_state["next"] < min(n, nb_used):
            i = cast_state["next"]
            dma("gpsimd", wscr[i], wblk[i], [], [f"wscr{i}", f"wcs{i % 8}"], f"wc{i % 8}")
            cast_state["next"] += 1

    cast_upto(8)

    wstate = {"next_load": 0, "seq": []}

    def w_issue():
        k = wstate["next_load"]
        if k >= len(wstate["seq"]):
            return
        blk = wstate["seq"][k]
        slot = k % NSLOT
        dma("sync", wsl[slot][:], wscr[blk], [f"wscr{blk}"], [f"ws{slot}"], f"wl{slot}")
        wstate["next_load"] += 1

    def wget():
        k = cnt["w"]
        cnt["w"] += 1
        if k < nb_used:
            cast_upto(k + 12)
        while wstate["next_load"] < min(k + NSLOT, len(wstate["seq"])):
            w_issue()
        slot = k % NSLOT
        return wsl[slot], f"ws{slot}"

    def mm(out, lhsT, rhs, start, stop, r, w):
        P.op("tensor", lambda e: e.matmul(out, lhsT, rhs, start=start, stop=stop), r=r, w=w)

    def proj_f(W, wn, col0, src, srcn, T, kcs=8):
        pt, pn = next_pm()
        Wv = W[:].rearrange("p (k n) -> p k n", k=kcs)
        for kc in range(kcs):
            if callable(src):
                rhs, rn = src(kc)
            else:
                rhs, rn = src[:, kc, :T], srcn
            mm(pt[:, :T], Wv[:, kc, col0:col0 + 128], rhs, kc == 0, kc == kcs - 1, [wn, rn], [pn])
        return pt, pn

    def rmsnorm(T, gcol, final=False):
        pt, pn = next_pm()
        for c in range(8):
            t, tn = next_tmp()
            tb16 = t[:].bitcast(BF16)
            P.op("scalar", lambda e, tb16=tb16, c=c: e.activation(out=tb16[:, :T], in_=x[:, c, :T], func=AF.Square), r=["x"], w=[tn])
            mm(pt[:, :T], ones_b[:], tb16[:, :T], c == 0, c == 7, ["ones_b", tn], [pn])
        sd, sdn = next_tmp()
        P.op("scalar", lambda e: e.activation(out=sd[:, :T], in_=pt[:, :T], func=AF.Ln, scale=1.0 / D, bias=epsb[:, 0:1]), r=[pn, "epsb"], w=[sdn])
        P.op("scalar", lambda e: e.activation(out=sd[:, :T], in_=sd[:, :T], func=AF.Exp, scale=-0.5), r=[sdn], w=[sdn])
        for c in range(8):
            if final:
                P.op("vector", lambda e, c=c: e.scalar_tensor_tensor(out=x[:, c, :T], in0=x[:, c, :T], scalar=sm[:, gcol + c:gcol + c + 1],
                                                                      in1=sd[:, :T], op0=ALU.mult, op1=ALU.mult), r=["x", "sm", sdn], w=["x"])
            else:
                P.op("vector", lambda e, c=c: e.scalar_tensor_tensor(out=hb[:, c, :T], in0=x[:, c, :T], scalar=sm[:, gcol + c:gcol + c + 1],
                                                                      in1=sd[:, :T], op0=ALU.mult, op1=ALU.mult), r=["x", "sm", sdn], w=["hb"])

    epsb = sb("epsb", [128, 1])
    P.op("vector", lambda e: e.memset(epsb[:], EPS), w=["epsb"])

    def out_proj(T, src, srcn):
        for b in range(2):
            W, wn = wget()
            for m in range(4):
                cm = 4 * b + m
                pt, pn = proj_f(W, wn, m * 128, src, srcn, T)
                P.op("vector", lambda e, pt=pt, cm=cm: e.tensor_tensor(out=x[:, cm, :T], in0=pt[:, :T], in1=x[:, cm, :T], op=ALU.add), r=[pn, "x"], w=["x"])

    def mlp(T, i):
        import os
        if os.environ.get("KSTOP", "") in ("a", "b", "b1", "b2", "b3", "c", "d", "e"):
            return
        rmsnorm(T, SM_MLP + 8 * i)
        for b in range(8):
            W, wn = wget()
            for m in range(4):
                kc = 4 * b + m
                pt, pn = proj_f(W, wn, m * 128, hb, "hb", T)
                t, tn = next_tmp()
                P.op("scalar", lambda e, pt=pt, t=t: e.activation(out=t[:, :T], in_=pt[:, :T], func=AF.Relu), r=[pn], w=[tn])
                P.op("gpsimd", lambda e, t=t, kc=kc: e.tensor_tensor(out=big[kc // 8][:, kc % 8, :T], in0=t[:, :T], in1=t[:, :T], op=ALU.mult),
                     r=[tn], w=[f"big{kc // 8}"])
        for m in range(8):
            W, wn = wget()
            pt, pn = proj_f(W, wn, 0, lambda kc: (big[kc // 8][:, kc % 8, :T], f"big{kc // 8}"), None, T, kcs=32)
            P.op("vector", lambda e, pt=pt, m=m: e.tensor_tensor(out=x[:, m, :T], in0=pt[:, :T], in1=x[:, m, :T], op=ALU.add), r=[pn, "x"], w=["x"])

    def attn_layer(T, j, tile):
        i = 2 * j
        qT, oT = big[0], big[1]
        nch = T // 64
        rslot = tile["rslot"]
        rmsnorm(T, SM_MIX + 8 * i)
        if tile["kind"] == "prompt":
            P.op("gpsimd", lambda e: e.tensor_copy(out=KA[:, :, 0:128], in_=KAc[j][:]), r=[f"KAc{j}"], w=["KA"])
            P.op("gpsimd", lambda e: e.tensor_copy(out=KB[:, :, 0:128], in_=KBc[j][:]), r=[f"KBc{j}"], w=["KB"])
            P.op("gpsimd", lambda e: e.tensor_copy(out=Vst[:, 0:2, :], in_=Vc[j][:]), r=[f"Vc{j}"], w=["Vst"])
        else:
            s = tile["s"]
            for u in range(2):
                dma("sync", ckst[:, :, u, :], ck[j, s].rearrange("t (g d) -> t g d", g=4), [], ["ckst"], f"ck{u}")
            for g in range(4):
                pt, pn = next_pm()
                P.op("tensor", lambda e, pt=pt, g=g: e.transpose(out=pt[:, 0:128], in_=ckst[:, g, :, :].rearrange("t u d -> t (u d)"), identity=ident_f[:]),
                     r=["ckst", "ident_f"], w=[pn])
                P.op("scalar", lambda e, pt=pt, g=g: e.copy(out=KA[0:64, g, 0:128], in_=pt[0:64, 0:128]), r=[pn], w=["KA"])
                P.op("scalar", lambda e, pt=pt, g=g: e.copy(out=KB[64:128, g, 0:128], in_=pt[64:128, 0:128]), r=[pn], w=["KB"])
            for blk in range(2):
                for u in range(2):
                    dma("gpsimd", Vst[0:64, blk, :].rearrange("p (g u d) -> p g u d", g=4, u=2)[:, :, u, :],
                        cv[j, s, blk * 64:(blk + 1) * 64, :].rearrange("t (g d) -> t g d", g=4), [], ["Vst"], f"cv{blk}{u}")
            dma("sync", kso[j, s, 0:64, :], ck[j, s, 64:128, :], [], [], "kso_c")
            dma("sync", vso[j, s, 0:64, :], cv[j, s, 64:128, :], [], [], "vso_c")
        need_out = tile["last"]
        uni = tile.get("uniform", False)
        osel = tile.get("osel", j)
        if uni and tile["kind"] == "prompt":
            fc = SM_FLAG + tile["idx"]
            P.op("vector", lambda e: e.tensor_scalar(out=vones[:], in0=ones_b[0:64, :], scalar1=sm[0:64, fc:fc + 1], scalar2=None, op0=ALU.mult), r=["ones_b", "sm"], w=["vones"])
        import os
        STOP = os.environ.get("KSTOP", "")
        if STOP == "a":
            return
        for qb in range(4):
            W, wn = wget()
            for jj in range(2):
                cq = 2 * qb + jj
                pa, pan = proj_f(W, wn, (2 * jj) * 128, hb, "hb", T)
                pb, pbn = proj_f(W, wn, (2 * jj + 1) * 128, hb, "hb", T)
                t1, t1n = next_tmp()
                t2, t2n = next_tmp()
                P.op("vector", lambda e, pa=pa, t1=t1: e.tensor_tensor(out=t1[:, :T], in0=pa[:, :T], in1=rC[rslot][:, :T], op=ALU.mult), r=[pan, f"rC{rslot}"], w=[t1n])
                P.op("vector", lambda e, pb=pb, t2=t2: e.tensor_tensor(out=t2[:, :T], in0=pb[:, :T], in1=rS[rslot][:, :T], op=ALU.mult), r=[pbn, f"rS{rslot}"], w=[t2n])
                P.op("gpsimd", lambda e, t1=t1, t2=t2, cq=cq: e.tensor_tensor(out=qT[:, cq, :T], in0=t1[:, :T], in1=t2[:, :T], op=ALU.add), r=[t1n, t2n], w=["big0"])
        if STOP == "b1":
            return
        for kb in range(2):
            W, wn = wget()
            for jj in range(2):
                g = 2 * kb + jj
                pa, pan = proj_f(W, wn, (2 * jj) * 128, hb, "hb", T)
                pb, pbn = proj_f(W, wn, (2 * jj + 1) * 128, hb, "hb", T)
                t1, t1n = next_tmp()
                t2, t2n = next_tmp()
                P.op("vector", lambda e, pa=pa, t1=t1: e.tensor_tensor(out=t1[:, :T], in0=pa[:, :T], in1=rC[rslot][:, :T], op=ALU.mult), r=[pan, f"rC{rslot}"], w=[t1n])
                P.op("vector", lambda e, pb=pb, t2=t2: e.tensor_tensor(out=t2[:, :T], in0=pb[:, :T], in1=rS[rslot][:, :T], op=ALU.mult), r=[pbn, f"rS{rslot}"], w=[t2n])
                P.op("gpsimd", lambda e, t1=t1, t2=t2, g=g: e.tensor_tensor(out=KA[0:64, g, 128:128 + T], in0=t1[0:64, :T], in1=t2[0:64, :T], op=ALU.add), r=[t1n, t2n], w=["KA"])
                P.op("gpsimd", lambda e, t1=t1, t2=t2, g=g: e.tensor_tensor(out=KB[64:128, g, 128:128 + T], in0=t1[64:128, :T], in1=t2[64:128, :T], op=ALU.add), r=[t1n, t2n], w=["KB"])
                if need_out:
                    n0 = T - 128 if T >= 128 else 0
                    nn = T - n0
                    P.op("vector", lambda e, t1=t1, t2=t2, g=g, n0=n0, nn=nn: e.tensor_tensor(out=kf[0:64, g, 0:nn], in0=t1[0:64, n0:T], in1=t2[0:64, n0:T], op=ALU.add),
                         r=[t1n, t2n], w=["kf"])
        if STOP == "b2":
            return
        W, wn = wget()
        Wv = W[:].rearrange("p (k n) -> p k n", k=8)
        for cb in range(nch):
            pt, pn = next_pm()
            for kc in range(8):
                mm(pt[0:64, :], hb[:, kc, cb * 64:(cb + 1) * 64], Wv[:, kc, :], kc == 0, kc == 7, ["hb", wn], [pn])
            P.op("scalar", lambda e, pt=pt, cb=cb: e.copy(out=Vst[0:64, 2 + cb, :], in_=pt[0:64, :]), r=[pn], w=["Vst"])
            if need_out and cb >= nch - 2 and STOP != "b3":
                oi = cb - (nch - 2) if nch >= 2 else 0
                for g in range(4):
                    P.op("scalar", lambda e, pt=pt, oi=oi, g=g: e.copy(out=vf[0:64, oi, g * 64:(g + 1) * 64], in_=pt[0:64, g * 128:g * 128 + 64]), r=[pn], w=["vf"])
        if STOP in ("b", "b3"):
            return
        units = []
        for c in range(nch):
            gc = tile["chunk0"] + c
            if tile["kind"] == "prompt" and uni:
                slots = [c, c + 1, c + 2]
            elif tile["kind"] == "prompt":
                blocks = [b for b in (gc - 2, gc - 1, gc) if b >= 0]
                slots = [b - tile["chunk0"] + 2 for b in blocks]
            else:
                slots = [0, 1, 2]
            for cq in range(8):
                units.append((c, cq, slots))

        def bufs(k):
            ki = k % 4
            return ([(pS[0], "pS0"), (pS[1], "pS1"), (pm[0], "pm0"), (pm[1], "pm1")][ki], [(pO[0], "pO0"), (pO[1], "pO1"), (pm[2], "pm2"), (pm[3], "pm3")][ki],
                    (pT[ki], f"pT{ki}"), (rec[ki], f"rec{ki}"))

        def stage_a(k):
            c, cq, slots = units[k]
            g = cq // 2
            (pst, psn), _, (ptt, ptn), _ = bufs(k)
            nbk = len(slots)
            for bi, sl in enumerate(slots):
                for hh in range(2):
                    Kt, Kn = (KA, "KA") if hh == 0 else (KB, "KB")
                    mm(pst[0:64, bi * 128 + hh * 64: bi * 128 + hh * 64 + 64], Kt[:, g, sl * 64:(sl + 1) * 64], qT[:, cq, c * 64:(c + 1) * 64],
                       True, True, [Kn, "big0"], [psn])
            P.op("scalar", lambda e, pst=pst, ptt=ptt, nbk=nbk: e.activation(out=ptt[:, 0:nbk * 128], in_=pst[0:64, 0:nbk * 128], func=AF.Exp, scale=0.125),
                 r=[psn], w=[ptn])

        def stage_b(k):
            c, cq, slots = units[k]
            g = cq // 2
            _, (pot, pon), (ptt, ptn), (rct, rcn) = bufs(k)
            nbk = len(slots)
            for bi, sl in enumerate(slots):
                mm(pot[:, 0:128], Vst[0:64, sl, g * 128:(g + 1) * 128], ptt[:, bi * 128:(bi + 1) * 128], bi == 0, bi == nbk - 1, ["Vst", ptn], [pon])
            for bi, sl in enumerate(slots):
                if uni and tile["kind"] == "prompt" and sl < 2:
                    mm(pot[:, 128:256], vones[:], ptt[:, bi * 128:(bi + 1) * 128], bi == 0, False, ["vones", ptn], [pon])
                else:
                    mm(pot[:, 128:256], ones_b[0:64, :], ptt[:, bi * 128:(bi + 1) * 128], bi == 0, False, ["ones_b", ptn], [pon])
            mm(pot[:, 128:256], ones_b[0:2, :], eshl[0:2, j * 1024 + cq * 128: j * 1024 + (cq + 1) * 128], False, True, ["ones_b", "eshl"], [pon])
            P.op("scalar", lambda e, pot=pot, rct=rct: e.activation(out=rct[:], in_=pot[:, 128:256], func=AF.Ln), r=[pon], w=[rcn])
            P.op("scalar", lambda e, rct=rct: e.activation(out=rct[:], in_=rct[:], func=AF.Exp, scale=-1.0), r=[rcn], w=[rcn])
            for hh in range(2):
                lo = hh * 64
                P.op("vector", lambda e, pot=pot, rct=rct, lo=lo, cq=cq, c=c: e.tensor_tensor(out=oT[lo:lo + 64, cq, c * 64:(c + 1) * 64], in0=pot[lo:lo + 64, lo:lo + 64],
                                                                                         in1=rct[lo:lo + 64, lo:lo + 64], op=ALU.mult), r=[pon, rcn], w=["big1"])

        LOOK = 2
        for k in range(min(LOOK, len(units))):
            stage_a(k)
        for k in range(len(units)):
            if k + LOOK < len(units):
                stage_a(k + LOOK)
            stage_b(k)
        if STOP == "c":
            return
        if tile["kind"] == "prompt":
            P.op("gpsimd", lambda e: e.tensor_copy(out=KAc[j][:], in_=KA[:, :, T:T + 128]), r=["KA"], w=[f"KAc{j}"])
            P.op("gpsimd", lambda e: e.tensor_copy(out=KBc[j][:], in_=KB[:, :, T:T + 128]), r=["KB"], w=[f"KBc{j}"])
            P.op("gpsimd", lambda e: e.tensor_copy(out=Vc[j][:], in_=Vst[:, nch:nch + 2, :]), r=["Vst"], w=[f"Vc{j}"])
        if need_out:
            nn = min(T, 128)
            for g in range(4):
                pt, pn = next_pm()
                P.op("tensor", lambda e, pt=pt, g=g: e.transpose(out=pt[0:nn, 0:64], in_=kf[0:64, g, 0:nn], identity=ident_f[0:64, 0:64]), r=["kf", "ident_f"], w=[pn])
                P.op("vector", lambda e, pt=pt, g=g: e.tensor_copy(out=kvo[0:nn, g * 64:(g + 1) * 64], in_=pt[0:nn, 0:64]), r=[pn], w=["kvo"])
            if tile["kind"] == "prompt":
                dma("sync", kpo[osel], kvo[:], ["kvo"], [], "kvo")
                for oi in range(2):
                    dma("sync", vpo[osel, oi * 64:(oi + 1) * 64, :], vf[0:64, oi, :], ["vf"], [], f"vfo{oi}")
            else:
                s = tile["s"]
                dma("sync", kso[j, s, 64:128, :], kvo[0:64, :], ["kvo"], [], "kvo")
                dma("sync", vso[j, s, 64:128, :], vf[0:64, 0, :], ["vf"], [], "vfo0")
        if STOP == "d":
            return
        out_proj(T, oT, "big1")

    def hgrn_layer(T, j, tile):
        i = 2 * j + 1
        qt, kt, Vh, gate = big[0], big[1], big[2], big[3]
        ob = big[3]
        nch = T // 64
        npair = (T + 127) // 128
        Sbufs = [(S[j], f"S{j}"), (S2[j], f"S{j}b")]
        Sj, Sn = Sbufs[scur[j]]
        rmsnorm(T, SM_MIX + 8 * i)
        first = tile["first"]
        if tile["kind"] == "sample":
            dma("sync", Sj[:], st[j, tile["s"]].rearrange("h k v -> k h v"), [], [f"{Sn}h{h}" for h in range(8)], f"stl{j}")
        Vh2 = Vh[:].rearrange("p a b -> p (a b)").rearrange("p (t n) -> p t n", n=1024)
        for b in range(2):
            W, wn = wget()
            Wv = W[:].rearrange("p (k n) -> p k n", k=8)
            for tb in range(npair):
                np_ = min(128, T - tb * 128)
                pt, pn = next_pm()
                for kc in range(8):
                    mm(pt[0:np_, :], hb[:, kc, tb * 128: tb * 128 + np_], Wv[:, kc, :], kc == 0, kc == 7, ["hb", wn], [pn])
                P.op("scalar", lambda e, pt=pt, tb=tb, b=b, np_=np_: e.copy(out=Vh2[0:np_, tb, b * 512:(b + 1) * 512], in_=pt[0:np_, :]), r=[pn], w=["big2"])
        for b in range(2):
            W, wn = wget()
            for m in range(4):
                h = 4 * b + m
                pt, pn = proj_f(W, wn, m * 128, hb, "hb", T)
                P.op("scalar", lambda e, pt=pt, h=h: e.activation(out=gate[:, h, :T], in_=pt[:, :T], func=AF.Silu), r=[pn], w=["big3"])
        pbfs = [pm[2][:].bitcast(BF16), pm[3][:].bitcast(BF16)]
        cnt["pm_n"] = 2
        for b in range(4):
            W, wn = wget()
            hs = (2 * b, 2 * b + 1)
            ctx = {}
            for jj, h in enumerate(hs):
                pf, pfn = proj_f(W, wn, (2 * jj) * 128, hb, "hb", T)
                pq, pqn = proj_f(W, wn, (2 * jj + 1) * 128, hb, "hb", T)
                c = dict(h=h, pf=pf, pfn=pfn)
                c["zq"], c["zqn"] = next_tmp()
                P.op("scalar", lambda e, pq=pq, c=c: e.copy(out=c["zq"][:, :T], in_=pq[:, :T]), r=[pqn], w=[c["zqn"]])
                c["sig"], c["sign"] = next_tmp()
                c["omu"], c["omun"] = next_tmp()
                c["bt"], c["btn"] = next_tmp()
                c["br"], c["brn"] = next_tmp()
                c["e1"], c["e1n"] = next_tmp()
                c["omlh"] = oml[:, j, h:h + 1]
                c["lbh"] = lb[:, j, h:h + 1]
                c["bt3"] = c["bt"][:, :T].rearrange("p (c s) -> p c s", s=64)
                c["br3"] = c["br"][:, :T].rearrange("p (c s) -> p c s", s=64)
                ctx[h] = c
                P.op("scalar", lambda e, c=c: e.activation(out=c["sig"][:, :T], in_=c["pf"][:, :T], func=AF.Sigmoid), r=[c["pfn"]], w=[c["sign"]])
            for h in hs:
                c = ctx[h]
                P.op("vector", lambda e, c=c: e.tensor_scalar(out=c["sig"][:, :T], in0=c["sig"][:, :T], scalar1=c["omlh"], scalar2=c["lbh"], op0=ALU.mult, op1=ALU.add),
                     r=[c["sign"], "oml", "lb"], w=[c["sign"]])
                P.op("gpsimd", lambda e, c=c: e.tensor_scalar(out=c["omu"][:, :T], in0=c["sig"][:, :T], scalar1=-1.0, scalar2=1.0, op0=ALU.mult, op1=ALU.add),
                     r=[c["sign"]], w=[c["omun"]])
            for h in hs:
                c = ctx[h]
                P.op("scalar", lambda e, c=c: e.activation(out=c["e1"][:, :T], in_=c["sig"][:, :T], func=AF.Ln), r=[c["sign"]], w=[c["e1n"]])
            for h in hs:
                c = ctx[h]
                P.op("vector", lambda e, c=c: e.tensor_tensor_scan(out=c["bt"][:, :T], data0=resetm[:, :T], data1=c["e1"][:, :T], initial=0.0, op0=ALU.mult, op1=ALU.add),
                     r=[c["e1n"], "resetm"], w=[c["btn"]])
                P.op("vector", lambda e, c=c: e.tensor_tensor(out=c["br3"], in0=c["bt3"], in1=c["bt3"][:, :, 31:32].to_broadcast([128, nch, 64]), op=ALU.subtract),
                     r=[c["btn"]], w=[c["brn"]])
                P.op("vector", lambda e, c=c, h=h: e.tensor_tensor(out=dl[:, h, 0:nch], in0=c["bt3"][:, :, 63], in1=c["bt3"][:, :, 31], op=ALU.subtract), r=[c["btn"]], w=["dl"])
            for h in hs:
                c = ctx[h]
                P.op("scalar", lambda e, c=c: e.activation(out=c["e1"][:, :T], in_=c["br"][:, :T], func=AF.Exp), r=[c["brn"]], w=[c["e1n"]])
                P.op("scalar", lambda e, c=c: e.activation(out=c["br"][:, :T], in_=c["br"][:, :T], func=AF.Exp, scale=-1.0), r=[c["brn"]], w=[c["brn"]])
                P.op("scalar", lambda e, c=c, h=h: e.activation(out=Elast[:, h, 0:nch], in_=c["bt3"][:, :, 63], func=AF.Exp), r=[c["btn"]], w=["Elast"])
                P.op("scalar", lambda e, c=c, h=h: e.activation(out=Emid[:, h, 0:nch], in_=c["bt3"][:, :, 31], func=AF.Exp), r=[c["btn"]], w=["Emid"])
                P.op("scalar", lambda e, h=h: e.activation(out=Elm[:, h, 0:nch], in_=dl[:, h, 0:nch], func=AF.Exp), r=["dl"], w=["Elm"])
            for h in hs:
                c = ctx[h]
                P.op("vector", lambda e, c=c, h=h: e.tensor_tensor(out=kt[:, h, :T], in0=c["omu"][:, :T], in1=c["br"][:, :T], op=ALU.mult),
                     r=[c["omun"], c["brn"]], w=["big1"])
                kz, kzn = ktz[h % 2], f"ktz{h % 2}"
                kz2, kz2n = ktz2[h % 2], f"ktzb{h % 2}"
                c.update(kz=kz, kzn=kzn, kz2=kz2, kz2n=kz2n)
                if T >= 128:
                    om4 = c["omu"][:, :T].rearrange("p (a b c) -> p a b c", b=2, c=64)
                    br4 = c["br"][:, :T].rearrange("p (a b c) -> p a b c", b=2, c=64)
                    P.op("vector", lambda e, kz=kz, om4=om4, br4=br4: e.tensor_tensor(out=kz[:, 0:npair * 192].rearrange("p (a b c) -> p a b c", b=3, c=64)[:, :, 0::2, :],
                                                                                   in0=om4, in1=br4, op=ALU.mult), r=[c["omun"], c["brn"]], w=[kzn])
                    P.op("gpsimd", lambda e, kz2=kz2, om4=om4, br4=br4: e.tensor_tensor(out=kz2[:, 0:npair * 192].rearrange("p (a b c) -> p a b c", b=3, c=64)[:, :, 0::2, 0:32],
                                                                                     in0=om4[:, :, :, 0:32], in1=br4[:, :, :, 0:32], op=ALU.mult), r=[c["omun"], c["brn"]], w=[kz2n])
                else:
                    P.op("gpsimd", lambda e, kz=kz, h=h: e.tensor_copy(out=kz[:, 0:64], in_=kt[:, h, 0:64]), r=["big1"], w=[kzn])
                    P.op("gpsimd", lambda e, kz2=kz2, h=h: e.tensor_copy(out=kz2[:, 0:32], in_=kt[:, h, 0:32]), r=["big1"], w=[kz2n])
            for h in hs:
                c = ctx[h]
                P.op("scalar", lambda e, c=c: e.activation(out=c["sig"][:, :T], in_=c["zq"][:, :T], func=AF.Silu), r=[c["zqn"], c["sign"]], w=[c["sign"]])
                P.op("vector", lambda e, c=c, h=h: e.tensor_tensor(out=qt[:, h, :T], in0=c["sig"][:, :T], in1=c["e1"][:, :T], op=ALU.mult), r=[c["sign"], c["e1n"]], w=["big0"])
            ohs = {hs[0]: (oh, "oh", osq, "osq"), hs[1]: (oh2, "oh2", osq2, "osq2")}
            import os
            for hs_run in ([hs] if not os.environ.get('KSEQ') else [(hs[0],), (hs[1],)]):
                par = scur[j]
                for p in range(npair):
                    np_ = min(128, T - p * 128)
                    t0 = p * 128
                    ncc = np_ // 64
                    stb = [next_pm() for _ in range(ncc)]
                    for hi, h in [(hh % 2, hh) for hh in hs_run]:
                        c = ctx[h]
                        kz, kzn, kz2, kz2n = c["kz"], c["kzn"], c["kz2"], c["kz2n"]
                        pbv, pbn = pbfs[hi][:, 0:128], f"pm{2 + hi}"
                        psv, psn_ = pS[hi][:, 0:128], f"pS{hi}"
                        P.op("tensor", lambda e, h=h, t0=t0, np_=np_, pbv=pbv: e.transpose(out=pbv[0:np_, :], in_=kt[:, h, t0:t0 + np_], identity=ident_b[:]),
                             r=["big1", "ident_b"], w=[pbn])
                        P.op("scalar", lambda e, h=h, p=p, np_=np_, pbv=pbv: e.copy(out=ktok[0:np_, h, p, :], in_=pbv[0:np_, :]), r=[pbn], w=[f"ktok{hi}"])
                        if np_ == 128:
                            mm(psv[:, 0:32], kz2[:, p * 192: p * 192 + 128], qt[:, h, t0:t0 + 32], True, True, [kz2n, "big0"], [psn_])
                            mm(psv[:, 32:64], kz[:, p * 192: p * 192 + 128], qt[:, h, t0 + 32:t0 + 64], True, True, [kzn, "big0"], [psn_])
                            mm(psv[:, 64:96], kz2[:, p * 192 + 64: p * 192 + 192], qt[:, h, t0 + 64:t0 + 96], True, True, [kz2n, "big0"], [psn_])
                            mm(psv[:, 96:128], kz[:, p * 192 + 64: p * 192 + 192], qt[:, h, t0 + 96:t0 + 128], True, True, [kzn, "big0"], [psn_])
                        else:
                            mm(psv[0:64, 0:32], kz2[:, 0:64], qt[:, h, 0:32], True, True, [kz2n, "big0"], [psn_])
                            mm(psv[0:64, 32:64], kz[:, 0:64], qt[:, h, 32:64], True, True, [kzn, "big0"], [psn_])
                        smk, smn = smask[hi], f"smask{hi}"
                        P.op("vector", lambda e, smk=smk, np_=np_, psv=psv: e.tensor_tensor(out=smk[0:np_, 0:np_], in0=psv[0:np_, 0:np_], in1=mask_f[0:np_, 0:np_], op=ALU.mult),
                             r=[psn_, "mask_f"], w=[smn])
                        pot, pon = pO[hi], f"pO{hi}"
                        mm(pot[:, 0:np_], Vh2[0:np_, p, h * 128:(h + 1) * 128], smk[0:np_, 0:np_], True, False, ["big2", smn], [pon])
                        for cc in range(ncc):
                            lo = cc * 64
                            pst, psn = stb[cc]
                            mm(pst[:, hi * 128:(hi + 1) * 128], ktok[lo:lo + 64, h, p, :], Vh2[lo:lo + 64, p, h * 128:(h + 1) * 128], True, True, [f"ktok{hi}", "big2"], [psn])
                    for cc in range(ncc):
                        ci = p * 2 + cc
                        lastmm = cc == ncc - 1
                        import os
                        PP = not os.environ.get("KNOPP")
                        (Sc, Scn), (Sx, Sxn) = Sbufs[par], Sbufs[(1 - par) if PP else par]
                        for hi, h in [(hh % 2, hh) for hh in hs_run]:
                            c = ctx[h]
                            pot, pon = pO[hi], f"pO{hi}"
                            pst, psn = stb[cc]
                            P.op("scalar", lambda e, h=h, ci=ci, Sc=Sc: e.activation(out=Sb[:, h, :], in_=Sc[:, h, :], func=AF.Copy, scale=Emid[:, h, ci:ci + 1]),
                                 r=[f"{Scn}h{h}", "Emid"], w=[f"Sb{h}"])
                            mm(pot[:, cc * 64:(cc + 1) * 64], Sb[:, h, :], qt[:, h, t0 + cc * 64: t0 + (cc + 1) * 64], False, lastmm, [f"Sb{h}", "big0"], [pon])
                            P.op("vector", lambda e, h=h, ci=ci, Sc=Sc, Sx=Sx: e.tensor_scalar(out=Sx[:, h, :], in0=Sc[:, h, :], scalar1=Elast[:, h, ci:ci + 1], scalar2=None, op0=ALU.mult),
                                 r=[f"{Scn}h{h}", "Elast"], w=[f"{Sxn}h{h}"])
                            P.op("vector", lambda e, pst=pst, h=h, ci=ci, hi=hi, Sx=Sx: e.scalar_tensor_tensor(out=Sx[:, h, :], in0=pst[:, hi * 128:(hi + 1) * 128], scalar=Elm[:, h, ci:ci + 1], in1=Sx[:, h, :],
                                                                                               op0=ALU.mult, op1=ALU.add), r=[psn, "Elm", f"{Sxn}h{h}"], w=[f"{Sxn}h{h}"])
                        par = (1 - par) if PP else par
                    for hi, h in [(hh % 2, hh) for hh in hs_run]:
                        pot, pon = pO[hi], f"pO{hi}"
                        oht, ohn, oqt, oqn = ohs[h]
                        P.op("scalar", lambda e, pot=pot, t0=t0, np_=np_, oht=oht: e.copy(out=oht[:, t0:t0 + np_], in_=pot[:, 0:np_]), r=[pon], w=[ohn])
                        P.op("scalar", lambda e, pot=pot, t0=t0, np_=np_, oqt=oqt: e.activation(out=oqt[:].bitcast(BF16)[:, t0:t0 + np_], in_=pot[:, 0:np_], func=AF.Square), r=[pon], w=[oqn])
            sds = {}
            for h in hs:
                oht, ohn, oqt, oqn = ohs[h]
                pt, pn = next_pm()
                mm(pt[:, :T], ones_b[:], oqt[:].bitcast(BF16)[:, :T], True, True, ["ones_b", oqn], [pn])
                sds[h] = (pt, pn) + next_tmp()
            for h in hs:
                pt, pn, sd, sdn = sds[h]
                P.op("scalar", lambda e, pt=pt, sd=sd: e.activation(out=sd[:, :T], in_=pt[:, :T], func=AF.Ln, scale=1.0 / 128, bias=epsb[:, 0:1]), r=[pn, "epsb"], w=[sdn])
                P.op("scalar", lambda e, sd=sd: e.activation(out=sd[:, :T], in_=sd[:, :T], func=AF.Exp, scale=-0.5), r=[sdn], w=[sdn])
            for h in hs:
                pt, pn, sd, sdn = sds[h]
                oht, ohn, oqt, oqn = ohs[h]
                P.op("vector", lambda e, sd=sd, oht=oht: e.scalar_tensor_tensor(out=sd[:, :T], in0=oht[:, :T], scalar=sm[:, SM_GNO + j:SM_GNO + j + 1], in1=sd[:, :T], op0=ALU.mult, op1=ALU.mult),
                     r=[ohn, "sm", sdn], w=[sdn])
                P.op("gpsimd", lambda e, sd=sd, h=h: e.tensor_tensor(out=ob[:, h, :T], in0=sd[:, :T], in1=gate[:, h, :T], op=ALU.mult), r=[sdn, "big3"], w=["big3"])
        cnt["pm_n"] = 4
        import os
        if not os.environ.get("KNOPP"):
            scur[j] ^= (nch % 2)
        Sj, Sn = Sbufs[scur[j]]
        Sres = [f"{Sn}h{h}" for h in range(8)]
        if tile["last"]:
            if tile["kind"] == "prompt":
                dma("sync", spo[tile.get("osel", j)].rearrange("h k v -> k h v"), Sj[:], Sres, [], f"so{j}")
            else:
                dma("sync", sso[j, tile["s"]].rearrange("h k v -> k h v"), Sj[:], Sres, [], f"so{j}")
        out_proj(T, ob, "big3")

    tiles = []
    if pipe:
        for t in range(NSTEP):
            tiles.append(dict(kind="prompt", T=TT, chunk0=t * 8, first=False, last=(t >= NT - 1), osel=(0 if t == NT - 1 else 1), pos0=t * TT, idx=t, uniform=True))
        for s in range(NS):
            tiles.append(dict(kind="sample", T=64, chunk0=0, first=False, last=True, pos0=SEQ + 64 * s, s=s, idx=NSTEP + s, uniform=True))
    else:
        for t in range(NT):
            tiles.append(dict(kind="prompt", T=TT, chunk0=t * 8, first=(t == 0), last=(t == NT - 1), pos0=t * TT, idx=t))
        for s in range(NS):
            tiles.append(dict(kind="sample", T=64, chunk0=0, first=False, last=True, pos0=SEQ, s=s, idx=NT + s))
    RG = [[0, 1], [2, 3], [4, 5], [6, 7]]
    per_pass = []
    off = 0
    for i in range(depth):
        n = NB_ATT if i % 2 == 0 else NB_HG
        per_pass += list(range(off, off + n))
        off += n
    for _ in tiles:
        wstate["seq"] += per_pass

    for ti, tile in enumerate(tiles):
        T = tile["T"]
        rslot = 0
        tile["rslot"] = rslot
        dma("sync", rC[rslot][:, :T], ropeC[:, tile["pos0"]:tile["pos0"] + T], [], [f"rC{rslot}"], f"rC{rslot}")
        dma("sync", rS[rslot][:, :T], ropeS[:, tile["pos0"]:tile["pos0"] + T], [], [f"rS{rslot}"], f"rS{rslot}")
        if pipe:
            if tile["kind"] == "prompt":
                P.op("gpsimd", lambda e: e.collective_compute("AllGather", ALU.bypass, replica_groups=RG, ins=[ein[:]], outs=[eout[:]]), r=["ein"], w=["eout"], stream="cc", inc=1)
            else:
                P.op("gpsimd", lambda e: e.collective_compute("AllGather", ALU.bypass, replica_groups=RG, ins=[eins[:]], outs=[eouts[:]]), r=["eins"], w=["eouts"], stream="cc", inc=1)
        ntb = (T + 127) // 128
        for tb in range(ntb):
            np_ = min(128, T - tb * 128)
            for half in range(2):
                cs = slice(half * 512, (half + 1) * 512)
                xi, xn = next_tmp()
                if tile["kind"] == "prompt":
                    src = xp[tile["pos0"] + tb * 128: tile["pos0"] + tb * 128 + np_, cs]
                else:
                    src = xs[tile["s"], :, cs]
                dma("sync", xi[0:np_, :], src, [], [xn], xn + "i")
                if pipe:
                    x2, x2n = next_tmp()
                    if tile["kind"] == "prompt":
                        src2, s2n = eout[tb * 128: tb * 128 + np_, cs], "eout"
                    else:
                        src2, s2n = eouts[0:64, cs], "eouts"
                    dma("sync", x2[0:np_, :], src2, [s2n], [x2n], x2n + "i")
                    P.op("vector", lambda e, xi=xi, np_=np_: e.tensor_scalar(out=xi[0:np_, :], in0=xi[0:np_, :], scalar1=sm[0:np_, SM_SELA:SM_SELA + 1], scalar2=None, op0=ALU.mult),
                         r=[xn, "sm"], w=[xn])
                    P.op("vector", lambda e, xi=xi, x2=x2, np_=np_: e.scalar_tensor_tensor(out=xi[0:np_, :], in0=x2[0:np_, :], scalar=sm[0:np_, SM_SELB:SM_SELB + 1], in1=xi[0:np_, :],
                                                                                       op0=ALU.mult, op1=ALU.add), r=[x2n, xn, "sm"], w=[xn])
                pt, pn = next_pm()
                for cc in range(4):
                    P.op("tensor", lambda e, pt=pt, xi=xi, cc=cc, np_=np_: e.transpose(out=pt[:, cc * 128: cc * 128 + np_], in_=xi[0:np_, cc * 128:(cc + 1) * 128], identity=ident_f[0:np_, 0:np_]),
                         r=[xn, "ident_f"], w=[pn])
                P.op("scalar", lambda e, pt=pt, half=half, tb=tb, np_=np_: e.copy(out=x[:, half * 4:half * 4 + 4, tb * 128: tb * 128 + np_],
                                                                                 in_=pt[:].rearrange("p (c t) -> p c t", c=4)[:, :, 0:np_]), r=[pn], w=["x"])
        for i in range(depth):
            if i % 2 == 0:
                attn_layer(T, i // 2, tile)
            else:
                hgrn_layer(T, i // 2, tile)
            mlp(T, i)
        for phase in (("ex", "y") if pipe else ("y",)):
            if phase == "y":
                rmsnorm(T, SM_FIN, final=True)
            for tb in range(ntb):
                np_ = min(128, T - tb * 128)
                for half in range(2):
                    cs = slice(half * 512, (half + 1) * 512)
                    pt, pn = next_pm()
                    for cc in range(4):
                        c = half * 4 + cc
                        P.op("tensor", lambda e, pt=pt, c=c, cc=cc, tb=tb, np_=np_: e.transpose(out=pt[0:np_, cc * 128:(cc + 1) * 128], in_=x[:, c, tb * 128: tb * 128 + np_], identity=ident_f[:]),
                             r=["x", "ident_f"], w=[pn])
                    yo, yn = next_tmp()
                    P.op("vector", lambda e, pt=pt, yo=yo, np_=np_: e.tensor_copy(out=yo[0:np_, :], in_=pt[0:np_, :]), r=[pn], w=[yn])
                    if phase == "ex":
                        if tile["kind"] == "prompt":
                            dma("sync", ein[tb * 128: tb * 128 + np_, cs], yo[0:np_, :], [yn], ["ein"], yn + "o")
                        else:
                            dma("sync", eins[0:64, cs], yo[0:np_, :], [yn], ["eins"], yn + "o")
                    else:
                        if tile["kind"] == "prompt":
                            dst = yp[tile["pos0"] + tb * 128: tile["pos0"] + tb * 128 + np_, cs]
                        else:
                            dst = ys[tile["s"], :, cs]
                        dma("sync", dst, yo[0:np_, :], [yn], [], yn + "o")

    P.emit(nc, es)
    es.close()
    return nc


def host_consts(SEQ):
    ident = np.eye(128, dtype=np.float32)
    m = np.zeros((128, 128), np.float32)
    for s in range(128):
        for t in range(128):
            if s // 64 == t // 64 and s <= t:
                m[s, t] = 1.0
    reset = np.ones((128, TT), np.float32)
    reset[:, 0::64] = 0.0
    pos = np.concatenate([np.arange(SEQ, dtype=np.float32), 2048.0 + np.arange(64, dtype=np.float32)])
    C, S = rope_tables(pos)
    return dict(cident=ident, cmask=m, creset=reset, ropeC=np.ascontiguousarray(C), ropeS=np.ascontiguousarray(S))


def host_small(mixer_norm, mlp_norm, final_norm, hgrn_out_norm, hgrn_lb):
    sm = np.zeros((128, 128), np.float32)
    for i in range(4):
        sm[:, 0 + 8 * i: 8 + 8 * i] = mixer_norm[i].reshape(8, 128).T
        sm[:, 32 + 8 * i: 40 + 8 * i] = mlp_norm[i].reshape(8, 128).T
    sm[:, 64:72] = final_norm.reshape(8, 128).T
    for j in range(2):
        sm[:, 72 + j] = hgrn_out_norm[j]
        sm[:, 80 + 8 * j: 88 + 8 * j] = hgrn_lb[j].reshape(8, 128).T
    return sm


_CACHE = {}


def run_pipe(inputs, NT=16):
    f = lambda k: np.asarray(inputs[k], dtype=np.float32)
    NSTEP = NT + 1
    SEQ = NSTEP * TT
    key = ("pipe", NT)
    if key not in _CACHE:
        _CACHE[key] = build_program(NT, 3, 2, pipe=True)
    nc = _CACHE[key]
    wb = build_weight_blocks(f("attn_w_qkv"), f("attn_w_o"), f("hgrn_w_in"), f("hgrn_w_o"), f("mlp_w_up"), f("mlp_w_down"))
    nbh = NB_ATT + NB_HG
    ident = np.eye(128, dtype=np.float32)
    base = host_consts(TT)
    mixer_norm, mlp_norm, final_norm = f("mixer_norm"), f("mlp_norm"), f("final_norm")
    gno, lbr, sinks = f("hgrn_out_norm"), f("hgrn_lb"), f("attn_sinks")
    xp_, xs_, ck_, cv_, st_ = f("x_prompt"), f("x_sample"), f("cache_k"), f("cache_v"), f("state_s")
    ck_ = ck_.reshape(2, 8, 128, 256)
    cv_ = cv_.reshape(2, 8, 128, 256)
    spos = 2048.0 + np.arange(64, dtype=np.float32)
    in_maps = []
    for c in range(8):
        b, role = c // 2, c % 2
        sm = np.zeros((128, 128), np.float32)
        for li in range(2):
            L = 2 * role + li
            sm[:, 8 * li: 8 * li + 8] = mixer_norm[L].reshape(8, 128).T
            sm[:, 32 + 8 * li: 40 + 8 * li] = mlp_norm[L].reshape(8, 128).T
        sm[:, 64:72] = final_norm.reshape(8, 128).T
        sm[:, 72] = gno[role]
        for j in range(2):
            sm[:, 80 + 8 * j: 88 + 8 * j] = lbr[j].reshape(8, 128).T
        for t in range(NSTEP):
            sm[:, 96 + t] = 1.0 if (t - role) >= 1 else 0.0
        sm[:, 120] = 1.0 if role == 0 else 0.0
        sm[:, 121] = 1.0 if role == 1 else 0.0
        sm[:, 122] = 1.0 if role == 1 else 0.0
        sinkrow = np.zeros((1, 2048), np.float32)
        sinkrow[0, :1024] = np.repeat(sinks[role], 64)
        xp = np.zeros((SEQ, D), np.float32)
        xs = np.zeros((3, 64, D), np.float32)
        ck = np.zeros((1, 3, 128, 256), np.float32)
        cv = np.zeros((1, 3, 128, 256), np.float32)
        st = np.zeros((1, 3, 8, 128, 128), np.float32)
        pos = np.zeros((SEQ + 192,), np.float32)
        if role == 0:
            xp[:NT * TT] = xp_[b, :NT * TT]
            xp[NT * TT:] = xp_[b, (NT - 1) * TT:NT * TT]
            xs[0:2] = xs_[2 * b:2 * b + 2]
            pos[:NT * TT] = np.arange(NT * TT, dtype=np.float32)
            so = 0
        else:
            pos[TT:TT + NT * TT] = np.arange(NT * TT, dtype=np.float32)
            so = 1
        ck[0, so:so + 2] = ck_[role, 2 * b:2 * b + 2]
        cv[0, so:so + 2] = cv_[role, 2 * b:2 * b + 2]
        st[0, so:so + 2] = st_[role, 2 * b:2 * b + 2]
        for k in range(3):
            pos[SEQ + 64 * k: SEQ + 64 * (k + 1)] = spos
        C, S_ = rope_tables(pos)
        m = dict(xp=xp, xs=xs, ck=ck, cv=cv, st=st, wblk=np.ascontiguousarray(wb[role * nbh:(role + 1) * nbh]), small=sm, sinkrow=sinkrow,
                 cident=ident, cmask=base["cmask"], creset=base["creset"], ropeC=np.ascontiguousarray(C), ropeS=np.ascontiguousarray(S_))
        in_maps.append(m)
    res = run_bass_kernel_spmd(nc, in_maps, core_ids=list(range(8)))
    R = list(res.results)
    B = 4
    yp = np.stack([R[2 * b + 1]["yp"][TT:TT + NT * TT] for b in range(B)])
    ys = np.concatenate([R[2 * b + 1]["ys"][1:3] for b in range(B)], 0)
    kp = np.stack([np.stack([R[2 * b + r]["kpo"][r] for b in range(B)]) for r in range(2)]).reshape(2, B, 128, 4, 64)
    vp = np.stack([np.stack([R[2 * b + r]["vpo"][r] for b in range(B)]) for r in range(2)]).reshape(2, B, 128, 4, 64)
    sp = np.stack([np.stack([R[2 * b + r]["spo"][r] for b in range(B)]) for r in range(2)])
    ks = np.stack([np.concatenate([R[2 * b + r]["kso"][0, r:r + 2] for b in range(B)], 0) for r in range(2)]).reshape(2, 8, 128, 4, 64)
    vs = np.stack([np.concatenate([R[2 * b + r]["vso"][0, r:r + 2] for b in range(B)], 0) for r in range(2)]).reshape(2, 8, 128, 4, 64)
    ss = np.stack([np.concatenate([R[2 * b + r]["sso"][0, r:r + 2] for b in range(B)], 0) for r in range(2)])
    return tuple(np.ascontiguousarray(a, dtype=np.float32) for a in (yp, ys, kp, vp, sp, ks, vs, ss))


def run(inputs, NT=16, n_cores=4, depth=4):
    f = lambda k: np.asarray(inputs[k], dtype=np.float32)
    SEQ = NT * TT
    key = (NT, depth)
    if key not in _CACHE:
        _CACHE[key] = build_program(NT, 2, depth)
    nc = _CACHE[key]
    wb = build_weight_blocks(f("attn_w_qkv"), f("attn_w_o"), f("hgrn_w_in"), f("hgrn_w_o"), f("mlp_w_up"), f("mlp_w_down"))
    consts = host_consts(SEQ)
    sm = host_small(f("mixer_norm"), f("mlp_norm"), f("final_norm"), f("hgrn_out_norm"), f("hgrn_lb"))
    sinkrow = np.repeat(f("attn_sinks").reshape(-1), 64).reshape(1, 2048).astype(np.float32)
    xp_, xs_, ck_, cv_, st_ = f("x_prompt"), f("x_sample"), f("cache_k"), f("cache_v"), f("state_s")
    in_maps = []
    for c in range(n_cores):
        b = c % 4
        m = dict(xp=np.ascontiguousarray(xp_[b, :SEQ]), xs=np.ascontiguousarray(xs_[2 * b:2 * b + 2]),
                 ck=np.ascontiguousarray(ck_[:, 2 * b:2 * b + 2].reshape(2, 2, 128, 256)),
                 cv=np.ascontiguousarray(cv_[:, 2 * b:2 * b + 2].reshape(2, 2, 128, 256)),
                 st=np.ascontiguousarray(st_[:, 2 * b:2 * b + 2]), wblk=wb, small=sm, sinkrow=sinkrow)
        m.update(consts)
        in_maps.append(m)
    res = run_bass_kernel_spmd(nc, in_maps, core_ids=list(range(n_cores)))
    R = list(res.results)
    R = [R[b % len(R)] for b in range(4)]
    B = 4
    yp = np.stack([R[b]["yp"] for b in range(B)]).reshape(B, SEQ, D)
    ys = np.concatenate([R[b]["ys"] for b in range(B)], 0).reshape(8, 64, D)
    kp = np.stack([R[b]["kpo"] for b in range(B)], 1).reshape(2, B, 128, 4, 64)
    vp = np.stack([R[b]["vpo"] for b in range(B)], 1).reshape(2, B, 128, 4, 64)
    sp = np.stack([R[b]["spo"] for b in range(B)], 1).reshape(2, B, 8, 128, 128)
    ks = np.concatenate([R[b]["kso"] for b in range(B)], 1).reshape(2, 8, 128, 4, 64)
    vs = np.concatenate([R[b]["vso"] for b in range(B)], 1).reshape(2, 8, 128, 4, 64)
    ss = np.concatenate([R[b]["sso"] for b in range(B)], 1).reshape(2, 8, 8, 128, 128)
    return tuple(np.ascontiguousarray(a, dtype=np.float32) for a in (yp, ys, kp, vp, sp, ks, vs, ss))


def kernel(**inputs):
    return run_pipe(inputs, NT=16)
```
